# Optimizing a Trainium2 kernel written in Bass

```python
import jax, jax.numpy as jnp
from jax import lax
import numpy as np

D_MODEL = 1024
BATCH = 8
SEQ = 4096
DEPTH = 1

MLA_HEADS = 8
NOPE_DIM = 64
ROPE_DIM = 32
QK_DIM = NOPE_DIM + ROPE_DIM
V_DIM = 64
Q_LORA = 512
KV_LORA = 256
ROPE_THETA = 10000.0
Q_BLOCK = 128
MLA_WIDTH = MLA_HEADS * V_DIM
HG_HEADS = 4
HG_KDIM = 128
HG_VDIM = 128
HG_WIDTH = HG_HEADS * HG_KDIM
HG_VWIDTH = HG_HEADS * HG_VDIM
CHUNK = 64
D_FF = 2816
CONV_W = 3
N_MOD = 6
EPS = 1e-6

IN_SPLITS = (Q_LORA, KV_LORA + ROPE_DIM, HG_WIDTH, HG_WIDTH, HG_VWIDTH, HG_VWIDTH, D_MODEL, D_MODEL)
IN_WIDTH = int(sum(IN_SPLITS))

kernel_name = "hybrid_mla_hgrn2_adaln_block"


def rms_norm(x, g):
    xf = x.astype(jnp.float32)
    y = xf * lax.rsqrt(jnp.mean(xf * xf, axis=-1, keepdims=True) + EPS)
    return (y * g.astype(jnp.float32)).astype(x.dtype)


def rope_tables(positions):
    inv_freq = ROPE_THETA ** (-jnp.arange(0, ROPE_DIM, 2, dtype=jnp.float32) / ROPE_DIM)
    ang = positions.astype(jnp.float32)[..., None] * inv_freq
    return jnp.cos(ang)[:, :, None, :], jnp.sin(ang)[:, :, None, :]


def rope_tail(t, cos, sin):
    t_pass, t_rot = t[..., :NOPE_DIM], t[..., NOPE_DIM:]
    tf = t_rot.astype(jnp.float32)
    t1, t2 = tf[..., : ROPE_DIM // 2], tf[..., ROPE_DIM // 2:]
    rot = jnp.concatenate([t1 * cos - t2 * sin, t2 * cos + t1 * sin], axis=-1)
    return jnp.concatenate([t_pass, rot.astype(t.dtype)], axis=-1)


def causal_attention_blocked(q, k, v):
    B, S, H, _ = q.shape
    nb = S // Q_BLOCK
    scale = 1.0 / float(np.sqrt(QK_DIM))
    qb = q.reshape(B, nb, Q_BLOCK, H, QK_DIM).transpose(1, 0, 2, 3, 4)
    k_pos = jnp.arange(S)

    def block(args):
        qi, i = args
        s = jnp.einsum('bqhd,bkhd->bhqk', qi, k).astype(jnp.float32) * scale
        q_pos = i * Q_BLOCK + jnp.arange(Q_BLOCK)
        mask = k_pos[None, :] <= q_pos[:, None]
        s = jnp.where(mask[None, None], s, -jnp.inf)
        p = jax.nn.softmax(s, axis=-1).astype(v.dtype)
        return jnp.einsum('bhqk,bkhd->bqhd', p, v)

    out = lax.map(block, (qb, jnp.arange(nb)))
    return out.transpose(1, 0, 2, 3, 4).reshape(B, S, H * V_DIM)


def mla_branch(c_q, c_kv_all, cos, sin, q_a_norm_g, w_uq, kv_a_norm_g, w_ukv, q_norm_g, k_norm_g):
    B, S, _ = c_q.shape
    c_q = rms_norm(c_q, q_a_norm_g)
    q = (c_q @ w_uq).reshape(B, S, MLA_HEADS, QK_DIM)
    c_kv, k_rope = c_kv_all[..., :KV_LORA], c_kv_all[..., KV_LORA:]
    c_kv = rms_norm(c_kv, kv_a_norm_g)
    kv = (c_kv @ w_ukv).reshape(B, S, MLA_HEADS, NOPE_DIM + V_DIM)
    k_nope, v = kv[..., :NOPE_DIM], kv[..., NOPE_DIM:]
    k_rope = jnp.broadcast_to(k_rope[:, :, None, :], (B, S, MLA_HEADS, ROPE_DIM))
    k = jnp.concatenate([k_nope, k_rope], axis=-1)
    q = rope_tail(rms_norm(q, q_norm_g), cos, sin)
    k = rope_tail(rms_norm(k, k_norm_g), cos, sin)
    return causal_attention_blocked(q, k, v)


def hgrn2_lower_bound(lb_table, layer):
    cum = jnp.cumsum(jax.nn.softmax(lb_table.astype(jnp.float32), axis=0), axis=0)
    return cum[layer + 1] - cum[0]


def hgrn2_branch(q_pre, f_pre, i_in, g_pre, lb, out_norm_g):
    B, S, _ = q_pre.shape
    nc = S // CHUNK
    f32 = jnp.float32
    q = jax.nn.silu(q_pre.astype(f32)).reshape(B, S, HG_HEADS, HG_KDIM)
    f = lb + (1.0 - lb) * jax.nn.sigmoid(f_pre.astype(f32))
    log_f = jnp.log(f).reshape(B, S, HG_HEADS, HG_KDIM)
    k = (1.0 - f).reshape(B, S, HG_HEADS, HG_KDIM)
    v = i_in.astype(f32).reshape(B, S, HG_HEADS, HG_VDIM)

    def to_chunks(t):
        return t.reshape(B, nc, CHUNK, HG_HEADS, t.shape[-1]).transpose(1, 0, 3, 2, 4)

    tri = jnp.tril(jnp.ones((CHUNK, CHUNK), dtype=bool))[None, None, :, :, None]

    def step(state, inp):
        qc, kc, vc, lfc = inp
        b = jnp.cumsum(lfc, axis=2)
        inter = jnp.einsum('bhck,bhkv->bhcv', qc * jnp.exp(b), state)
        diff = b[:, :, :, None, :] - b[:, :, None, :, :]
        decay = jnp.exp(jnp.where(tri, diff, -jnp.inf))
        attn = jnp.einsum('bhtk,bhsk,bhtsk->bhts', qc, kc, decay)
        intra = jnp.einsum('bhts,bhsv->bhtv', attn, vc)
        b_last = b[:, :, -1:, :]
        new_state = (jnp.exp(b_last[:, :, 0, :])[..., None] * state
                     + jnp.einsum('bhsk,bhsv->bhkv', kc * jnp.exp(b_last - b), vc))
        return new_state, inter + intra

    state0 = jnp.zeros((B, HG_HEADS, HG_KDIM, HG_VDIM), f32)
    _, o = lax.scan(step, state0, (to_chunks(q), to_chunks(k), to_chunks(v), to_chunks(log_f)))
    o = o.transpose(1, 0, 3, 2, 4).reshape(B, S, HG_HEADS, HG_VDIM)
    o = rms_norm(o, out_norm_g) * jax.nn.silu(g_pre.astype(f32)).reshape(B, S, HG_HEADS, HG_VDIM)
    return o.reshape(B, S, HG_VWIDTH).astype(q_pre.dtype)


def causal_depthwise_conv(u, w, b):
    S = u.shape[1]
    up = jnp.pad(u, ((0, 0), (CONV_W - 1, 0), (0, 0)))
    y = b
    for j in range(CONV_W):
        y = y + w[j] * up[:, j:j + S, :]
    return y


def setup_inputs(seed: int = 0) -> dict:
    key = jax.random.key(seed)
    ks = jax.random.split(key, 32)
    f32 = jnp.float32
    L = DEPTH

    def nrm(k, shape, scale):
        return jax.random.normal(k, shape, f32) * scale

    def gain(k, shape):
        return 1.0 + 0.02 * jax.random.normal(k, shape, f32)

    x = jax.random.normal(ks[0], (BATCH, SEQ, D_MODEL), f32)
    c = jax.random.normal(ks[1], (BATCH, D_MODEL), f32)
    offsets = jax.random.randint(ks[2], (BATCH, 1), 0, 1024, dtype=jnp.int32)
    positions = offsets + jnp.arange(SEQ, dtype=jnp.int32)[None, :]
    return {
        "x": x,
        "c": c,
        "positions": positions,
        "w_ada": nrm(ks[3], (L, D_MODEL, N_MOD * D_MODEL), 0.5 * D_MODEL ** -0.5),
        "b_ada": nrm(ks[4], (L, N_MOD * D_MODEL), 0.02),
        "norm1_g": gain(ks[5], (L, D_MODEL)),
        "w_in": nrm(ks[6], (L, D_MODEL, IN_WIDTH), D_MODEL ** -0.5),
        "q_a_norm_g": gain(ks[7], (L, Q_LORA)),
        "w_uq": nrm(ks[8], (L, Q_LORA, MLA_HEADS * QK_DIM), Q_LORA ** -0.5),
        "kv_a_norm_g": gain(ks[9], (L, KV_LORA)),
        "w_ukv": nrm(ks[10], (L, KV_LORA, MLA_HEADS * (NOPE_DIM + V_DIM)), KV_LORA ** -0.5),
        "q_norm_g": gain(ks[11], (L, QK_DIM)),
        "k_norm_g": gain(ks[12], (L, QK_DIM)),
        "hg_lower_bound": nrm(ks[13], (L + 1, HG_WIDTH), 0.5),
        "hg_out_norm_g": gain(ks[14], (L, HG_VDIM)),
        "w_branch_a": nrm(ks[15], (L, MLA_WIDTH, D_MODEL), MLA_WIDTH ** -0.5),
        "w_branch_b": nrm(ks[16], (L, HG_VWIDTH, D_MODEL), HG_VWIDTH ** -0.5),
        "w_out": nrm(ks[17], (L, D_MODEL, D_MODEL), D_MODEL ** -0.5),
        "norm2_g": gain(ks[18], (L, D_MODEL)),
        "w_up": nrm(ks[19], (L, D_MODEL, 2 * D_FF), D_MODEL ** -0.5),
        "conv_w": nrm(ks[20], (L, CONV_W, 2 * D_FF), CONV_W ** -0.5),
        "conv_b": nrm(ks[21], (L, 2 * D_FF), 0.02),
        "w_down": nrm(ks[22], (L, D_FF, D_MODEL), D_FF ** -0.5),
    }


def reference(x, c, positions, w_ada, b_ada, norm1_g, w_in, q_a_norm_g, w_uq, kv_a_norm_g,
              w_ukv, q_norm_g, k_norm_g, hg_lower_bound, hg_out_norm_g, w_branch_a,
              w_branch_b, w_out, norm2_g, w_up, conv_w, conv_b, w_down):
    cos, sin = rope_tables(positions)
    split_idx = np.cumsum(IN_SPLITS)[:-1].tolist()
    for l in range(DEPTH):
        mod = (c @ w_ada[l] + b_ada[l])[:, None, :]
        shift1, scale1, gate1, shift2, scale2, gate2 = jnp.split(mod, N_MOD, axis=-1)

        h = rms_norm(x, norm1_g[l]) * (1.0 + scale1) + shift1
        proj = h @ w_in[l]
        c_q, c_kv_all, hq, hf, hi, hg, ga, gb = jnp.split(proj, split_idx, axis=-1)

        y_a = mla_branch(c_q, c_kv_all, cos, sin, q_a_norm_g[l], w_uq[l], kv_a_norm_g[l],
                         w_ukv[l], q_norm_g[l], k_norm_g[l]) @ w_branch_a[l]
        lb = hgrn2_lower_bound(hg_lower_bound, l)
        y_b = hgrn2_branch(hq, hf, hi, hg, lb, hg_out_norm_g[l]) @ w_branch_b[l]

        merged = jax.nn.sigmoid(ga) * y_a + jax.nn.sigmoid(gb) * y_b
        x = x + gate1 * (merged @ w_out[l])

        h2 = rms_norm(x, norm2_g[l]) * (1.0 + scale2) + shift2
        up = causal_depthwise_conv(h2 @ w_up[l], conv_w[l], conv_b[l])
        u_gate, u_val = up[..., :D_FF], up[..., D_FF:]
        x = x + gate2 * ((jax.nn.silu(u_gate) * u_val) @ w_down[l])
    return x
```

```python
import numpy as np
import concourse.bass as bass
import concourse.mybir as mybir
from concourse.bass_utils import run_bass_kernel_spmd
from contextlib import ExitStack

F32 = mybir.dt.float32
BF16 = mybir.dt.bfloat16
I32 = mybir.dt.int32
AF = mybir.ActivationFunctionType
ALU = mybir.AluOpType
AX = mybir.AxisListType

S = 4096
D = 1024
NBLK = 8
EPS = 1e-6
DFF = 2816
TWO_PI = 6.283185307179586
PI = 3.141592653589793

NDS = 12
ENGS = ("pe", "act", "dve", "pool", "sp")


class R:
    __slots__ = ("w", "r", "name")

    def __init__(self, name=""):
        self.w = None
        self.r = []
        self.name = name


class Prog:
    def __init__(self, nc, es):
        self.nc = nc
        self.sem = {e: es.enter_context(nc.semaphore("s_" + e)) for e in ENGS if e != "sp"}
        self.cnt = {e: 0 for e in ENGS}
        self.seen = {e: {} for e in ENGS}
        self.q = {e: [] for e in ENGS}
        self.dsem = {e: [es.enter_context(nc.semaphore("d_%s%d" % (e, i))) for i in range(NDS)]
                     for e in ("sp", "pool")}
        self.dcnt = {e: [0] * NDS for e in ("sp", "pool")}
        self.dnext = {e: 0 for e in ("sp", "pool")}
        self.opn = {}

    def _semh(self, key):
        if key[0] == "c":
            return self.sem[key[1]]
        return self.dsem[key[1]][key[2]]

    def op(self, eng, fn, rd=(), wr=(), dma=False, n=1):
        deps = []
        raw = set()
        for t in rd:
            if t.w is not None:
                deps.append(t.w)
                raw.add(t.w)
        for t in wr:
            if t.w is not None:
                deps.append(t.w)
            deps.extend(t.r)
        waits = {}
        if dma:
            slot = self.dnext[eng]
            self.dnext[eng] = (slot + 1) % NDS
            key = ("d", eng, slot)
            prev = self.dcnt[eng][slot]
            if prev > 0:
                deps.append((key, prev, eng))
            self.dcnt[eng][slot] = prev + 16
            me = (key, prev + 16, eng)
            sig = (self.dsem[eng][slot], 16)
        else:
            self.cnt[eng] += 1
            key = ("c", eng)
            me = (key, self.cnt[eng], eng)
            self.opn[me] = n
            sig = (self.sem[eng], 1)
        seen = self.seen[eng]
        for (k, v, e) in deps:
            if k[0] == "c" and e == eng:
                if eng == "pe" or (k, v, e) not in raw:
                    continue
                if n >= 256 and self.opn.get((k, v, e), 1) >= 256:
                    continue
            if seen.get(k, 0) >= v:
                continue
            if waits.get(k, 0) < v:
                waits[k] = v
        for k, v in waits.items():
            seen[k] = v
        for t in rd:
            if len(t.r) > 64:
                t.r = t.r[-32:] if t.w is None or True else t.r
            t.r.append(me)
        for t in wr:
            t.w = me
            t.r = []
        self.q[eng].append(([(self._semh(k), v) for k, v in waits.items()], fn, sig))

    def barrier(self):
        for e in ENGS:
            waits = []
            for e2 in ENGS:
                if e2 != "sp" and e2 != e and self.cnt[e2] > 0 and self.seen[e].get(("c", e2), 0) < self.cnt[e2]:
                    waits.append((self.sem[e2], self.cnt[e2]))
                    self.seen[e][("c", e2)] = self.cnt[e2]
            for qe in ("sp", "pool"):
                for i in range(NDS):
                    v = self.dcnt[qe][i]
                    if v > 0 and self.seen[e].get(("d", qe, i), 0) < v:
                        waits.append((self.dsem[qe][i], v))
                        self.seen[e][("d", qe, i)] = v
            self.q[e].append((waits, None, None))

    def emit(self):
        nc = self.nc
        q = self.q
        print("emit ops:", {e: (len(v), sum(len(w) for w, _, _ in v)) for e, v in q.items()}, flush=True)

        def run(e, items):
            for waits, fn, sig in items:
                for (sh, v) in waits:
                    e.wait_ge(sh, v)
                if fn is not None:
                    ins = fn(e)
                    ins.then_inc(sig[0], sig[1])

        with nc.allow_low_precision("bf16 staging of fp32-computed values is intended"), nc.Block() as block:
            @block.tensor
            def _(e):
                run(e, q["pe"])

            @block.scalar
            def _(e):
                run(e, q["act"])

            @block.vector
            def _(e):
                run(e, q["dve"])

            @block.gpsimd
            def _(e):
                run(e, q["pool"])

            @block.sync
            def _(e):
                run(e, q["sp"])
        self.q = {e: [] for e in ENGS}

    def dma(self, out, in_, rd=(), wr=(), eng="sp", **kw):
        self.op(eng, lambda e: e.dma_start(out=out, in_=in_, **kw), rd, wr, dma=True)

    def mm(self, out, pairs, rd=(), wr=(), start=True, stop=True):
        n = len(pairs)

        def fn(e):
            ins = None
            for i, (l, r) in enumerate(pairs):
                ins = e.matmul(out, l, r, start=(start and i == 0), stop=(stop and i == n - 1))
            return ins
        self.op("pe", fn, rd, wr)

    def tr(self, out, in_, ident, rd=(), wr=()):
        self.op("pe", lambda e: e.transpose(out, in_, ident), rd, wr)

    @staticmethod
    def _n(ap):
        n = 1
        for d in ap.shape[1:]:
            n *= int(d)
        return n

    def act(self, out, in_, func, rd=(), wr=(), **kw):
        n = 1 if "accum_out" in kw else self._n(out)
        self.op("act", lambda e: e.activation(out=out, in_=in_, func=func, **kw), rd, wr, n=n)

    def ts(self, out, in0, s1, s2, op0, op1=None, rd=(), wr=(), eng="dve"):
        if op1 is None:
            self.op(eng, lambda e: e.tensor_scalar(out=out, in0=in0, scalar1=s1, scalar2=None, op0=op0), rd, wr, n=self._n(out))
        else:
            self.op(eng, lambda e: e.tensor_scalar(out=out, in0=in0, scalar1=s1, scalar2=s2, op0=op0, op1=op1), rd, wr, n=self._n(out))

    def tt(self, out, in0, in1, op, rd=(), wr=(), eng="dve"):
        self.op(eng, lambda e: e.tensor_tensor(out=out, in0=in0, in1=in1, op=op), rd, wr, n=self._n(out))

    def stt(self, out, in0, scalar, in1, op0, op1, rd=(), wr=(), eng="dve"):
        self.op(eng, lambda e: e.scalar_tensor_tensor(out=out, in0=in0, scalar=scalar, in1=in1, op0=op0, op1=op1), rd, wr, n=self._n(out))

    def cp(self, out, in_, rd=(), wr=(), eng="dve"):
        self.op(eng, lambda e: e.tensor_copy(out=out, in_=in_), rd, wr, n=self._n(out))

    def red(self, out, in_, rd=(), wr=(), eng="dve"):
        self.op(eng, lambda e: e.tensor_reduce(out=out, in_=in_, axis=AX.X, op=ALU.add), rd, wr)

    def recip(self, out, in_, rd=(), wr=()):
        self.op("dve", lambda e: e.reciprocal(out=out, in_=in_), rd, wr, n=self._n(out))

    def memset(self, ap, val, wr=(), eng="pool"):
        self.op(eng, lambda e: e.memset(ap, val), (), wr)


PC_BADA, PC_G1, PC_G2, PC_GQA, PC_GKVA, PC_LB0, PC_LB1, PC_GOUT, PC_C, PC_CB, PC_CW = 0, 48, 56, 64, 68, 70, 74, 78, 79, 87, 131
NP = 131 + 132
BC_GQ, BC_GKN, BC_GKR, BC_IF = 0, 768, 1280, 1312
NB = 1328
C_CQ, C_CKV, C_KR, C_HQ, C_HF, C_HI, C_HG, C_GA, C_GB = 0, 512, 768, 800, 1312, 1824, 2336, 2848, 3872


def build(debug=False, phases=99):
    nc = bass.Bass("TRN2", target_bir_lowering=False)

    def din(name, shape, dt=F32):
        return nc.dram_tensor(name, shape, dt, kind="ExternalInput").ap()

    x = din("x", [S, D])
    prm_d = din("prm", [128, NP])
    bcp_d = din("bcp", [128, NB])
    pos_d = din("posT", [128, 32], I32)
    cst_d = din("cst", [128, 3 * 128])
    w_ada = din("w_ada", [D, 6 * D])
    w_in = din("w_in", [D, 4896])
    w_uq = din("w_uq", [512, 768])
    w_ukv = din("w_ukv", [256, 1024])
    w_ba = din("w_ba", [512, D])
    w_bb = din("w_bb", [512, D])
    w_out = din("w_out", [D, D])
    w_up = din("w_up", [D, 2 * DFF])
    w_dn = din("w_dn", [DFF, D])
    out = nc.dram_tensor("out", [S, D], F32, kind="ExternalOutput").ap()

    def scr(name, shape, dt):
        kind = "ExternalOutput" if debug else "Internal"
        return nc.dram_tensor(name, shape, dt, kind=kind).ap()

    modscr = scr("modscr", [48, 128], F32)
    qT_scr = scr("qT_scr", [96, 8, S], BF16)
    kT_scr = scr("kT_scr", [96, 8, S], BF16)
    V_scr = scr("V_scr", [S, 512], BF16)
    hgo_scr = scr("hgo_scr", [512, S], BF16)
    sga_scr = scr("sga_scr", [D, S], BF16)
    sgb_scr = scr("sgb_scr", [D, S], BF16)
    ON_scr = scr("ON_scr", [64, 8, S], BF16)

    with ExitStack() as es0:
        P = Prog(nc, es0)

        def sbuf(es, name, shape, dt=F32):
            return es.enter_context(nc.sbuf_tensor("sb_" + name, shape, dt))

        prm = sbuf(es0, "prm", [128, NP]); r_prm = R()
        cstf = sbuf(es0, "cstf", [128, 384]); r_cstf = R()
        cstb = sbuf(es0, "cstb", [128, 384], BF16); r_cstb = R()
        modT = sbuf(es0, "modT", [128, 48]); r_mod = R()
        gs1 = sbuf(es0, "gs1", [128, 8]); gs2 = sbuf(es0, "gs2", [128, 8]); r_gs = R()
        r_gbc = R()
        lb = sbuf(es0, "lb", [128, 4]); oml = sbuf(es0, "oml", [128, 4]); r_lb = R()
        epsb = sbuf(es0, "epsb", [128, 1]); r_eps = R()
        ones_b = sbuf(es0, "ones_b", [128, 128], BF16); ones_f = sbuf(es0, "ones_f", [128, 64]); r_ones = R()
        es01 = ExitStack()
        bcp = sbuf(es01, "bcp", [128, NB]); r_bcp = R()
        cos_t = sbuf(es01, "cos_t", [128, 32, 16]); sin_t = sbuf(es01, "sin_t", [128, 32, 16]); r_cs = R()
        identb = cstb[:, 0:128]
        tri_f = cstf[:, 128:256]
        bdtri_f = cstf[:, 256:384]
        tri_b = cstb[:, 128:256]
        identf = cstf[:, 0:128]

        P.dma(prm[:], prm_d[:, :], wr=[r_prm])
        P.dma(bcp[:], bcp_d[:, :], wr=[r_bcp])
        P.dma(cstf[:], cst_d[:, :], wr=[r_cstf])
        P.dma(cstb[:], cst_d[:, :], wr=[r_cstb], eng="pool")
        P.memset(epsb[:], EPS, wr=[r_eps])
        P.memset(ones_b[:], 1.0, wr=[r_ones])
        P.memset(ones_f[:], 1.0, wr=[r_ones])

        es_w1 = ExitStack()
        win = sbuf(es_w1, "win", [128, 8, 4896], BF16); r_win = R()
        wuq = sbuf(es_w1, "wuq", [128, 4, 768], BF16); r_wuq = R()
        wukv = sbuf(es_w1, "wukv", [128, 2, 1024], BF16); r_wukv = R()
        for k in range(8):
            P.dma(win[:, k, :], w_in[k * 128:(k + 1) * 128, :], wr=[r_win], eng="pool")
        P.dma(wuq[:], w_uq.rearrange("(k p) n -> p k n", p=128), wr=[r_wuq], eng="pool")
        P.dma(wukv[:], w_ukv.rearrange("(k p) n -> p k n", p=128), wr=[r_wukv], eng="pool")
        with ExitStack() as es:
            wa_ring = [sbuf(es, "wada%d" % i, [128, 6 * D]) for i in range(2)]
            r_wa = [R(), R()]
            pm = es.enter_context(nc.psum_tensor("p0_pm", [128, 512], F32)); r_pm = R()
            pt = es.enter_context(nc.psum_tensor("p0_pt", [128, 512], F32)); r_pt = R()
            for k in range(8):
                wt = wa_ring[k % 2]
                P.dma(wt[:], w_ada[k * 128:(k + 1) * 128, :], wr=[r_wa[k % 2]])

                def f0(e, wt=wt, k=k):
                    ins = None
                    for j in range(48):
                        ins = e.matmul(pm[:, k * 48 + j:k * 48 + j + 1], wt[:, j * 128:(j + 1) * 128],
                                       prm[:, PC_C + k:PC_C + k + 1], start=True, stop=True)
                    return ins
                P.op("pe", f0, [r_wa[k % 2], r_prm], [r_pm])
            P.red(modT[:], pm[:, 0:384].rearrange("p (k j) -> p j k", k=8), rd=[r_pm], wr=[r_mod])
            P.tt(modT[:], modT[:], prm[:, PC_BADA:PC_BADA + 48], ALU.add, rd=[r_mod, r_prm], wr=[r_mod])
            P.stt(gs1[:], modT[:, 8:16], 1.0, prm[:, PC_G1:PC_G1 + 8], ALU.add, ALU.mult, rd=[r_mod, r_prm], wr=[r_gs])
            P.stt(gs2[:], modT[:, 32:40], 1.0, prm[:, PC_G2:PC_G2 + 8], ALU.add, ALU.mult, rd=[r_mod, r_prm], wr=[r_gs])
            P.tr(pt[0:48, 0:128], modT[:, 0:48], identf, rd=[r_mod, r_cstf], wr=[r_pt])
            modrow = sbuf(es, "modrow", [48, 128]); r_mr = R()
            P.cp(modrow[:], pt[0:48, 0:128], rd=[r_pt], wr=[r_mr])
            r_ms = R()
            P.dma(modscr[:, :], modrow[:], rd=[r_mr], wr=[r_ms])
            g1src = modscr[16:24, :].rearrange("(o a) b -> o (a b)", o=1)
            g2src = modscr[40:48, :].rearrange("(o a) b -> o (a b)", o=1)
            posi = sbuf(es, "posi", [128, 32], I32); r_pi = R()
            posf = sbuf(es, "posf", [128, 32]); r_pf = R()
            P.dma(posi[:], pos_d[:, :], wr=[r_pi])
            P.cp(posf[:], posi[:], rd=[r_pi], wr=[r_pf])
            ang = sbuf(es, "ang", [128, 32, 16]); r_ang = R()
            ang2 = sbuf(es, "ang2", [128, 32, 16]); r_ang2 = R()
            kf = sbuf(es, "kf", [128, 32, 16]); r_kf = R()
            ki = sbuf(es, "ki", [128, 32, 16], I32); r_ki = R()
            P.tt(ang[:], posf[:].unsqueeze(2).to_broadcast([128, 32, 16]),
                 bcp[:, BC_IF:BC_IF + 16].unsqueeze(1).to_broadcast([128, 32, 16]), ALU.mult,
                 rd=[r_pf, r_bcp], wr=[r_ang])

            def range_reduce(a, ra):
                P.ts(kf[:], a[:], 1.0 / TWO_PI, None, ALU.mult, rd=[ra], wr=[r_kf])
                P.cp(ki[:], kf[:], rd=[r_kf], wr=[r_ki])
                P.cp(kf[:], ki[:], rd=[r_ki], wr=[r_kf])
                P.stt(a[:], kf[:], -TWO_PI, a[:], ALU.mult, ALU.add, rd=[r_kf, ra], wr=[ra])
                P.ts(kf[:], a[:], PI, None, ALU.is_gt, rd=[ra], wr=[r_kf])
                P.stt(a[:], kf[:], -TWO_PI, a[:], ALU.mult, ALU.add, rd=[r_kf, ra], wr=[ra])
                P.ts(kf[:], a[:], -PI, None, ALU.is_lt, rd=[ra], wr=[r_kf])
                P.stt(a[:], kf[:], TWO_PI, a[:], ALU.mult, ALU.add, rd=[r_kf, ra], wr=[ra])
                P.ts(a[:], a[:], PI, -PI, ALU.min, ALU.max, rd=[ra], wr=[ra])
            P.ts(ang2[:], ang[:], PI / 2, None, ALU.add, rd=[r_ang], wr=[r_ang2])
            range_reduce(ang, r_ang)
            range_reduce(ang2, r_ang2)
            P.act(sin_t[:], ang[:], AF.Sin, rd=[r_ang], wr=[r_cs])
            P.act(cos_t[:], ang2[:], AF.Sin, rd=[r_ang2], wr=[r_cs])
            P.tt(lb[:], prm[:, PC_LB1:PC_LB1 + 4], prm[:, PC_LB0:PC_LB0 + 4], ALU.subtract, rd=[r_prm], wr=[r_lb])
            P.act(lb[:], lb[:], AF.Sigmoid, rd=[r_lb], wr=[r_lb])
            P.ts(oml[:], lb[:], -1.0, 1.0, ALU.mult, ALU.add, rd=[r_lb], wr=[r_lb])
            P.ts(bcp[:, BC_GQ:BC_GQ + 768], bcp[:, BC_GQ:BC_GQ + 768], 1.0 / float(np.sqrt(96.0)), None, ALU.mult,
                 rd=[r_bcp], wr=[r_bcp])
            P.barrier()
            P.emit()

        if phases < 1:
            es_w1.close()
            es01.close()
            return nc
        with ExitStack() as es:
            banks = [es.enter_context(nc.psum_tensor("p1_b%d" % i, [128, 512], F32)) for i in range(8)]
            rb = [R() for _ in range(8)]
            xring = [sbuf(es, "xr%d" % i, [128, D]) for i in range(2)]; r_x = [R(), R()]
            xnr = [sbuf(es, "xn%d" % i, [128, D], BF16) for i in range(2)]; r_xn = [R(), R()]
            junkb = sbuf(es, "junkb", [128, D], BF16); r_jb = R()
            junkf = sbuf(es, "junkf", [128, D]); r_jf = R()
            st4 = sbuf(es, "st4", [128, 16]); r_st4 = R()
            hTs = [sbuf(es, "hT%d" % i, [128, 8, 512], BF16) for i in range(2)]; r_hTs = [R(), R()]
            cqT = sbuf(es, "cqT", [128, 4, 512], BF16); r_cqT = R()
            ckvT = sbuf(es, "ckvT", [128, 2, 512], BF16); r_ckvT = R()
            sqb = sbuf(es, "sqb", [128, 6, 512], BF16); r_sqb = R()
            rq = sbuf(es, "rq", [128, 16]); r_rq = R()
            silq = sbuf(es, "silq", [128, 512]); r_silq = R()
            sgf = sbuf(es, "sgf", [128, 512]); r_sgf = R()
            silg = sbuf(es, "silg", [128, 512], BF16); r_silg = R()
            gst = [sbuf(es, "gst%d" % i, [128, 512], BF16) for i in range(2)]; r_gst = [R() for _ in range(2)]
            vtok = sbuf(es, "vtok", [128, 4, 512], BF16); r_vtok = [R() for _ in range(4)]
            qst = [sbuf(es, "qst%d" % i, [128, 8, 128], BF16) for i in range(2)]; r_qst = [R(), R()]
            kst = [sbuf(es, "kst%d" % i, [128, 8, 128], BF16) for i in range(2)]; r_kst = [R(), R()]
            vst = [sbuf(es, "vst%d" % i, [128, 512], BF16) for i in range(2)]; r_vst = [R(), R()]
            qn = sbuf(es, "qn", [128, 768]); r_qn = R()
            qf = sbuf(es, "qf", [128, 768], BF16); r_qf = R()
            kfin = sbuf(es, "kfin", [128, 768], BF16); r_kfin = R()
            sm = sbuf(es, "sm", [128, 64]); r_sm = R()
            rp = sbuf(es, "rp", [128, 8, 64]); r_rp = R()
            ff = sbuf(es, "ff", [128, 512]); r_ff = R()
            lf = sbuf(es, "lf", [128, 512]); r_lf = R()
            kk = sbuf(es, "kk", [128, 512]); r_kk = R()
            bcum = sbuf(es, "bcum", [128, 512]); r_bc = R()
            E1 = sbuf(es, "E1", [128, 512]); E2 = sbuf(es, "E2", [128, 512]); r_E = R()
            nbm = sbuf(es, "nbm", [128, 8]); emid = sbuf(es, "emid", [128, 8]); elast = sbuf(es, "elast", [128, 8]); r_eb = R()
            qtT = sbuf(es, "qtT", [128, 512], BF16); ktT = sbuf(es, "ktT", [128, 512], BF16); r_qk = R()
            kttok = sbuf(es, "kttok", [128, 4, 128], BF16); r_kttok = R()
            state = sbuf(es, "state", [128, 4, 128]); r_state = [R() for _ in range(4)]
            stb = [sbuf(es, "stb%d" % i, [128, 128], BF16) for i in range(2)]; r_stb = [R(), R()]
            Am = [sbuf(es, "Am%d" % i, [128, 128], BF16) for i in range(2)]; r_Am = [R(), R()]
            kvt = sbuf(es, "kvt", [128, 128]); r_kvt = R()
            osq = sbuf(es, "osq", [128, 512], BF16); r_osq = R()
            hgst = [sbuf(es, "hgst%d" % i, [128, 512], BF16) for i in range(2)]; r_hgst = [R(), R()]
            P.memset(state[:], 0.0, wr=r_state)

            pF = [banks[0], banks[1]]; r_pF = [rb[0], rb[1]]
            pO, r_pO = banks[2], rb[2]
            pS, r_pS = banks[3], rb[3]
            pKV = (banks[4], banks[5]); r_pKV = [rb[4], rb[5]]
            pQ = (banks[6], banks[7]); r_pQ = [rb[6], rb[7]]
            pfc = [0]

            def nextF():
                i = pfc[0] % 2
                pfc[0] += 1
                return pF[i], r_pF[i]
            gsc = [0]
            hgc = [0]
            stc = [0]

            def rstd(out_ap, in_ap, scale, rd, rw):
                P.act(out_ap, in_ap, AF.Ln, rd=list(rd) + [r_eps], wr=[rw], scale=scale, bias=epsb[:, 0:1])
                P.act(out_ap, out_ap, AF.Exp, rd=[rw], wr=[rw], scale=-0.5)

            def sig_parts(tmp, rtmp, in_ap, rd):
                P.act(tmp, in_ap, AF.Exp, rd=rd, wr=[rtmp], scale=-1.0)
                P.act(tmp, tmp, AF.Ln, rd=[rtmp, r_ones], wr=[rtmp], bias=ones_f[:, 0:1])
                P.act(tmp, tmp, AF.Exp, rd=[rtmp], wr=[rtmp], scale=-1.0)
            tmpS = sbuf(es, "tmpS", [128, 512]); r_tmpS = R()
            smask = sbuf(es, "smask", [128, 512]); r_smask = R()
            P.memset(smask[:], 1.0, wr=[r_smask])
            P.memset(smask[:].rearrange("p (c t) -> p c t", t=64)[:, :, 0:1], 0.0, wr=[r_smask])
            gtmp = [sbuf(es, "gtmp%d" % i, [128, 512]) for i in range(2)]; r_gtmp = [R(), R()]
            print("P1 sbuf bytes remaining:", nc.sbuf_bytes_remaining, flush=True)

            def run_interleaved(gens):
                gens = list(gens)
                while gens:
                    for g in list(gens):
                        try:
                            next(g)
                        except StopIteration:
                            gens.remove(g)

            for b in range(NBLK):
                T0 = b * 512
                hT = hTs[b % 2]
                r_hT = r_hTs[b % 2]

                def fproj(c0, hT=hT, r_hT=r_hT):
                    pf, rpf = nextF()
                    P.mm(pf[:, :], [(win[:, k, c0:c0 + 128], hT[:, k, :]) for k in range(8)],
                         rd=[r_win, r_hT], wr=[rpf])
                    return pf, rpf

                def stage12(bb):
                    TT = bb * 512
                    hTn, r_hTn = hTs[bb % 2], r_hTs[bb % 2]
                    for j in range(4):
                        xt, rxt = xring[j % 2], r_x[j % 2]
                        xn_, rxn = xnr[j % 2], r_xn[j % 2]
                        P.dma(xt[:], x[TT + j * 128:TT + (j + 1) * 128, :], wr=[rxt])
                        P.act(junkb[:], xt[:], AF.Square, rd=[rxt], wr=[r_jb, r_st4], accum_out=st4[:, j:j + 1])
                        rstd(st4[:, 8 + j:9 + j], st4[:, j:j + 1], 1.0 / D, [r_st4], r_st4)
                        P.ts(xn_[:], xt[:], st4[:, 8 + j:9 + j], None, ALU.mult, rd=[rxt, r_st4], wr=[rxn])
                        for k in range(8):
                            bv = banks[4 + k // 2][:].bitcast(BF16)
                            c0 = (k % 2) * 512 + j * 128
                            P.tr(bv[:, c0:c0 + 128], xn_[:, k * 128:(k + 1) * 128], identb,
                                 rd=[rxn, r_cstb], wr=[rb[4 + k // 2]])
                        yield
                    for k in range(8):
                        bv = banks[4 + k // 2][:].bitcast(BF16)
                        P.act(hTn[:, k, :], bv[:, (k % 2) * 512:(k % 2) * 512 + 512], AF.Identity,
                              rd=[rb[4 + k // 2], r_gs, r_mod], wr=[r_hTn], scale=gs1[:, k:k + 1], bias=modT[:, k:k + 1])
                    yield
                    for i in range(4):
                        pf, rpf = fproj(C_CQ + i * 128, hTn, r_hTn)
                        P.act(cqT[:, i, :], pf[:, :], AF.Copy, rd=[rpf, r_prm], wr=[r_cqT], scale=prm[:, PC_GQA + i:PC_GQA + i + 1])
                        P.act(sqb[:, i, :], pf[:, :], AF.Square, rd=[rpf], wr=[r_sqb])
                        yield
                    for i in range(2):
                        pf, rpf = fproj(C_CKV + i * 128, hTn, r_hTn)
                        P.act(ckvT[:, i, :], pf[:, :], AF.Copy, rd=[rpf, r_prm], wr=[r_ckvT], scale=prm[:, PC_GKVA + i:PC_GKVA + i + 1])
                        P.act(sqb[:, 4 + i, :], pf[:, :], AF.Square, rd=[rpf], wr=[r_sqb])
                        yield
                    pss, r_pss = banks[5], rb[5]
                    for j in range(4):
                        P.mm(pss[:, j:j + 1], [(sqb[:, i, j * 128:(j + 1) * 128], ones_b[:, 0:1]) for i in range(4)],
                             rd=[r_sqb, r_ones], wr=[r_pss])
                    for j in range(4):
                        P.mm(pss[:, 4 + j:5 + j], [(sqb[:, 4 + i, j * 128:(j + 1) * 128], ones_b[:, 0:1]) for i in range(2)],
                             rd=[r_sqb, r_ones], wr=[r_pss])
                    rstd(rq[:, 0:4], pss[:, 0:4], 1.0 / 512, [r_pss], r_rq)
                    rstd(rq[:, 4:8], pss[:, 4:8], 1.0 / 256, [r_pss], r_rq)
                    P.tt(rq[:, 8:16], rq[:, 0:8], rq[:, 0:8], ALU.mult, rd=[r_rq], wr=[r_rq])
                    yield
                if b == 0:
                    for _ in stage12(0):
                        pass

                for j in range(4):
                    pf, rpf = nextF()
                    P.mm(pf[:, :], [(hT[:, k, j * 128:(j + 1) * 128], win[:, k, C_HI:C_HI + 512]) for k in range(8)],
                         rd=[r_hT, r_win], wr=[rpf])
                    P.act(vtok[:, j, :], pf[:, :], AF.Copy, rd=[rpf], wr=[r_vtok[j]])
                def gen_B(b=b, T0=T0):
                    for j in range(4):
                        jt = b * 4 + j
                        js = slice(j * 128, (j + 1) * 128)
                        si_ = stc[0] % 2
                        stc[0] += 1
                        P.mm(pQ[0][:, :], [(cqT[:, i, js], wuq[:, i, 0:512]) for i in range(4)], rd=[r_cqT, r_wuq], wr=[r_pQ[0]])
                        P.mm(pQ[1][:, 0:256], [(cqT[:, i, js], wuq[:, i, 512:768]) for i in range(4)], rd=[r_cqT, r_wuq], wr=[r_pQ[1]])
                        P.mm(pQ[1][:, 256:288], [(hT[:, k, js], win[:, k, C_KR:C_KR + 32]) for k in range(8)], rd=[r_hT, r_win], wr=[r_pQ[1]])
                        P.mm(pKV[0][:, :], [(ckvT[:, i, js], wukv[:, i, 0:512]) for i in range(2)], rd=[r_ckvT, r_wukv], wr=[r_pKV[0]])
                        P.mm(pKV[1][:, :], [(ckvT[:, i, js], wukv[:, i, 512:1024]) for i in range(2)], rd=[r_ckvT, r_wukv], wr=[r_pKV[1]])
                        yield
                        cosb8 = cos_t[:, jt, :].unsqueeze(1).to_broadcast([128, 8, 16])
                        sinb8 = sin_t[:, jt, :].unsqueeze(1).to_broadcast([128, 8, 16])
                        P.act(junkf[:, 0:512], pQ[0][:, :], AF.Square, rd=[r_pQ[0]], wr=[r_jf])
                        P.act(junkf[:, 512:768], pQ[1][:, 0:256], AF.Square, rd=[r_pQ[1]], wr=[r_jf])
                        P.red(sm[:, 0:8], junkf[:, 0:768].rearrange("p (h d) -> p h d", d=96), rd=[r_jf], wr=[r_sm])
                        P.ts(sm[:, 0:8], sm[:, 0:8], rq[:, 8 + j:9 + j], 1.0 / 96, ALU.mult, ALU.mult, rd=[r_sm, r_rq], wr=[r_sm])
                        rstd(sm[:, 8:16], sm[:, 0:8], 1.0, [r_sm], r_sm)
                        P.ts(sm[:, 16:24], sm[:, 8:16], rq[:, j:j + 1], None, ALU.mult, rd=[r_sm, r_rq], wr=[r_sm])
                        yield
                        qn3 = qn[:].rearrange("p (h d) -> p h d", d=96)
                        P.tt(qn3[:, 0:5, :], pQ[0][:, 0:480].rearrange("p (h d) -> p h d", d=96),
                             sm[:, 16:21].unsqueeze(2).to_broadcast([128, 5, 96]), ALU.mult, rd=[r_pQ[0], r_sm], wr=[r_qn])
                        P.ts(qn[:, 480:512], pQ[0][:, 480:512], sm[:, 21:22], None, ALU.mult, rd=[r_pQ[0], r_sm], wr=[r_qn])
                        P.ts(qn[:, 512:576], pQ[1][:, 0:64], sm[:, 21:22], None, ALU.mult, rd=[r_pQ[1], r_sm], wr=[r_qn])
                        P.tt(qn3[:, 6:8, :], pQ[1][:, 64:256].rearrange("p (h d) -> p h d", d=96),
                             sm[:, 22:24].unsqueeze(2).to_broadcast([128, 2, 96]), ALU.mult, rd=[r_pQ[1], r_sm], wr=[r_qn])
                        P.tt(qn[:], qn[:], bcp[:, BC_GQ:BC_GQ + 768], ALU.mult, rd=[r_qn, r_bcp], wr=[r_qn])
                        yield
                        qf3 = qf[:].rearrange("p (h d) -> p h d", d=96)
                        rp3 = rp[:]
                        P.tt(rp3[:, :, 0:16], qn3[:, :, 64:80], cosb8, ALU.mult, rd=[r_qn, r_cs], wr=[r_rp])
                        P.tt(rp3[:, :, 16:32], qn3[:, :, 80:96], sinb8, ALU.mult, rd=[r_qn, r_cs], wr=[r_rp])
                        P.tt(qf3[:, :, 64:80], rp3[:, :, 0:16], rp3[:, :, 16:32], ALU.subtract, rd=[r_rp], wr=[r_qf])
                        P.tt(rp3[:, :, 32:48], qn3[:, :, 80:96], cosb8, ALU.mult, rd=[r_qn, r_cs], wr=[r_rp])
                        P.tt(rp3[:, :, 48:64], qn3[:, :, 64:80], sinb8, ALU.mult, rd=[r_qn, r_cs], wr=[r_rp])
                        P.tt(qf3[:, :, 80:96], rp3[:, :, 32:48], rp3[:, :, 48:64], ALU.add, rd=[r_rp], wr=[r_qf])
                        P.cp(qf3[:, :, 0:64], qn3[:, :, 0:64], rd=[r_qn], wr=[r_qf], eng="pool")
                        yield
                        kv0 = pKV[0][:, :].rearrange("p (h d) -> p h d", d=128)
                        kv1 = pKV[1][:, :].rearrange("p (h d) -> p h d", d=128)
                        vst3 = vst[si_][:].rearrange("p (h d) -> p h d", d=64)
                        P.ts(vst3[:, 0:4, :], kv0[:, :, 64:128], rq[:, 4 + j:5 + j], None, ALU.mult, rd=[r_pKV[0], r_rq], wr=[r_vst[si_]])
                        P.ts(vst3[:, 4:8, :], kv1[:, :, 64:128], rq[:, 4 + j:5 + j], None, ALU.mult, rd=[r_pKV[1], r_rq], wr=[r_vst[si_]])
                        P.act(junkf[:, 0:512], pKV[0][:, :], AF.Square, rd=[r_pKV[0]], wr=[r_jf])
                        P.act(junkf[:, 512:1024], pKV[1][:, :], AF.Square, rd=[r_pKV[1]], wr=[r_jf])
                        P.red(sm[:, 24:32], junkf[:].rearrange("p (h d) -> p h d", d=128)[:, :, 0:64], rd=[r_jf], wr=[r_sm])
                        P.act(junkb[:, 0:32], pQ[1][:, 256:288], AF.Square, rd=[r_pQ[1]], wr=[r_jb, r_sm], accum_out=sm[:, 32:33])
                        P.ts(sm[:, 24:32], sm[:, 24:32], rq[:, 12 + j:13 + j], sm[:, 32:33], ALU.mult, ALU.add, rd=[r_sm, r_rq], wr=[r_sm])
                        rstd(sm[:, 40:48], sm[:, 24:32], 1.0 / 96, [r_sm], r_sm)
                        P.ts(sm[:, 48:56], sm[:, 40:48], rq[:, 4 + j:5 + j], None, ALU.mult, rd=[r_sm, r_rq], wr=[r_sm])
                        yield
                        kf3 = kfin[:].rearrange("p (h d) -> p h d", d=96)
                        gkn3 = bcp[:, BC_GKN:BC_GKN + 512].rearrange("p (h d) -> p h d", d=64)
                        P.tt(rp3[:, 0:4, :], kv0[:, :, 0:64], sm[:, 48:52].unsqueeze(2).to_broadcast([128, 4, 64]), ALU.mult,
                             rd=[r_pKV[0], r_sm, r_rp], wr=[r_rp])
                        P.tt(rp3[:, 4:8, :], kv1[:, :, 0:64], sm[:, 52:56].unsqueeze(2).to_broadcast([128, 4, 64]), ALU.mult,
                             rd=[r_pKV[1], r_sm], wr=[r_rp])
                        P.tt(kf3[:, :, 0:64], rp3[:, :, :], gkn3, ALU.mult, rd=[r_rp, r_bcp], wr=[r_kfin])
                        P.tt(junkf[:, 0:32], pQ[1][:, 256:288], bcp[:, BC_GKR:BC_GKR + 32], ALU.mult, rd=[r_pQ[1], r_bcp, r_jf], wr=[r_jf])
                        c16 = cos_t[:, jt, :]
                        s16 = sin_t[:, jt, :]
                        P.tt(junkf[:, 32:48], junkf[:, 0:16], c16, ALU.mult, rd=[r_jf, r_cs], wr=[r_jf])
                        P.tt(junkf[:, 48:64], junkf[:, 16:32], s16, ALU.mult, rd=[r_jf, r_cs], wr=[r_jf])
                        P.tt(junkf[:, 96:112], junkf[:, 32:48], junkf[:, 48:64], ALU.subtract, rd=[r_jf], wr=[r_jf])
                        P.tt(junkf[:, 64:80], junkf[:, 16:32], c16, ALU.mult, rd=[r_jf, r_cs], wr=[r_jf])
                        P.tt(junkf[:, 80:96], junkf[:, 0:16], s16, ALU.mult, rd=[r_jf, r_cs], wr=[r_jf])
                        P.tt(junkf[:, 112:128], junkf[:, 64:80], junkf[:, 80:96], ALU.add, rd=[r_jf], wr=[r_jf])
                        P.tt(kf3[:, :, 64:96], junkf[:, 96:128].unsqueeze(1).to_broadcast([128, 8, 32]),
                             sm[:, 40:48].unsqueeze(2).to_broadcast([128, 8, 32]), ALU.mult, rd=[r_jf, r_sm], wr=[r_kfin])
                        yield
                        qbv = pQ[0][:].bitcast(BF16)
                        kbv = pKV[0][:].bitcast(BF16)
                        for h in range(8):
                            P.tr(qbv[0:96, h * 128:(h + 1) * 128], qf[:, h * 96:(h + 1) * 96], identb, rd=[r_qf, r_cstb], wr=[r_pQ[0]])
                        for h in range(8):
                            P.tr(kbv[0:96, h * 128:(h + 1) * 128], kfin[:, h * 96:(h + 1) * 96], identb, rd=[r_kfin, r_cstb], wr=[r_pKV[0]])
                        P.act(qst[si_][0:96, :, :], qbv[0:96, :].rearrange("p (h t) -> p h t", t=128), AF.Copy, rd=[r_pQ[0]], wr=[r_qst[si_]])
                        P.cp(kst[si_][0:96, :, :], kbv[0:96, :].rearrange("p (h t) -> p h t", t=128), rd=[r_pKV[0]], wr=[r_kst[si_]])
                        P.dma(qT_scr[:, :, T0 + j * 128:T0 + (j + 1) * 128], qst[si_][0:96, :, :], rd=[r_qst[si_]])
                        P.dma(kT_scr[:, :, T0 + j * 128:T0 + (j + 1) * 128], kst[si_][0:96, :, :], rd=[r_kst[si_]])
                        P.dma(V_scr[T0 + j * 128:T0 + (j + 1) * 128, :], vst[si_][:], rd=[r_vst[si_]])
                        yield

                def gen_C(b=b, T0=T0):
                    for h in range(4):
                        hc = slice(h * 128, (h + 1) * 128)
                        pf, rpf = fproj(C_HF + h * 128)
                        sig_parts(tmpS[:], r_tmpS, pf[:, :], [rpf])
                        P.ts(ff[:], tmpS[:], oml[:, h:h + 1], lb[:, h:h + 1], ALU.mult, ALU.add, rd=[r_tmpS, r_lb], wr=[r_ff])
                        yield
                        pf, rpf = fproj(C_HQ + h * 128)
                        sig_parts(tmpS[:], r_tmpS, pf[:, :], [rpf])
                        P.tt(silq[:], pf[:, :], tmpS[:], ALU.mult, rd=[rpf, r_tmpS], wr=[r_silq])
                        yield
                        pf, rpf = fproj(C_HG + h * 128)
                        sig_parts(tmpS[:], r_tmpS, pf[:, :], [rpf])
                        P.tt(silg[:], pf[:, :], tmpS[:], ALU.mult, rd=[rpf, r_tmpS], wr=[r_silg])
                        yield
                        P.act(lf[:], ff[:], AF.Ln, rd=[r_ff], wr=[r_lf])
                        P.ts(kk[:], ff[:], -1.0, 1.0, ALU.mult, ALU.add, rd=[r_ff], wr=[r_kk])
                        P.op("dve", lambda e: e.tensor_tensor_scan(out=bcum[:], data0=smask[:], data1=lf[:],
                                                                   initial=0.0, op0=ALU.mult, op1=ALU.add),
                             [r_lf, r_smask], [r_bc], n=512)
                        b3 = bcum[:].rearrange("p (c t) -> p c t", t=64)
                        P.tt(lf[:].rearrange("p (c t) -> p c t", t=64), b3, b3[:, :, 31:32].to_broadcast([128, 8, 64]), ALU.subtract,
                             rd=[r_bc], wr=[r_lf])
                        yield
                        P.act(E1[:], lf[:], AF.Exp, rd=[r_lf], wr=[r_E])
                        P.act(E2[:], lf[:], AF.Exp, rd=[r_lf], wr=[r_E], scale=-1.0)
                        yield
                        P.act(emid[:], b3[:, :, 31], AF.Exp, rd=[r_bc], wr=[r_eb])
                        P.act(elast[:], b3[:, :, 63], AF.Exp, rd=[r_bc], wr=[r_eb])
                        P.tt(qtT[:], silq[:], E1[:], ALU.mult, rd=[r_silq, r_E], wr=[r_qk])
                        P.tt(ktT[:], kk[:], E2[:], ALU.mult, rd=[r_kk, r_E], wr=[r_qk])
                        psb = pS[:].bitcast(BF16)
                        for j in range(4):
                            P.tr(psb[:, 256 + j * 128:256 + (j + 1) * 128], ktT[:, j * 128:(j + 1) * 128], identb, rd=[r_qk, r_cstb], wr=[r_pS])
                        P.cp(kttok[:], psb[:, 256:768].rearrange("p (j k) -> p j k", k=128), rd=[r_pS], wr=[r_kttok])
                        yield
                        E13 = E1[:].rearrange("p (c t) -> p c t", t=64)
                        for j in range(4):
                            js = slice(j * 128, (j + 1) * 128)
                            ai = j % 2
                            P.mm(pS[:, 0:128], [(ktT[:, js], qtT[:, js])], rd=[r_qk], wr=[r_pS])
                            P.tt(Am[ai][:], pS[:, 0:128], bdtri_f, ALU.mult, rd=[r_pS, r_cstf], wr=[r_Am[ai]])
                            for cc in range(2):
                                c = 2 * j + cc
                                cs = slice(c * 64, (c + 1) * 64)
                                r0 = cc * 64
                                si = c % 2
                                P.mm(pS[:, 384:512], [(kttok[r0:r0 + 64, j, :], vtok[r0:r0 + 64, j, hc])], rd=[r_kttok, r_vtok[j]], wr=[r_pS])
                                P.ts(stb[si][:], state[:, h, :], emid[:, c:c + 1], None, ALU.mult, rd=[r_state[h], r_eb], wr=[r_stb[si]])
                                P.mm(pO[:, cs], [(stb[si][:], qtT[:, cs]), (vtok[:, j, hc], Am[ai][:, r0:r0 + 64])],
                                     rd=[r_stb[si], r_qk, r_vtok[j], r_Am[ai]], wr=[r_pO])
                                P.ts(kvt[:], pS[:, 384:512], E13[:, c, 63:64], None, ALU.mult, rd=[r_pS, r_E], wr=[r_kvt])
                                P.stt(state[:, h, :], state[:, h, :], elast[:, c:c + 1], kvt[:], ALU.mult, ALU.add,
                                      rd=[r_state[h], r_eb, r_kvt], wr=[r_state[h]])
                                yield
                        P.act(osq[:], pO[:, :], AF.Square, rd=[r_pO], wr=[r_osq])
                        pf, rpf = nextF()
                        P.mm(pf[:, :], [(ones_b[:, :], osq[:])], rd=[r_ones, r_osq], wr=[rpf])
                        rstd(ff[:], pf[:, :], 1.0 / 128, [rpf], r_ff)
                        P.tt(lf[:], pO[:, :], ff[:], ALU.mult, rd=[r_pO, r_ff], wr=[r_lf])
                        hi_ = hgc[0] % 2
                        hgc[0] += 1
                        P.stt(hgst[hi_][:], lf[:], prm[:, PC_GOUT:PC_GOUT + 1], silg[:], ALU.mult, ALU.mult,
                              rd=[r_lf, r_prm, r_silg], wr=[r_hgst[hi_]])
                        P.dma(hgo_scr[h * 128:(h + 1) * 128, T0:T0 + 512], hgst[hi_][:], rd=[r_hgst[hi_]])
                        yield

                def gen_D(b=b, T0=T0):
                    for (cbase, dst) in ((C_GA, sga_scr), (C_GB, sgb_scr)):
                        for m in range(8):
                            pf, rpf = fproj(cbase + m * 128)
                            gi = gsc[0] % 2
                            gsc[0] += 1
                            P.act(gtmp[gi][:], pf[:, :], AF.Exp, rd=[rpf], wr=[r_gtmp[gi]], scale=-1.0)
                            P.act(gtmp[gi][:], gtmp[gi][:], AF.Ln, rd=[r_gtmp[gi], r_ones], wr=[r_gtmp[gi]], bias=ones_f[:, 0:1])
                            P.act(gst[gi][:], gtmp[gi][:], AF.Exp, rd=[r_gtmp[gi]], wr=[r_gst[gi]], scale=-1.0)
                            P.dma(dst[m * 128:(m + 1) * 128, T0:T0 + 512], gst[gi][:], rd=[r_gst[gi]])
                            yield
                            yield
                def gen_B_next(b=b):
                    yield from gen_B()
                    if b + 1 < NBLK:
                        yield from stage12(b + 1)
                run_interleaved([gen_B_next(), gen_C(), gen_D()])
            P.barrier()
            P.emit()
        es_w1.close()
        es01.close()

        if phases < 2:
            return nc
        es_w3 = ExitStack()
        wup = sbuf(es_w3, "wup", [128, 8, 2 * DFF], BF16); r_wup = R()
        es_w2b = ExitStack()
        wa = sbuf(es_w2b, "wa", [64, 8, D], BF16); r_wa_ = R()
        wb = sbuf(es_w2b, "wb", [128, 4, D], BF16); r_wb = R()
        wo = sbuf(es_w2b, "wo", [128, 8, D], BF16); r_wo = R()
        prefetch = []
        prefetch.append(lambda rd: P.dma(wa[:], w_ba.rearrange("(h p) n -> p h n", p=64), rd=rd, wr=[r_wa_], eng="pool"))
        prefetch.append(lambda rd: P.dma(wb[:], w_bb.rearrange("(h p) n -> p h n", p=128), rd=rd, wr=[r_wb], eng="pool"))
        for k in range(8):
            prefetch.append(lambda rd, k=k: P.dma(wo[:, k, :], w_out[k * 128:(k + 1) * 128, :], rd=rd, wr=[r_wo], eng="pool"))
        for k in range(8):
            for hf_ in range(2):
                prefetch.append(lambda rd, k=k, hf_=hf_: P.dma(wup[:, k, hf_ * DFF:(hf_ + 1) * DFF],
                                                               w_up[k * 128:(k + 1) * 128, hf_ * DFF:(hf_ + 1) * DFF],
                                                               rd=rd, wr=[r_wup], eng="pool"))
        with ExitStack() as es:
            kTh = [sbuf(es, "kTh%d" % i, [128, S], BF16) for i in range(2)]; r_kTh = [R(), R()]
            vh = [sbuf(es, "vh%d" % i, [128, 32, 65], BF16) for i in range(2)]; r_vh = [R(), R()]
            qTb = [sbuf(es, "qTb%d" % i, [128, 512], BF16) for i in range(2)]; r_qTb = [R(), R()]
            ptr = [sbuf(es, "ptr%d" % i, [128, 512], BF16) for i in range(3)]; r_ptr = [R() for _ in range(3)]
            den = sbuf(es, "den", [128, 512]); r_den = R()
            osb = sbuf(es, "osb", [64, 512]); r_osb = R()
            onst = [sbuf(es, "onst%d" % i, [64, 512], BF16) for i in range(2)]; r_onst = [R(), R()]
            banks = [es.enter_context(nc.psum_tensor("p2_b%d" % i, [128, 512], F32)) for i in range(6)]
            rb = [R() for _ in range(6)]
            for i in range(2):
                P.memset(vh[i][:, :, 64:65], 1.0, wr=[r_vh[i]])
            LA = 2
            items = []
            for h in range(8):
                for qb in range(8):
                    nkt = 4 * qb + 4
                    for kt in range(nkt):
                        items.append((h, qb, kt, nkt))
            n_it = len(items)
            slot_of = {}

            def load_head(h):
                hi_ = h % 2
                P.dma(kTh[hi_][0:96, :], kT_scr[:, h, :], wr=[r_kTh[hi_]])
                for g in range(4):
                    P.dma(vh[hi_][:, g * 8:(g + 1) * 8, 0:64],
                          V_scr[g * 1024:(g + 1) * 1024, h * 64:(h + 1) * 64].rearrange("(kt p) v -> p kt v", p=128),
                          wr=[r_vh[hi_]])

            def load_q(h, qb):
                qi = (h * 8 + qb) % 2
                P.dma(qTb[qi][0:96, :], qT_scr[:, h, qb * 512:(qb + 1) * 512], wr=[r_qTb[qi]])
            load_head(0)
            load_q(0, 0)
            pending = []
            den2 = [sbuf(es, "den2_%d" % i, [128, 512]) for i in range(2)]; r_den2 = [R(), R()]
            osb2 = [sbuf(es, "osb2_%d" % i, [64, 512]) for i in range(2)]; r_osb2 = [R(), R()]
            for idx in range(n_it + LA + 4):
                if idx < n_it:
                    h, qb, kt, nkt = items[idx]
                    hi_ = h % 2
                    qi = (h * 8 + qb) % 2
                    if kt == 0:
                        nq = h * 8 + qb + 1
                        if nq < 64:
                            load_q(nq // 8, nq % 8)
                    r = kt - 4 * qb
                    c0 = 128 * r if r > 0 else 0
                    si = idx % 3
                    pSb, r_pSb = banks[si], rb[si]
                    P.mm(pSb[:, c0:512], [(kTh[hi_][0:96, kt * 128:(kt + 1) * 128], qTb[qi][0:96, c0:512])],
                         rd=[r_kTh[hi_], r_qTb[qi]], wr=[r_pSb])
                    if kt == 0 and prefetch and (h * 8 + qb) % 2 == 0:
                        r_pace = R()
                        P.act(ptr[si][:, c0:512], pSb[:, c0:512], AF.Exp, rd=[r_pSb], wr=[r_ptr[si], r_pace])
                        prefetch.pop(0)([r_pace])
                    else:
                        P.act(ptr[si][:, c0:512], pSb[:, c0:512], AF.Exp, rd=[r_pSb], wr=[r_ptr[si]])
                    if r >= 0:
                        P.tt(ptr[si][:, c0:c0 + 128], ptr[si][:, c0:c0 + 128], tri_b, ALU.mult,
                             rd=[r_ptr[si], r_cstb], wr=[r_ptr[si]])
                i2 = idx - LA
                if 0 <= i2 < n_it:
                    h, qb, kt, nkt = items[i2]
                    hi_ = h % 2
                    qi = (h * 8 + qb) % 2
                    r = kt - 4 * qb
                    c0 = 128 * r if r > 0 else 0
                    si = i2 % 3
                    pOb, r_pOb = banks[3 + qi], rb[3 + qi]
                    if kt == 0 and qb == 0 and h + 1 < 8:
                        load_head(h + 1)
                    P.mm(pOb[0:65, c0:512], [(vh[hi_][:, kt, 0:65], ptr[si][:, c0:512])],
                         rd=[r_vh[hi_], r_ptr[si]], wr=[r_pOb], start=(kt == 0), stop=(kt == nkt - 1))
                    if kt == nkt - 1:
                        P.act(den2[qi][64:65, :], pOb[64:65, :], AF.Ln, rd=[r_pOb], wr=[r_den2[qi]])
                        P.act(den2[qi][64:65, :], den2[qi][64:65, :], AF.Exp, rd=[r_den2[qi]], wr=[r_den2[qi]], scale=-1.0)
                        P.cp(osb2[qi][:], pOb[0:64, :], rd=[r_pOb], wr=[r_osb2[qi]])

                        def tail(h=h, qb=qb, qi=qi):
                            P.mm(banks[5][0:64, :], [(ones_f[64:65, 0:64], den2[qi][64:65, :])], rd=[r_ones, r_den2[qi]], wr=[rb[5]])
                            P.tt(onst[qi][:], osb2[qi][:], banks[5][0:64, :], ALU.mult, rd=[r_osb2[qi], rb[5]], wr=[r_onst[qi]])
                            P.dma(ON_scr[:, h, qb * 512:(qb + 1) * 512], onst[qi][:], rd=[r_onst[qi]])
                        pending.append((idx + 3, tail))
                while pending and pending[0][0] <= idx:
                    pending.pop(0)[1]()
            while prefetch:
                prefetch.pop(0)([])
            P.barrier()
            P.emit()

        if phases < 3:
            es_w2b.close()
            es_w3.close()
            return nc
        with ExitStack() as es:
            g1bc = sbuf(es, "g1bc", [128, D])
            P.dma(g1bc[:], g1src.partition_broadcast(128), wr=[r_gbc])
            onb = [sbuf(es, "onb%d" % i, [64, 8, 512], BF16) for i in range(2)]; r_onb = [R(), R()]
            hgb = [sbuf(es, "hgb%d" % i, [128, 4, 512], BF16) for i in range(2)]; r_hgb = [R(), R()]
            sgl = [sbuf(es, "sgl%d" % i, [128, 2, 512], BF16) for i in range(6)]; r_sgl = [R() for _ in range(6)]
            t1 = sbuf(es, "t1", [128, 512]); t2 = sbuf(es, "t2", [128, 512]); r_t = R()
            mg = sbuf(es, "mg", [128, 8, 512], BF16); r_mg = R()
            xr = [sbuf(es, "x2r%d" % i, [128, D]) for i in range(2)]; r_xr = [R(), R()]
            xo = [sbuf(es, "x2o%d" % i, [128, D]) for i in range(2)]; r_xo = [R(), R()]
            banks = [es.enter_context(nc.psum_tensor("p3_b%d" % i, [128, 512], F32)) for i in range(8)]
            rb = [R() for _ in range(8)]
            def load_blk(b):
                bi = b % 2
                P.dma(onb[bi][:], ON_scr[:, :, b * 512:(b + 1) * 512], wr=[r_onb[bi]])
                P.dma(hgb[bi][:], hgo_scr[:, b * 512:(b + 1) * 512].rearrange("(h p) t -> p h t", p=128), wr=[r_hgb[bi]])

            def load_gate(t):
                b, m = divmod(t, 8)
                gi = t % 6
                ms = slice(m * 128, (m + 1) * 128)
                P.dma(sgl[gi][:, 0, :], sga_scr[ms, b * 512:(b + 1) * 512], wr=[r_sgl[gi]])
                P.dma(sgl[gi][:, 1, :], sgb_scr[ms, b * 512:(b + 1) * 512], wr=[r_sgl[gi]])

            def load_x(t):
                b, j = divmod(t, 4)
                P.dma(xr[t % 2][:], x[b * 512 + j * 128:b * 512 + (j + 1) * 128, :], wr=[r_xr[t % 2]])
            load_blk(0)
            for t in range(4):
                load_gate(t)
            load_x(0)
            for b in range(NBLK):
                T0 = b * 512
                bi = b % 2
                if b + 1 < NBLK:
                    load_blk(b + 1)
                for m in range(8):
                    t = b * 8 + m
                    if t + 4 < NBLK * 8:
                        load_gate(t + 4)
                    ms = slice(m * 128, (m + 1) * 128)
                    gi = t % 6
                    pa, rpa = banks[(2 * m) % 4], rb[(2 * m) % 4]
                    pb, rpb = banks[(2 * m + 1) % 4], rb[(2 * m + 1) % 4]
                    P.mm(pa[:, :], [(wa[:, h, ms], onb[bi][:, h, :]) for h in range(8)], rd=[r_wa_, r_onb[bi]], wr=[rpa])
                    P.mm(pb[:, :], [(wb[:, h, ms], hgb[bi][:, h, :]) for h in range(4)], rd=[r_wb, r_hgb[bi]], wr=[rpb])
                    P.tt(t1[:], pa[:, :], sgl[gi][:, 0, :], ALU.mult, rd=[rpa, r_sgl[gi]], wr=[r_t])
                    P.tt(t2[:], pb[:, :], sgl[gi][:, 1, :], ALU.mult, rd=[rpb, r_sgl[gi]], wr=[r_t])
                    P.tt(mg[:, m, :], t1[:], t2[:], ALU.add, rd=[r_t], wr=[r_mg])
                for j in range(4):
                    js = slice(j * 128, (j + 1) * 128)
                    t = b * 4 + j
                    xi = t % 2
                    if t + 1 < NBLK * 4:
                        load_x(t + 1)
                    p0, rp0 = banks[4 + 2 * xi], rb[4 + 2 * xi]
                    p1, rp1 = banks[5 + 2 * xi], rb[5 + 2 * xi]
                    P.mm(p0[:, :], [(mg[:, k, js], wo[:, k, 0:512]) for k in range(8)], rd=[r_mg, r_wo], wr=[rp0])
                    P.mm(p1[:, :], [(mg[:, k, js], wo[:, k, 512:1024]) for k in range(8)], rd=[r_mg, r_wo], wr=[rp1])
                    P.tt(xo[xi][:, 0:512], p0[:, :], g1bc[:, 0:512], ALU.mult, rd=[rp0, r_gbc], wr=[r_xo[xi]])
                    P.tt(xo[xi][:, 512:1024], p1[:, :], g1bc[:, 512:1024], ALU.mult, rd=[rp1, r_gbc], wr=[r_xo[xi]])
                    P.tt(xo[xi][:], xo[xi][:], xr[xi][:], ALU.add, rd=[r_xo[xi], r_xr[xi]], wr=[r_xo[xi]], eng="pool")
                    P.dma(out[T0 + j * 128:T0 + (j + 1) * 128, :], xo[xi][:], rd=[r_xo[xi]])
            P.barrier()
            P.emit()

        es_w2b.close()
        if phases < 4:
            es_w3.close()
            return nc
        with ExitStack() as es:
            g2bc = sbuf(es, "g2bc", [128, D])
            P.dma(g2bc[:], g2src.partition_broadcast(128), wr=[r_gbc])
            wdn = sbuf(es, "wdn", [128, 22, D], BF16); r_wdn = R()
            for g in range(2):
                P.dma(wdn[:, g * 11:(g + 1) * 11, :], w_dn[g * 1408:(g + 1) * 1408, :].rearrange("(k p) n -> p k n", p=128),
                      wr=[r_wdn], eng="pool")
            xring = [sbuf(es, "x3r%d" % i, [128, D]) for i in range(2)]; r_x = [R(), R()]
            xnr = [sbuf(es, "x3n%d" % i, [128, D], BF16) for i in range(2)]; r_xn = [R(), R()]
            junkb = sbuf(es, "junk3", [128, D], BF16); r_jb = R()
            st4 = sbuf(es, "st43", [128, 16]); r_st4 = R()
            h2T = [sbuf(es, "h2T%d" % i, [128, 8, 512], BF16) for i in range(2)]; r_h2T = [R(), R()]
            actT = sbuf(es, "actT", [128, 22, 512], BF16); r_actT = R()
            uext = [sbuf(es, "uext%d" % i, [128, 514]) for i in range(2)]; r_ue = [R(), R()]
            yv = [sbuf(es, "yv%d" % i, [128, 512]) for i in range(2)]; r_yv = [R(), R()]
            gsil = sbuf(es, "gsil", [128, 512]); r_gsil = R()
            halo = sbuf(es, "halo", [128, 44, 2]); r_halo = [R() for _ in range(44)]
            r_uh = [R(), R()]
            _xo = sbuf(es, "x3o", [128, D]); _rxo = R()
            xo = [_xo, _xo]; r_xo = [_rxo, _rxo]
            banks = [es.enter_context(nc.psum_tensor("p4_b%d" % i, [128, 512], F32)) for i in range(8)]
            rb = [R() for _ in range(8)]
            P.memset(halo[:], 0.0, wr=r_halo)
            r_out = [[R() for _ in range(4)] for _ in range(NBLK)]
            uc = 0
            fc = 0
            xc = 0
            def stage1(b):
                T0 = b * 512
                hT_ = h2T[b % 2]
                for j in range(4):
                    xt, rxt = xring[j % 2], r_x[j % 2]
                    xn_, rxn = xnr[j % 2], r_xn[j % 2]
                    P.dma(xt[:], out[T0 + j * 128:T0 + (j + 1) * 128, :], rd=[r_out[b][j]], wr=[rxt])
                    P.act(junkb[:], xt[:], AF.Square, rd=[rxt], wr=[r_jb, r_st4], accum_out=st4[:, j:j + 1])
                    P.act(st4[:, 4 + j:5 + j], st4[:, j:j + 1], AF.Sqrt, rd=[r_st4, r_eps], wr=[r_st4],
                          scale=1.0 / D, bias=epsb[:, 0:1])
                    P.recip(st4[:, 8 + j:9 + j], st4[:, 4 + j:5 + j], rd=[r_st4], wr=[r_st4])
                    P.ts(xn_[:], xt[:], st4[:, 8 + j:9 + j], None, ALU.mult, rd=[rxt, r_st4], wr=[rxn])
                    for k in range(8):
                        bv = banks[4 + k // 2][:].bitcast(BF16)
                        c0 = (k % 2) * 512 + j * 128
                        P.tr(bv[:, c0:c0 + 128], xn_[:, k * 128:(k + 1) * 128], identb, rd=[rxn, r_cstb], wr=[rb[4 + k // 2]])
                for k in range(8):
                    bv = banks[4 + k // 2][:].bitcast(BF16)
                    P.act(hT_[:, k, :], bv[:, (k % 2) * 512:(k % 2) * 512 + 512], AF.Identity,
                          rd=[rb[4 + k // 2], r_gs, r_mod], wr=[r_h2T[b % 2]], scale=gs2[:, k:k + 1], bias=modT[:, 24 + k:25 + k])
            stage1(0)
            print("P3 sbuf bytes remaining:", nc.sbuf_bytes_remaining, flush=True)
            for b in range(NBLK):
                T0 = b * 512
                h2c, r_h2c = h2T[b % 2], r_h2T[b % 2]

                def upchunk(c):
                    nonlocal uc, fc
                    pf, rpf = banks[fc % 4], rb[fc % 4]
                    fc += 1
                    ui = uc % 2
                    uc += 1
                    ue, rue = uext[ui], r_ue[ui]
                    y, ry = yv[ui], r_yv[ui]
                    P.mm(pf[:, :], [(wup[:, k, c * 128:(c + 1) * 128], h2c[:, k, :]) for k in range(8)], rd=[r_wup, r_h2c], wr=[rpf])
                    ruh = r_uh[ui]
                    P.cp(ue[:, 0:2], halo[:, c, :], rd=[r_halo[c]], wr=[ruh])
                    P.act(ue[:, 2:514], pf[:, :], AF.Copy, rd=[rpf], wr=[rue])
                    P.act(y[:], pf[:, :], AF.Identity, rd=[rpf, r_prm], wr=[ry],
                          scale=prm[:, PC_CW + 2 * 44 + c:PC_CW + 2 * 44 + c + 1], bias=prm[:, PC_CB + c:PC_CB + c + 1])
                    P.stt(y[:], ue[:, 1:513], prm[:, PC_CW + 44 + c:PC_CW + 44 + c + 1], y[:], ALU.mult, ALU.add, rd=[rue, ruh, r_prm, ry], wr=[ry])
                    P.stt(y[:], ue[:, 0:512], prm[:, PC_CW + c:PC_CW + c + 1], y[:], ALU.mult, ALU.add, rd=[rue, ruh, r_prm, ry], wr=[ry])
                    P.cp(halo[:, c, :], ue[:, 512:514], rd=[rue], wr=[r_halo[c]])
                    return y, ry
                for c in range(22):
                    y, ry = upchunk(c)
                    y2, ry2 = upchunk(22 + c)
                    P.act(gsil[:], y[:], AF.Silu, rd=[ry], wr=[r_gsil])
                    P.tt(actT[:, c, :], gsil[:], y2[:], ALU.mult, rd=[r_gsil, ry2], wr=[r_actT])
                    if c == 15 and b + 1 < NBLK:
                        stage1(b + 1)
                for j in range(4):
                    js = slice(j * 128, (j + 1) * 128)
                    xi = xc % 2
                    xc += 1
                    xt, rxt = xring[xi], r_x[xi]
                    P.dma(xt[:], out[T0 + j * 128:T0 + (j + 1) * 128, :], rd=[r_out[b][j]], wr=[rxt])
                    p0, rp0 = banks[4 + 2 * xi], rb[4 + 2 * xi]
                    p1, rp1 = banks[5 + 2 * xi], rb[5 + 2 * xi]
                    P.mm(p0[:, :], [(actT[:, k, js], wdn[:, k, 0:512]) for k in range(22)], rd=[r_actT, r_wdn], wr=[rp0])
                    P.mm(p1[:, :], [(actT[:, k, js], wdn[:, k, 512:1024]) for k in range(22)], rd=[r_actT, r_wdn], wr=[rp1])
                    P.tt(xo[xi][:, 0:512], p0[:, :], g2bc[:, 0:512], ALU.mult, rd=[rp0, r_gbc], wr=[r_xo[xi]])
                    P.tt(xo[xi][:, 512:1024], p1[:, :], g2bc[:, 512:1024], ALU.mult, rd=[rp1, r_gbc], wr=[r_xo[xi]])
                    P.tt(xo[xi][:], xo[xi][:], xt[:], ALU.add, rd=[r_xo[xi], rxt], wr=[r_xo[xi]])
                    P.dma(out[T0 + j * 128:T0 + (j + 1) * 128, :], xo[xi][:], rd=[r_xo[xi]], wr=[r_out[b][j]])
            P.barrier()
            P.emit()
        es_w3.close()
    return nc


def _host_layout(inputs):
    f32 = np.float32
    g = {k: np.asarray(v) for k, v in inputs.items()}

    def fm(v):
        v = np.asarray(v, f32).reshape(-1, 128)
        return np.ascontiguousarray(v.T)
    shared = np.zeros((128, NP), f32)
    shared[:, PC_BADA:PC_BADA + 48] = fm(g["b_ada"][0])
    shared[:, PC_G1:PC_G1 + 8] = fm(g["norm1_g"][0])
    shared[:, PC_G2:PC_G2 + 8] = fm(g["norm2_g"][0])
    shared[:, PC_GQA:PC_GQA + 4] = fm(g["q_a_norm_g"][0])
    shared[:, PC_GKVA:PC_GKVA + 2] = fm(g["kv_a_norm_g"][0])
    shared[:, PC_LB0:PC_LB0 + 4] = fm(g["hg_lower_bound"][0])
    shared[:, PC_LB1:PC_LB1 + 4] = fm(g["hg_lower_bound"][1])
    shared[:, PC_GOUT] = np.asarray(g["hg_out_norm_g"][0], f32)
    shared[:, PC_CB:PC_CB + 44] = fm(g["conv_b"][0])
    for jj in range(3):
        shared[:, PC_CW + jj * 44:PC_CW + (jj + 1) * 44] = fm(g["conv_w"][0][jj])
    bc = np.zeros((128, NB), f32)
    bc[:, BC_GQ:BC_GQ + 768] = np.tile(np.asarray(g["q_norm_g"][0], f32), 8)[None, :]
    bc[:, BC_GKN:BC_GKN + 512] = np.tile(np.asarray(g["k_norm_g"][0], f32)[:64], 8)[None, :]
    bc[:, BC_GKR:BC_GKR + 32] = np.asarray(g["k_norm_g"][0], f32)[64:96][None, :]
    inv_freq = (np.float32(10000.0) ** (-np.arange(0, 32, 2, dtype=np.float32) / np.float32(32))).astype(f32)
    bc[:, BC_IF:BC_IF + 16] = inv_freq[None, :]
    p = np.arange(128)
    cst = np.zeros((128, 384), f32)
    cst[:, 0:128] = np.eye(128, dtype=f32)
    cst[:, 128:256] = (p[:, None] <= p[None, :]).astype(f32)
    cst[:, 256:384] = ((p[:, None] <= p[None, :]) & ((p[:, None] // 64) == (p[None, :] // 64))).astype(f32)
    common = {
        "bcp": bc, "cst": cst,
        "w_ada": np.ascontiguousarray(g["w_ada"][0], f32), "w_in": np.ascontiguousarray(g["w_in"][0], f32),
        "w_uq": np.ascontiguousarray(g["w_uq"][0], f32), "w_ukv": np.ascontiguousarray(g["w_ukv"][0], f32),
        "w_ba": np.ascontiguousarray(g["w_branch_a"][0], f32), "w_bb": np.ascontiguousarray(g["w_branch_b"][0], f32),
        "w_out": np.ascontiguousarray(g["w_out"][0], f32), "w_up": np.ascontiguousarray(g["w_up"][0], f32),
        "w_dn": np.ascontiguousarray(g["w_down"][0], f32),
    }
    in_maps = []
    for c in range(8):
        prm = shared.copy()
        prm[:, PC_C:PC_C + 8] = fm(g["c"][c])
        m = dict(common)
        m["x"] = np.ascontiguousarray(g["x"][c], f32)
        m["prm"] = prm
        m["posT"] = np.ascontiguousarray(np.asarray(g["positions"][c], np.int32).reshape(32, 128).T)
        in_maps.append(m)
    return in_maps


def kernel(**inputs):
    in_maps = _host_layout(inputs)
    nc = build()
    res = run_bass_kernel_spmd(nc, in_maps, core_ids=list(range(8)))
    return np.stack([np.asarray(r["out"], np.float32) for r in res.results], axis=0)
```

```python
import numpy as np
import concourse.bass as bass
import concourse.mybir as mybir
from concourse.bass_utils import run_bass_kernel_spmd
from contextlib import ExitStack

F32 = mybir.dt.float32
BF16 = mybir.dt.bfloat16
I32 = mybir.dt.int32
AF = mybir.ActivationFunctionType
ALU = mybir.AluOpType
AX = mybir.AxisListType

S = 4096
D = 1024
NBLK = 8
EPS = 1e-6
DFF = 2816
TWO_PI = 6.283185307179586
PI = 3.141592653589793

NDS = 12
ENGS = ("pe", "act", "dve", "pool", "sp")


class R:
    __slots__ = ("w", "r", "name")

    def __init__(self, name=""):
        self.w = None
        self.r = []
        self.name = name


class Prog:
    def __init__(self, nc, es):
        self.nc = nc
        self.sem = {e: es.enter_context(nc.semaphore("s_" + e)) for e in ENGS if e != "sp"}
        self.cnt = {e: 0 for e in ENGS}
        self.seen = {e: {} for e in ENGS}
        self.q = {e: [] for e in ENGS}
        self.dsem = {e: [es.enter_context(nc.semaphore("d_%s%d" % (e, i))) for i in range(NDS)]
                     for e in ("sp", "pool")}
        self.dcnt = {e: [0] * NDS for e in ("sp", "pool")}
        self.dnext = {e: 0 for e in ("sp", "pool")}
        self.opn = {}

    def _semh(self, key):
        if key[0] == "c":
            return self.sem[key[1]]
        return self.dsem[key[1]][key[2]]

    def op(self, eng, fn, rd=(), wr=(), dma=False, n=1):
        deps = []
        raw = set()
        for t in rd:
            if t.w is not None:
                deps.append(t.w)
                raw.add(t.w)
        for t in wr:
            if t.w is not None:
                deps.append(t.w)
            deps.extend(t.r)
        waits = {}
        if dma:
            slot = self.dnext[eng]
            self.dnext[eng] = (slot + 1) % NDS
            key = ("d", eng, slot)
            prev = self.dcnt[eng][slot]
            if prev > 0:
                deps.append((key, prev, eng))
            self.dcnt[eng][slot] = prev + 16
            me = (key, prev + 16, eng)
            sig = (self.dsem[eng][slot], 16)
        else:
            self.cnt[eng] += 1
            key = ("c", eng)
            me = (key, self.cnt[eng], eng)
            self.opn[me] = n
            sig = (self.sem[eng], 1)
        seen = self.seen[eng]
        for (k, v, e) in deps:
            if k[0] == "c" and e == eng:
                if eng == "pe" or (k, v, e) not in raw:
                    continue
                if n >= 256 and self.opn.get((k, v, e), 1) >= 256:
                    continue
            if seen.get(k, 0) >= v:
                continue
            if waits.get(k, 0) < v:
                waits[k] = v
        for k, v in waits.items():
            seen[k] = v
        for t in rd:
            if len(t.r) > 64:
                t.r = t.r[-32:] if t.w is None or True else t.r
            t.r.append(me)
        for t in wr:
            t.w = me
            t.r = []
        self.q[eng].append(([(self._semh(k), v) for k, v in waits.items()], fn, sig))

    def barrier(self):
        for e in ENGS:
            waits = []
            for e2 in ENGS:
                if e2 != "sp" and e2 != e and self.cnt[e2] > 0 and self.seen[e].get(("c", e2), 0) < self.cnt[e2]:
                    waits.append((self.sem[e2], self.cnt[e2]))
                    self.seen[e][("c", e2)] = self.cnt[e2]
            for qe in ("sp", "pool"):
                for i in range(NDS):
                    v = self.dcnt[qe][i]
                    if v > 0 and self.seen[e].get(("d", qe, i), 0) < v:
                        waits.append((self.dsem[qe][i], v))
                        self.seen[e][("d", qe, i)] = v
            self.q[e].append((waits, None, None))

    def emit(self):
        nc = self.nc
        q = self.q
        print("emit ops:", {e: (len(v), sum(len(w) for w, _, _ in v)) for e, v in q.items()}, flush=True)

        def run(e, items):
            for waits, fn, sig in items:
                for (sh, v) in waits:
                    e.wait_ge(sh, v)
                if fn is not None:
                    ins = fn(e)
                    ins.then_inc(sig[0], sig[1])

        with nc.allow_low_precision("bf16 staging of fp32-computed values is intended"), nc.Block() as block:
            @block.tensor
            def _(e):
                run(e, q["pe"])

            @block.scalar
            def _(e):
                run(e, q["act"])

            @block.vector
            def _(e):
                run(e, q["dve"])

            @block.gpsimd
            def _(e):
                run(e, q["pool"])

            @block.sync
            def _(e):
                run(e, q["sp"])
        self.q = {e: [] for e in ENGS}

    def dma(self, out, in_, rd=(), wr=(), eng="sp", **kw):
        self.op(eng, lambda e: e.dma_start(out=out, in_=in_, **kw), rd, wr, dma=True)

    def mm(self, out, pairs, rd=(), wr=(), start=True, stop=True):
        n = len(pairs)

        def fn(e):
            ins = None
            for i, (l, r) in enumerate(pairs):
                ins = e.matmul(out, l, r, start=(start and i == 0), stop=(stop and i == n - 1))
            return ins
        self.op("pe", fn, rd, wr)

    def tr(self, out, in_, ident, rd=(), wr=()):
        self.op("pe", lambda e: e.transpose(out, in_, ident), rd, wr)

    @staticmethod
    def _n(ap):
        n = 1
        for d in ap.shape[1:]:
            n *= int(d)
        return n

    def act(self, out, in_, func, rd=(), wr=(), **kw):
        n = 1 if "accum_out" in kw else self._n(out)
        self.op("act", lambda e: e.activation(out=out, in_=in_, func=func, **kw), rd, wr, n=n)

    def ts(self, out, in0, s1, s2, op0, op1=None, rd=(), wr=(), eng="dve"):
        if op1 is None:
            self.op(eng, lambda e: e.tensor_scalar(out=out, in0=in0, scalar1=s1, scalar2=None, op0=op0), rd, wr, n=self._n(out))
        else:
            self.op(eng, lambda e: e.tensor_scalar(out=out, in0=in0, scalar1=s1, scalar2=s2, op0=op0, op1=op1), rd, wr, n=self._n(out))

    def tt(self, out, in0, in1, op, rd=(), wr=(), eng="dve"):
        self.op(eng, lambda e: e.tensor_tensor(out=out, in0=in0, in1=in1, op=op), rd, wr, n=self._n(out))

    def stt(self, out, in0, scalar, in1, op0, op1, rd=(), wr=(), eng="dve"):
        self.op(eng, lambda e: e.scalar_tensor_tensor(out=out, in0=in0, scalar=scalar, in1=in1, op0=op0, op1=op1), rd, wr, n=self._n(out))

    def cp(self, out, in_, rd=(), wr=(), eng="dve"):
        self.op(eng, lambda e: e.tensor_copy(out=out, in_=in_), rd, wr, n=self._n(out))

    def red(self, out, in_, rd=(), wr=(), eng="dve"):
        self.op(eng, lambda e: e.tensor_reduce(out=out, in_=in_, axis=AX.X, op=ALU.add), rd, wr)

    def recip(self, out, in_, rd=(), wr=()):
        self.op("dve", lambda e: e.reciprocal(out=out, in_=in_), rd, wr, n=self._n(out))

    def memset(self, ap, val, wr=(), eng="pool"):
        self.op(eng, lambda e: e.memset(ap, val), (), wr)


PC_BADA, PC_G1, PC_G2, PC_GQA, PC_GKVA, PC_LB0, PC_LB1, PC_GOUT, PC_C, PC_CB, PC_CW = 0, 48, 56, 64, 68, 70, 74, 78, 79, 87, 131
NP = 131 + 132
BC_GQ, BC_GKN, BC_GKR, BC_IF = 0, 768, 1280, 1312
NB = 1328
C_CQ, C_CKV, C_KR, C_HQ, C_HF, C_HI, C_HG, C_GA, C_GB = 0, 512, 768, 800, 1312, 1824, 2336, 2848, 3872


def build(debug=False, phases=99):
    nc = bass.Bass("TRN2", target_bir_lowering=False)

    def din(name, shape, dt=F32):
        return nc.dram_tensor(name, shape, dt, kind="ExternalInput").ap()

    x = din("x", [S, D])
    prm_d = din("prm", [128, NP])
    bcp_d = din("bcp", [128, NB])
    pos_d = din("posT", [128, 32], I32)
    cst_d = din("cst", [128, 3 * 128])
    w_ada = din("w_ada", [D, 6 * D])
    w_in = din("w_in", [D, 4896])
    w_uq = din("w_uq", [512, 768])
    w_ukv = din("w_ukv", [256, 1024])
    w_ba = din("w_ba", [512, D])
    w_bb = din("w_bb", [512, D])
    w_out = din("w_out", [D, D])
    w_up = din("w_up", [D, 2 * DFF])
    w_dn = din("w_dn", [DFF, D])
    out = nc.dram_tensor("out", [S, D], F32, kind="ExternalOutput").ap()

    def scr(name, shape, dt):
        kind = "ExternalOutput" if debug else "Internal"
        return nc.dram_tensor(name, shape, dt, kind=kind).ap()

    modscr = scr("modscr", [48, 128], F32)
    qT_scr = scr("qT_scr", [96, 8, S], BF16)
    kT_scr = scr("kT_scr", [96, 8, S], BF16)
    V_scr = scr("V_scr", [S, 512], BF16)
    hgo_scr = scr("hgo_scr", [512, S], BF16)
    sga_scr = scr("sga_scr", [D, S], BF16)
    sgb_scr = scr("sgb_scr", [D, S], BF16)
    ON_scr = scr("ON_scr", [64, 8, S], BF16)

    with ExitStack() as es0:
        P = Prog(nc, es0)

        def sbuf(es, name, shape, dt=F32):
            return es.enter_context(nc.sbuf_tensor("sb_" + name, shape, dt))

        prm = sbuf(es0, "prm", [128, NP]); r_prm = R()
        cstf = sbuf(es0, "cstf", [128, 384]); r_cstf = R()
        cstb = sbuf(es0, "cstb", [128, 384], BF16); r_cstb = R()
        modT = sbuf(es0, "modT", [128, 48]); r_mod = R()
        gs1 = sbuf(es0, "gs1", [128, 8]); gs2 = sbuf(es0, "gs2", [128, 8]); r_gs = R()
        r_gbc = R()
        lb = sbuf(es0, "lb", [128, 4]); oml = sbuf(es0, "oml", [128, 4]); r_lb = R()
        epsb = sbuf(es0, "epsb", [128, 1]); r_eps = R()
        ones_b = sbuf(es0, "ones_b", [128, 128], BF16); ones_f = sbuf(es0, "ones_f", [128, 64]); r_ones = R()
        es01 = ExitStack()
        bcp = sbuf(es01, "bcp", [128, NB]); r_bcp = R()
        cos_t = sbuf(es01, "cos_t", [128, 32, 16]); sin_t = sbuf(es01, "sin_t", [128, 32, 16]); r_cs = R()
        identb = cstb[:, 0:128]
        tri_f = cstf[:, 128:256]
        bdtri_f = cstf[:, 256:384]
        tri_b = cstb[:, 128:256]
        identf = cstf[:, 0:128]

        P.dma(prm[:], prm_d[:, :], wr=[r_prm])
        P.dma(bcp[:], bcp_d[:, :], wr=[r_bcp])
        P.dma(cstf[:], cst_d[:, :], wr=[r_cstf])
        P.dma(cstb[:], cst_d[:, :], wr=[r_cstb], eng="pool")
        P.memset(epsb[:], EPS, wr=[r_eps])
        P.memset(ones_b[:], 1.0, wr=[r_ones])
        P.memset(ones_f[:], 1.0, wr=[r_ones])

        es_w1 = ExitStack()
        win = sbuf(es_w1, "win", [128, 8, 4896], BF16); r_win = R()
        wuq = sbuf(es_w1, "wuq", [128, 4, 768], BF16); r_wuq = R()
        wukv = sbuf(es_w1, "wukv", [128, 2, 1024], BF16); r_wukv = R()
        for k in range(8):
            P.dma(win[:, k, :], w_in[k * 128:(k + 1) * 128, :], wr=[r_win], eng="pool")
        P.dma(wuq[:], w_uq.rearrange("(k p) n -> p k n", p=128), wr=[r_wuq], eng="pool")
        P.dma(wukv[:], w_ukv.rearrange("(k p) n -> p k n", p=128), wr=[r_wukv], eng="pool")
        with ExitStack() as es:
            wa_ring = [sbuf(es, "wada%d" % i, [128, 6 * D]) for i in range(2)]
            r_wa = [R(), R()]
            pm = es.enter_context(nc.psum_tensor("p0_pm", [128, 512], F32)); r_pm = R()
            pt = es.enter_context(nc.psum_tensor("p0_pt", [128, 512], F32)); r_pt = R()
            for k in range(8):
                wt = wa_ring[k % 2]
                P.dma(wt[:], w_ada[k * 128:(k + 1) * 128, :], wr=[r_wa[k % 2]])

                def f0(e, wt=wt, k=k):
                    ins = None
                    for j in range(48):
                        ins = e.matmul(pm[:, k * 48 + j:k * 48 + j + 1], wt[:, j * 128:(j + 1) * 128],
                                       prm[:, PC_C + k:PC_C + k + 1], start=True, stop=True)
                    return ins
                P.op("pe", f0, [r_wa[k % 2], r_prm], [r_pm])
            P.red(modT[:], pm[:, 0:384].rearrange("p (k j) -> p j k", k=8), rd=[r_pm], wr=[r_mod])
            P.tt(modT[:], modT[:], prm[:, PC_BADA:PC_BADA + 48], ALU.add, rd=[r_mod, r_prm], wr=[r_mod])
            P.stt(gs1[:], modT[:, 8:16], 1.0, prm[:, PC_G1:PC_G1 + 8], ALU.add, ALU.mult, rd=[r_mod, r_prm], wr=[r_gs])
            P.stt(gs2[:], modT[:, 32:40], 1.0, prm[:, PC_G2:PC_G2 + 8], ALU.add, ALU.mult, rd=[r_mod, r_prm], wr=[r_gs])
            P.tr(pt[0:48, 0:128], modT[:, 0:48], identf, rd=[r_mod, r_cstf], wr=[r_pt])
            modrow = sbuf(es, "modrow", [48, 128]); r_mr = R()
            P.cp(modrow[:], pt[0:48, 0:128], rd=[r_pt], wr=[r_mr])
            r_ms = R()
            P.dma(modscr[:, :], modrow[:], rd=[r_mr], wr=[r_ms])
            g1src = modscr[16:24, :].rearrange("(o a) b -> o (a b)", o=1)
            g2src = modscr[40:48, :].rearrange("(o a) b -> o (a b)", o=1)
            posi = sbuf(es, "posi", [128, 32], I32); r_pi = R()
            posf = sbuf(es, "posf", [128, 32]); r_pf = R()
            P.dma(posi[:], pos_d[:, :], wr=[r_pi])
            P.cp(posf[:], posi[:], rd=[r_pi], wr=[r_pf])
            ang = sbuf(es, "ang", [128, 32, 16]); r_ang = R()
            ang2 = sbuf(es, "ang2", [128, 32, 16]); r_ang2 = R()
            kf = sbuf(es, "kf", [128, 32, 16]); r_kf = R()
            ki = sbuf(es, "ki", [128, 32, 16], I32); r_ki = R()
            P.tt(ang[:], posf[:].unsqueeze(2).to_broadcast([128, 32, 16]),
                 bcp[:, BC_IF:BC_IF + 16].unsqueeze(1).to_broadcast([128, 32, 16]), ALU.mult,
                 rd=[r_pf, r_bcp], wr=[r_ang])

            def range_reduce(a, ra):
                P.ts(kf[:], a[:], 1.0 / TWO_PI, None, ALU.mult, rd=[ra], wr=[r_kf])
                P.cp(ki[:], kf[:], rd=[r_kf], wr=[r_ki])
                P.cp(kf[:], ki[:], rd=[r_ki], wr=[r_kf])
                P.stt(a[:], kf[:], -TWO_PI, a[:], ALU.mult, ALU.add, rd=[r_kf, ra], wr=[ra])
                P.ts(kf[:], a[:], PI, None, ALU.is_gt, rd=[ra], wr=[r_kf])
                P.stt(a[:], kf[:], -TWO_PI, a[:], ALU.mult, ALU.add, rd=[r_kf, ra], wr=[ra])
                P.ts(kf[:], a[:], -PI, None, ALU.is_lt, rd=[ra], wr=[r_kf])
                P.stt(a[:], kf[:], TWO_PI, a[:], ALU.mult, ALU.add, rd=[r_kf, ra], wr=[ra])
                P.ts(a[:], a[:], PI, -PI, ALU.min, ALU.max, rd=[ra], wr=[ra])
            P.ts(ang2[:], ang[:], PI / 2, None, ALU.add, rd=[r_ang], wr=[r_ang2])
            range_reduce(ang, r_ang)
            range_reduce(ang2, r_ang2)
            P.act(sin_t[:], ang[:], AF.Sin, rd=[r_ang], wr=[r_cs])
            P.act(cos_t[:], ang2[:], AF.Sin, rd=[r_ang2], wr=[r_cs])
            P.tt(lb[:], prm[:, PC_LB1:PC_LB1 + 4], prm[:, PC_LB0:PC_LB0 + 4], ALU.subtract, rd=[r_prm], wr=[r_lb])
            P.act(lb[:], lb[:], AF.Sigmoid, rd=[r_lb], wr=[r_lb])
            P.ts(oml[:], lb[:], -1.0, 1.0, ALU.mult, ALU.add, rd=[r_lb], wr=[r_lb])
            P.ts(bcp[:, BC_GQ:BC_GQ + 768], bcp[:, BC_GQ:BC_GQ + 768], 1.0 / float(np.sqrt(96.0)), None, ALU.mult,
                 rd=[r_bcp], wr=[r_bcp])
            P.barrier()
            P.emit()

        if phases < 1:
            es_w1.close()
            es01.close()
            return nc
        with ExitStack() as es:
            banks = [es.enter_context(nc.psum_tensor("p1_b%d" % i, [128, 512], F32)) for i in range(8)]
            rb = [R() for _ in range(8)]
            xring = [sbuf(es, "xr%d" % i, [128, D]) for i in range(2)]; r_x = [R(), R()]
            xnr = [sbuf(es, "xn%d" % i, [128, D], BF16) for i in range(2)]; r_xn = [R(), R()]
            junkb = sbuf(es, "junkb", [128, D], BF16); r_jb = R()
            junkf = sbuf(es, "junkf", [128, D]); r_jf = R()
            st4 = sbuf(es, "st4", [128, 16]); r_st4 = R()
            hT = sbuf(es, "hT", [128, 8, 512], BF16); r_hT = R()
            cqT = sbuf(es, "cqT", [128, 4, 512], BF16); r_cqT = R()
            ckvT = sbuf(es, "ckvT", [128, 2, 512], BF16); r_ckvT = R()
            sqb = sbuf(es, "sqb", [128, 6, 512], BF16); r_sqb = R()
            rq = sbuf(es, "rq", [128, 16]); r_rq = R()
            silq = sbuf(es, "silq", [128, 512]); r_silq = R()
            sgf = sbuf(es, "sgf", [128, 512]); r_sgf = R()
            silg = sbuf(es, "silg", [128, 512], BF16); r_silg = R()
            gst = [sbuf(es, "gst%d" % i, [128, 512], BF16) for i in range(2)]; r_gst = [R() for _ in range(2)]
            vtok = sbuf(es, "vtok", [128, 4, 512], BF16); r_vtok = [R() for _ in range(4)]
            qst = [sbuf(es, "qst%d" % i, [128, 8, 128], BF16) for i in range(2)]; r_qst = [R(), R()]
            kst = [sbuf(es, "kst%d" % i, [128, 8, 128], BF16) for i in range(2)]; r_kst = [R(), R()]
            vst = [sbuf(es, "vst%d" % i, [128, 512], BF16) for i in range(2)]; r_vst = [R(), R()]
            qn = sbuf(es, "qn", [128, 768]); r_qn = R()
            qf = sbuf(es, "qf", [128, 768], BF16); r_qf = R()
            kfin = sbuf(es, "kfin", [128, 768], BF16); r_kfin = R()
            sm = sbuf(es, "sm", [128, 64]); r_sm = R()
            rp = sbuf(es, "rp", [128, 8, 64]); r_rp = R()
            ff = sbuf(es, "ff", [128, 512]); r_ff = R()
            lf = sbuf(es, "lf", [128, 512]); r_lf = R()
            kk = sbuf(es, "kk", [128, 512]); r_kk = R()
            bcum = sbuf(es, "bcum", [128, 512]); r_bc = R()
            E1 = sbuf(es, "E1", [128, 512]); E2 = sbuf(es, "E2", [128, 512]); r_E = R()
            nbm = sbuf(es, "nbm", [128, 8]); emid = sbuf(es, "emid", [128, 8]); elast = sbuf(es, "elast", [128, 8]); r_eb = R()
            qtT = sbuf(es, "qtT", [128, 512], BF16); ktT = sbuf(es, "ktT", [128, 512], BF16); r_qk = R()
            kttok = sbuf(es, "kttok", [128, 4, 128], BF16); r_kttok = R()
            state = sbuf(es, "state", [128, 4, 128]); r_state = [R() for _ in range(4)]
            stb = [sbuf(es, "stb%d" % i, [128, 128], BF16) for i in range(2)]; r_stb = [R(), R()]
            Am = [sbuf(es, "Am%d" % i, [128, 128], BF16) for i in range(2)]; r_Am = [R(), R()]
            kvt = sbuf(es, "kvt", [128, 128]); r_kvt = R()
            osq = sbuf(es, "osq", [128, 512], BF16); r_osq = R()
            hgst = [sbuf(es, "hgst%d" % i, [128, 512], BF16) for i in range(2)]; r_hgst = [R(), R()]
            P.memset(state[:], 0.0, wr=r_state)

            pF = [banks[0], banks[1]]; r_pF = [rb[0], rb[1]]
            pO, r_pO = banks[2], rb[2]
            pS, r_pS = banks[3], rb[3]
            pKV = (banks[4], banks[5]); r_pKV = [rb[4], rb[5]]
            pQ = (banks[6], banks[7]); r_pQ = [rb[6], rb[7]]
            pfc = [0]

            def nextF():
                i = pfc[0] % 2
                pfc[0] += 1
                return pF[i], r_pF[i]
            gsc = [0]
            hgc = [0]
            stc = [0]

            def rstd(out_ap, in_ap, scale, rd, rw):
                P.act(out_ap, in_ap, AF.Ln, rd=list(rd) + [r_eps], wr=[rw], scale=scale, bias=epsb[:, 0:1])
                P.act(out_ap, out_ap, AF.Exp, rd=[rw], wr=[rw], scale=-0.5)

            def sig_parts(tmp, rtmp, in_ap, rd):
                P.act(tmp, in_ap, AF.Exp, rd=rd, wr=[rtmp], scale=-1.0)
                P.act(tmp, tmp, AF.Ln, rd=[rtmp, r_ones], wr=[rtmp], bias=ones_f[:, 0:1])
                P.act(tmp, tmp, AF.Exp, rd=[rtmp], wr=[rtmp], scale=-1.0)
            tmpS = sbuf(es, "tmpS", [128, 512]); r_tmpS = R()
            smask = sbuf(es, "smask", [128, 512]); r_smask = R()
            P.memset(smask[:], 1.0, wr=[r_smask])
            P.memset(smask[:].rearrange("p (c t) -> p c t", t=64)[:, :, 0:1], 0.0, wr=[r_smask])
            gtmp = [sbuf(es, "gtmp%d" % i, [128, 512]) for i in range(2)]; r_gtmp = [R(), R()]
            print("P1 sbuf bytes remaining:", nc.sbuf_bytes_remaining, flush=True)

            def run_interleaved(gens):
                gens = list(gens)
                while gens:
                    for g in list(gens):
                        try:
                            next(g)
                        except StopIteration:
                            gens.remove(g)

            for b in range(NBLK):
                T0 = b * 512
                for j in range(4):
                    xt, rxt = xring[j % 2], r_x[j % 2]
                    xn_, rxn = xnr[j % 2], r_xn[j % 2]
                    P.dma(xt[:], x[T0 + j * 128:T0 + (j + 1) * 128, :], wr=[rxt])
                    P.act(junkb[:], xt[:], AF.Square, rd=[rxt], wr=[r_jb, r_st4], accum_out=st4[:, j:j + 1])
                    rstd(st4[:, 8 + j:9 + j], st4[:, j:j + 1], 1.0 / D, [r_st4], r_st4)
                    P.ts(xn_[:], xt[:], st4[:, 8 + j:9 + j], None, ALU.mult, rd=[rxt, r_st4], wr=[rxn])
                    for k in range(8):
                        bv = banks[4 + k // 2][:].bitcast(BF16)
                        c0 = (k % 2) * 512 + j * 128
                        P.tr(bv[:, c0:c0 + 128], xn_[:, k * 128:(k + 1) * 128], identb,
                             rd=[rxn, r_cstb], wr=[rb[4 + k // 2]])
                for k in range(8):
                    bv = banks[4 + k // 2][:].bitcast(BF16)
                    P.act(hT[:, k, :], bv[:, (k % 2) * 512:(k % 2) * 512 + 512], AF.Identity,
                          rd=[rb[4 + k // 2], r_gs, r_mod], wr=[r_hT], scale=gs1[:, k:k + 1], bias=modT[:, k:k + 1])

                def fproj(c0):
                    pf, rpf = nextF()
                    P.mm(pf[:, :], [(win[:, k, c0:c0 + 128], hT[:, k, :]) for k in range(8)],
                         rd=[r_win, r_hT], wr=[rpf])
                    return pf, rpf
                for i in range(4):
                    pf, rpf = fproj(C_CQ + i * 128)
                    P.act(cqT[:, i, :], pf[:, :], AF.Copy, rd=[rpf, r_prm], wr=[r_cqT], scale=prm[:, PC_GQA + i:PC_GQA + i + 1])
                    P.act(sqb[:, i, :], pf[:, :], AF.Square, rd=[rpf], wr=[r_sqb])
                for i in range(2):
                    pf, rpf = fproj(C_CKV + i * 128)
                    P.act(ckvT[:, i, :], pf[:, :], AF.Copy, rd=[rpf, r_prm], wr=[r_ckvT], scale=prm[:, PC_GKVA + i:PC_GKVA + i + 1])
                    P.act(sqb[:, 4 + i, :], pf[:, :], AF.Square, rd=[rpf], wr=[r_sqb])
                for j in range(4):
                    P.mm(pS[:, j:j + 1], [(sqb[:, i, j * 128:(j + 1) * 128], ones_b[:, 0:1]) for i in range(4)],
                         rd=[r_sqb, r_ones], wr=[r_pS])
                for j in range(4):
                    P.mm(pS[:, 4 + j:5 + j], [(sqb[:, 4 + i, j * 128:(j + 1) * 128], ones_b[:, 0:1]) for i in range(2)],
                         rd=[r_sqb, r_ones], wr=[r_pS])
                rstd(rq[:, 0:4], pS[:, 0:4], 1.0 / 512, [r_pS], r_rq)
                rstd(rq[:, 4:8], pS[:, 4:8], 1.0 / 256, [r_pS], r_rq)
                P.tt(rq[:, 8:16], rq[:, 0:8], rq[:, 0:8], ALU.mult, rd=[r_rq], wr=[r_rq])

                for j in range(4):
                    pf, rpf = nextF()
                    P.mm(pf[:, :], [(hT[:, k, j * 128:(j + 1) * 128], win[:, k, C_HI:C_HI + 512]) for k in range(8)],
                         rd=[r_hT, r_win], wr=[rpf])
                    P.act(vtok[:, j, :], pf[:, :], AF.Copy, rd=[rpf], wr=[r_vtok[j]])
                def gen_B(b=b, T0=T0):
                    for j in range(4):
                        jt = b * 4 + j
                        js = slice(j * 128, (j + 1) * 128)
                        si_ = stc[0] % 2
                        stc[0] += 1
                        P.mm(pQ[0][:, :], [(cqT[:, i, js], wuq[:, i, 0:512]) for i in range(4)], rd=[r_cqT, r_wuq], wr=[r_pQ[0]])
                        P.mm(pQ[1][:, 0:256], [(cqT[:, i, js], wuq[:, i, 512:768]) for i in range(4)], rd=[r_cqT, r_wuq], wr=[r_pQ[1]])
                        P.mm(pQ[1][:, 256:288], [(hT[:, k, js], win[:, k, C_KR:C_KR + 32]) for k in range(8)], rd=[r_hT, r_win], wr=[r_pQ[1]])
                        P.mm(pKV[0][:, :], [(ckvT[:, i, js], wukv[:, i, 0:512]) for i in range(2)], rd=[r_ckvT, r_wukv], wr=[r_pKV[0]])
                        P.mm(pKV[1][:, :], [(ckvT[:, i, js], wukv[:, i, 512:1024]) for i in range(2)], rd=[r_ckvT, r_wukv], wr=[r_pKV[1]])
                        yield
                        cosb8 = cos_t[:, jt, :].unsqueeze(1).to_broadcast([128, 8, 16])
                        sinb8 = sin_t[:, jt, :].unsqueeze(1).to_broadcast([128, 8, 16])
                        P.act(junkf[:, 0:512], pQ[0][:, :], AF.Square, rd=[r_pQ[0]], wr=[r_jf])
                        P.act(junkf[:, 512:768], pQ[1][:, 0:256], AF.Square, rd=[r_pQ[1]], wr=[r_jf])
                        P.red(sm[:, 0:8], junkf[:, 0:768].rearrange("p (h d) -> p h d", d=96), rd=[r_jf], wr=[r_sm])
                        P.ts(sm[:, 0:8], sm[:, 0:8], rq[:, 8 + j:9 + j], 1.0 / 96, ALU.mult, ALU.mult, rd=[r_sm, r_rq], wr=[r_sm])
                        rstd(sm[:, 8:16], sm[:, 0:8], 1.0, [r_sm], r_sm)
                        P.ts(sm[:, 16:24], sm[:, 8:16], rq[:, j:j + 1], None, ALU.mult, rd=[r_sm, r_rq], wr=[r_sm])
                        yield
                        qn3 = qn[:].rearrange("p (h d) -> p h d", d=96)
                        P.tt(qn3[:, 0:5, :], pQ[0][:, 0:480].rearrange("p (h d) -> p h d", d=96),
                             sm[:, 16:21].unsqueeze(2).to_broadcast([128, 5, 96]), ALU.mult, rd=[r_pQ[0], r_sm], wr=[r_qn])
                        P.ts(qn[:, 480:512], pQ[0][:, 480:512], sm[:, 21:22], None, ALU.mult, rd=[r_pQ[0], r_sm], wr=[r_qn])
                        P.ts(qn[:, 512:576], pQ[1][:, 0:64], sm[:, 21:22], None, ALU.mult, rd=[r_pQ[1], r_sm], wr=[r_qn])
                        P.tt(qn3[:, 6:8, :], pQ[1][:, 64:256].rearrange("p (h d) -> p h d", d=96),
                             sm[:, 22:24].unsqueeze(2).to_broadcast([128, 2, 96]), ALU.mult, rd=[r_pQ[1], r_sm], wr=[r_qn])
                        P.tt(qn[:], qn[:], bcp[:, BC_GQ:BC_GQ + 768], ALU.mult, rd=[r_qn, r_bcp], wr=[r_qn], eng="pool")
                        yield
                        qf3 = qf[:].rearrange("p (h d) -> p h d", d=96)
                        rp3 = rp[:]
                        P.tt(rp3[:, :, 0:16], qn3[:, :, 64:80], cosb8, ALU.mult, rd=[r_qn, r_cs], wr=[r_rp], eng="pool")
                        P.tt(rp3[:, :, 16:32], qn3[:, :, 80:96], sinb8, ALU.mult, rd=[r_qn, r_cs], wr=[r_rp], eng="pool")
                        P.tt(qf3[:, :, 64:80], rp3[:, :, 0:16], rp3[:, :, 16:32], ALU.subtract, rd=[r_rp], wr=[r_qf], eng="pool")
                        P.tt(rp3[:, :, 32:48], qn3[:, :, 80:96], cosb8, ALU.mult, rd=[r_qn, r_cs], wr=[r_rp], eng="pool")
                        P.tt(rp3[:, :, 48:64], qn3[:, :, 64:80], sinb8, ALU.mult, rd=[r_qn, r_cs], wr=[r_rp], eng="pool")
                        P.tt(qf3[:, :, 80:96], rp3[:, :, 32:48], rp3[:, :, 48:64], ALU.add, rd=[r_rp], wr=[r_qf], eng="pool")
                        P.cp(qf3[:, :, 0:64], qn3[:, :, 0:64], rd=[r_qn], wr=[r_qf], eng="pool")
                        yield
                        kv0 = pKV[0][:, :].rearrange("p (h d) -> p h d", d=128)
                        kv1 = pKV[1][:, :].rearrange("p (h d) -> p h d", d=128)
                        vst3 = vst[si_][:].rearrange("p (h d) -> p h d", d=64)
                        P.ts(vst3[:, 0:4, :], kv0[:, :, 64:128], rq[:, 4 + j:5 + j], None, ALU.mult, rd=[r_pKV[0], r_rq], wr=[r_vst[si_]])
                        P.ts(vst3[:, 4:8, :], kv1[:, :, 64:128], rq[:, 4 + j:5 + j], None, ALU.mult, rd=[r_pKV[1], r_rq], wr=[r_vst[si_]])
                        P.act(junkf[:, 0:512], pKV[0][:, :], AF.Square, rd=[r_pKV[0]], wr=[r_jf])
                        P.act(junkf[:, 512:1024], pKV[1][:, :], AF.Square, rd=[r_pKV[1]], wr=[r_jf])
                        P.red(sm[:, 24:32], junkf[:].rearrange("p (h d) -> p h d", d=128)[:, :, 0:64], rd=[r_jf], wr=[r_sm])
                        P.act(junkb[:, 0:32], pQ[1][:, 256:288], AF.Square, rd=[r_pQ[1]], wr=[r_jb, r_sm], accum_out=sm[:, 32:33])
                        P.ts(sm[:, 24:32], sm[:, 24:32], rq[:, 12 + j:13 + j], sm[:, 32:33], ALU.mult, ALU.add, rd=[r_sm, r_rq], wr=[r_sm])
                        rstd(sm[:, 40:48], sm[:, 24:32], 1.0 / 96, [r_sm], r_sm)
                        P.ts(sm[:, 48:56], sm[:, 40:48], rq[:, 4 + j:5 + j], None, ALU.mult, rd=[r_sm, r_rq], wr=[r_sm])
                        yield
                        kf3 = kfin[:].rearrange("p (h d) -> p h d", d=96)
                        gkn3 = bcp[:, BC_GKN:BC_GKN + 512].rearrange("p (h d) -> p h d", d=64)
                        P.tt(rp3[:, 0:4, :], kv0[:, :, 0:64], sm[:, 48:52].unsqueeze(2).to_broadcast([128, 4, 64]), ALU.mult,
                             rd=[r_pKV[0], r_sm, r_rp], wr=[r_rp])
                        P.tt(rp3[:, 4:8, :], kv1[:, :, 0:64], sm[:, 52:56].unsqueeze(2).to_broadcast([128, 4, 64]), ALU.mult,
                             rd=[r_pKV[1], r_sm], wr=[r_rp])
                        P.tt(kf3[:, :, 0:64], rp3[:, :, :], gkn3, ALU.mult, rd=[r_rp, r_bcp], wr=[r_kfin], eng="pool")
                        P.tt(junkf[:, 0:32], pQ[1][:, 256:288], bcp[:, BC_GKR:BC_GKR + 32], ALU.mult, rd=[r_pQ[1], r_bcp, r_jf], wr=[r_jf])
                        c16 = cos_t[:, jt, :]
                        s16 = sin_t[:, jt, :]
                        P.tt(junkf[:, 32:48], junkf[:, 0:16], c16, ALU.mult, rd=[r_jf, r_cs], wr=[r_jf], eng="pool")
                        P.tt(junkf[:, 48:64], junkf[:, 16:32], s16, ALU.mult, rd=[r_jf, r_cs], wr=[r_jf], eng="pool")
                        P.tt(junkf[:, 96:112], junkf[:, 32:48], junkf[:, 48:64], ALU.subtract, rd=[r_jf], wr=[r_jf], eng="pool")
                        P.tt(junkf[:, 64:80], junkf[:, 16:32], c16, ALU.mult, rd=[r_jf, r_cs], wr=[r_jf], eng="pool")
                        P.tt(junkf[:, 80:96], junkf[:, 0:16], s16, ALU.mult, rd=[r_jf, r_cs], wr=[r_jf], eng="pool")
                        P.tt(junkf[:, 112:128], junkf[:, 64:80], junkf[:, 80:96], ALU.add, rd=[r_jf], wr=[r_jf], eng="pool")
                        P.tt(kf3[:, :, 64:96], junkf[:, 96:128].unsqueeze(1).to_broadcast([128, 8, 32]),
                             sm[:, 40:48].unsqueeze(2).to_broadcast([128, 8, 32]), ALU.mult, rd=[r_jf, r_sm], wr=[r_kfin], eng="pool")
                        yield
                        qbv = pQ[0][:].bitcast(BF16)
                        kbv = pKV[0][:].bitcast(BF16)
                        for h in range(8):
                            P.tr(qbv[0:96, h * 128:(h + 1) * 128], qf[:, h * 96:(h + 1) * 96], identb, rd=[r_qf, r_cstb], wr=[r_pQ[0]])
                        for h in range(8):
                            P.tr(kbv[0:96, h * 128:(h + 1) * 128], kfin[:, h * 96:(h + 1) * 96], identb, rd=[r_kfin, r_cstb], wr=[r_pKV[0]])
                        P.act(qst[si_][0:96, :, :], qbv[0:96, :].rearrange("p (h t) -> p h t", t=128), AF.Copy, rd=[r_pQ[0]], wr=[r_qst[si_]])
                        P.cp(kst[si_][0:96, :, :], kbv[0:96, :].rearrange("p (h t) -> p h t", t=128), rd=[r_pKV[0]], wr=[r_kst[si_]])
                        P.dma(qT_scr[:, :, T0 + j * 128:T0 + (j + 1) * 128], qst[si_][0:96, :, :], rd=[r_qst[si_]])
                        P.dma(kT_scr[:, :, T0 + j * 128:T0 + (j + 1) * 128], kst[si_][0:96, :, :], rd=[r_kst[si_]])
                        P.dma(V_scr[T0 + j * 128:T0 + (j + 1) * 128, :], vst[si_][:], rd=[r_vst[si_]])
                        yield

                def gen_C(b=b, T0=T0):
                    for h in range(4):
                        hc = slice(h * 128, (h + 1) * 128)
                        pf, rpf = fproj(C_HF + h * 128)
                        sig_parts(tmpS[:], r_tmpS, pf[:, :], [rpf])
                        P.ts(ff[:], tmpS[:], oml[:, h:h + 1], lb[:, h:h + 1], ALU.mult, ALU.add, rd=[r_tmpS, r_lb], wr=[r_ff])
                        yield
                        pf, rpf = fproj(C_HQ + h * 128)
                        sig_parts(tmpS[:], r_tmpS, pf[:, :], [rpf])
                        P.tt(silq[:], pf[:, :], tmpS[:], ALU.mult, rd=[rpf, r_tmpS], wr=[r_silq])
                        yield
                        pf, rpf = fproj(C_HG + h * 128)
                        sig_parts(tmpS[:], r_tmpS, pf[:, :], [rpf])
                        P.tt(silg[:], pf[:, :], tmpS[:], ALU.mult, rd=[rpf, r_tmpS], wr=[r_silg])
                        yield
                        P.act(lf[:], ff[:], AF.Ln, rd=[r_ff], wr=[r_lf])
                        P.ts(kk[:], ff[:], -1.0, 1.0, ALU.mult, ALU.add, rd=[r_ff], wr=[r_kk])
                        P.op("dve", lambda e: e.tensor_tensor_scan(out=bcum[:], data0=smask[:], data1=lf[:],
                                                                   initial=0.0, op0=ALU.mult, op1=ALU.add),
                             [r_lf, r_smask], [r_bc], n=512)
                        b3 = bcum[:].rearrange("p (c t) -> p c t", t=64)
                        P.tt(lf[:].rearrange("p (c t) -> p c t", t=64), b3, b3[:, :, 31:32].to_broadcast([128, 8, 64]), ALU.subtract,
                             rd=[r_bc], wr=[r_lf])
                        yield
                        P.act(E1[:], lf[:], AF.Exp, rd=[r_lf], wr=[r_E])
                        P.act(E2[:], lf[:], AF.Exp, rd=[r_lf], wr=[r_E], scale=-1.0)
                        yield
                        P.act(emid[:], b3[:, :, 31], AF.Exp, rd=[r_bc], wr=[r_eb])
                        P.act(elast[:], b3[:, :, 63], AF.Exp, rd=[r_bc], wr=[r_eb])
                        P.tt(qtT[:], silq[:], E1[:], ALU.mult, rd=[r_silq, r_E], wr=[r_qk])
                        P.tt(ktT[:], kk[:], E2[:], ALU.mult, rd=[r_kk, r_E], wr=[r_qk])
                        psb = pS[:].bitcast(BF16)
                        for j in range(4):
                            P.tr(psb[:, 256 + j * 128:256 + (j + 1) * 128], ktT[:, j * 128:(j + 1) * 128], identb, rd=[r_qk, r_cstb], wr=[r_pS])
                        P.cp(kttok[:], psb[:, 256:768].rearrange("p (j k) -> p j k", k=128), rd=[r_pS], wr=[r_kttok])
                        yield
                        E13 = E1[:].rearrange("p (c t) -> p c t", t=64)
                        for j in range(4):
                            js = slice(j * 128, (j + 1) * 128)
                            ai = j % 2
                            P.mm(pS[:, 0:128], [(ktT[:, js], qtT[:, js])], rd=[r_qk], wr=[r_pS])
                            P.tt(Am[ai][:], pS[:, 0:128], bdtri_f, ALU.mult, rd=[r_pS, r_cstf], wr=[r_Am[ai]])
                            for cc in range(2):
                                c = 2 * j + cc
                                cs = slice(c * 64, (c + 1) * 64)
                                r0 = cc * 64
                                si = c % 2
                                P.mm(pS[:, 384:512], [(kttok[r0:r0 + 64, j, :], vtok[r0:r0 + 64, j, hc])], rd=[r_kttok, r_vtok[j]], wr=[r_pS])
                                P.ts(stb[si][:], state[:, h, :], emid[:, c:c + 1], None, ALU.mult, rd=[r_state[h], r_eb], wr=[r_stb[si]])
                                P.mm(pO[:, cs], [(stb[si][:], qtT[:, cs]), (vtok[:, j, hc], Am[ai][:, r0:r0 + 64])],
                                     rd=[r_stb[si], r_qk, r_vtok[j], r_Am[ai]], wr=[r_pO])
                                P.ts(kvt[:], pS[:, 384:512], E13[:, c, 63:64], None, ALU.mult, rd=[r_pS, r_E], wr=[r_kvt])
                                P.stt(state[:, h, :], state[:, h, :], elast[:, c:c + 1], kvt[:], ALU.mult, ALU.add,
                                      rd=[r_state[h], r_eb, r_kvt], wr=[r_state[h]])
                                yield
                        P.act(osq[:], pO[:, :], AF.Square, rd=[r_pO], wr=[r_osq])
                        pf, rpf = nextF()
                        P.mm(pf[:, :], [(ones_b[:, :], osq[:])], rd=[r_ones, r_osq], wr=[rpf])
                        rstd(ff[:], pf[:, :], 1.0 / 128, [rpf], r_ff)
                        P.tt(lf[:], pO[:, :], ff[:], ALU.mult, rd=[r_pO, r_ff], wr=[r_lf])
                        hi_ = hgc[0] % 2
                        hgc[0] += 1
                        P.stt(hgst[hi_][:], lf[:], prm[:, PC_GOUT:PC_GOUT + 1], silg[:], ALU.mult, ALU.mult,
                              rd=[r_lf, r_prm, r_silg], wr=[r_hgst[hi_]])
                        P.dma(hgo_scr[h * 128:(h + 1) * 128, T0:T0 + 512], hgst[hi_][:], rd=[r_hgst[hi_]])
                        yield

                def gen_D(b=b, T0=T0):
                    for (cbase, dst) in ((C_GA, sga_scr), (C_GB, sgb_scr)):
                        for m in range(8):
                            pf, rpf = fproj(cbase + m * 128)
                            gi = gsc[0] % 2
                            gsc[0] += 1
                            P.act(gtmp[gi][:], pf[:, :], AF.Exp, rd=[rpf], wr=[r_gtmp[gi]], scale=-1.0)
                            P.act(gtmp[gi][:], gtmp[gi][:], AF.Ln, rd=[r_gtmp[gi], r_ones], wr=[r_gtmp[gi]], bias=ones_f[:, 0:1])
                            P.act(gst[gi][:], gtmp[gi][:], AF.Exp, rd=[r_gtmp[gi]], wr=[r_gst[gi]], scale=-1.0)
                            P.dma(dst[m * 128:(m + 1) * 128, T0:T0 + 512], gst[gi][:], rd=[r_gst[gi]])
                            yield
                            yield
                run_interleaved([gen_B(), gen_C(), gen_D()])
            P.barrier()
            P.emit()
        es_w1.close()
        es01.close()

        if phases < 2:
            return nc
        es_w3 = ExitStack()
        wup = sbuf(es_w3, "wup", [128, 8, 2 * DFF], BF16); r_wup = R()
        es_w2b = ExitStack()
        wa = sbuf(es_w2b, "wa", [64, 8, D], BF16); r_wa_ = R()
        wb = sbuf(es_w2b, "wb", [128, 4, D], BF16); r_wb = R()
        wo = sbuf(es_w2b, "wo", [128, 8, D], BF16); r_wo = R()
        prefetch = []
        prefetch.append(lambda rd: P.dma(wa[:], w_ba.rearrange("(h p) n -> p h n", p=64), rd=rd, wr=[r_wa_], eng="pool"))
        prefetch.append(lambda rd: P.dma(wb[:], w_bb.rearrange("(h p) n -> p h n", p=128), rd=rd, wr=[r_wb], eng="pool"))
        for k in range(8):
            prefetch.append(lambda rd, k=k: P.dma(wo[:, k, :], w_out[k * 128:(k + 1) * 128, :], rd=rd, wr=[r_wo], eng="pool"))
        for k in range(8):
            for hf_ in range(2):
                prefetch.append(lambda rd, k=k, hf_=hf_: P.dma(wup[:, k, hf_ * DFF:(hf_ + 1) * DFF],
                                                               w_up[k * 128:(k + 1) * 128, hf_ * DFF:(hf_ + 1) * DFF],
                                                               rd=rd, wr=[r_wup], eng="pool"))
        with ExitStack() as es:
            kTh = [sbuf(es, "kTh%d" % i, [128, S], BF16) for i in range(2)]; r_kTh = [R(), R()]
            vh = [sbuf(es, "vh%d" % i, [128, 32, 65], BF16) for i in range(2)]; r_vh = [R(), R()]
            qTb = [sbuf(es, "qTb%d" % i, [128, 512], BF16) for i in range(2)]; r_qTb = [R(), R()]
            ptr = [sbuf(es, "ptr%d" % i, [128, 512], BF16) for i in range(3)]; r_ptr = [R() for _ in range(3)]
            den = sbuf(es, "den", [128, 512]); r_den = R()
            osb = sbuf(es, "osb", [64, 512]); r_osb = R()
            onst = [sbuf(es, "onst%d" % i, [64, 512], BF16) for i in range(2)]; r_onst = [R(), R()]
            banks = [es.enter_context(nc.psum_tensor("p2_b%d" % i, [128, 512], F32)) for i in range(6)]
            rb = [R() for _ in range(6)]
            for i in range(2):
                P.memset(vh[i][:, :, 64:65], 1.0, wr=[r_vh[i]])
            LA = 2
            items = []
            for h in range(8):
                for qb in range(8):
                    nkt = 4 * qb + 4
                    for kt in range(nkt):
                        items.append((h, qb, kt, nkt))
            n_it = len(items)
            slot_of = {}

            def load_head(h):
                hi_ = h % 2
                P.dma(kTh[hi_][0:96, :], kT_scr[:, h, :], wr=[r_kTh[hi_]])
                for g in range(4):
                    P.dma(vh[hi_][:, g * 8:(g + 1) * 8, 0:64],
                          V_scr[g * 1024:(g + 1) * 1024, h * 64:(h + 1) * 64].rearrange("(kt p) v -> p kt v", p=128),
                          wr=[r_vh[hi_]])

            def load_q(h, qb):
                qi = (h * 8 + qb) % 2
                P.dma(qTb[qi][0:96, :], qT_scr[:, h, qb * 512:(qb + 1) * 512], wr=[r_qTb[qi]])
            load_head(0)
            load_q(0, 0)
            pending = []
            den2 = [sbuf(es, "den2_%d" % i, [128, 512]) for i in range(2)]; r_den2 = [R(), R()]
            osb2 = [sbuf(es, "osb2_%d" % i, [64, 512]) for i in range(2)]; r_osb2 = [R(), R()]
            for idx in range(n_it + LA + 4):
                if idx < n_it:
                    h, qb, kt, nkt = items[idx]
                    hi_ = h % 2
                    qi = (h * 8 + qb) % 2
                    if kt == 0:
                        nq = h * 8 + qb + 1
                        if nq < 64:
                            load_q(nq // 8, nq % 8)
                    r = kt - 4 * qb
                    c0 = 128 * r if r > 0 else 0
                    si = idx % 3
                    pSb, r_pSb = banks[si], rb[si]
                    P.mm(pSb[:, c0:512], [(kTh[hi_][0:96, kt * 128:(kt + 1) * 128], qTb[qi][0:96, c0:512])],
                         rd=[r_kTh[hi_], r_qTb[qi]], wr=[r_pSb])
                    if kt == 0 and prefetch and (h * 8 + qb) % 2 == 0:
                        r_pace = R()
                        P.act(ptr[si][:, c0:512], pSb[:, c0:512], AF.Exp, rd=[r_pSb], wr=[r_ptr[si], r_pace])
                        prefetch.pop(0)([r_pace])
                    else:
                        P.act(ptr[si][:, c0:512], pSb[:, c0:512], AF.Exp, rd=[r_pSb], wr=[r_ptr[si]])
                    if r >= 0:
                        P.tt(ptr[si][:, c0:c0 + 128], ptr[si][:, c0:c0 + 128], tri_b, ALU.mult,
                             rd=[r_ptr[si], r_cstb], wr=[r_ptr[si]])
                i2 = idx - LA
                if 0 <= i2 < n_it:
                    h, qb, kt, nkt = items[i2]
                    hi_ = h % 2
                    qi = (h * 8 + qb) % 2
                    r = kt - 4 * qb
                    c0 = 128 * r if r > 0 else 0
                    si = i2 % 3
                    pOb, r_pOb = banks[3 + qi], rb[3 + qi]
                    if kt == 0 and qb == 0 and h + 1 < 8:
                        load_head(h + 1)
                    P.mm(pOb[0:65, c0:512], [(vh[hi_][:, kt, 0:65], ptr[si][:, c0:512])],
                         rd=[r_vh[hi_], r_ptr[si]], wr=[r_pOb], start=(kt == 0), stop=(kt == nkt - 1))
                    if kt == nkt - 1:
                        P.act(den2[qi][64:65, :], pOb[64:65, :], AF.Ln, rd=[r_pOb], wr=[r_den2[qi]])
                        P.act(den2[qi][64:65, :], den2[qi][64:65, :], AF.Exp, rd=[r_den2[qi]], wr=[r_den2[qi]], scale=-1.0)
                        P.cp(osb2[qi][:], pOb[0:64, :], rd=[r_pOb], wr=[r_osb2[qi]])

                        def tail(h=h, qb=qb, qi=qi):
                            P.mm(banks[5][0:64, :], [(ones_f[64:65, 0:64], den2[qi][64:65, :])], rd=[r_ones, r_den2[qi]], wr=[rb[5]])
                            P.tt(onst[qi][:], osb2[qi][:], banks[5][0:64, :], ALU.mult, rd=[r_osb2[qi], rb[5]], wr=[r_onst[qi]])
                            P.dma(ON_scr[:, h, qb * 512:(qb + 1) * 512], onst[qi][:], rd=[r_onst[qi]])
                        pending.append((idx + 3, tail))
                while pending and pending[0][0] <= idx:
                    pending.pop(0)[1]()
            while prefetch:
                prefetch.pop(0)([])
            P.barrier()
            P.emit()

        if phases < 3:
            es_w2b.close()
            es_w3.close()
            return nc
        with ExitStack() as es:
            g1bc = sbuf(es, "g1bc", [128, D])
            P.dma(g1bc[:], g1src.partition_broadcast(128), wr=[r_gbc])
            onb = [sbuf(es, "onb%d" % i, [64, 8, 512], BF16) for i in range(2)]; r_onb = [R(), R()]
            hgb = [sbuf(es, "hgb%d" % i, [128, 4, 512], BF16) for i in range(2)]; r_hgb = [R(), R()]
            sgl = [sbuf(es, "sgl%d" % i, [128, 2, 512], BF16) for i in range(6)]; r_sgl = [R() for _ in range(6)]
            t1 = sbuf(es, "t1", [128, 512]); t2 = sbuf(es, "t2", [128, 512]); r_t = R()
            mg = sbuf(es, "mg", [128, 8, 512], BF16); r_mg = R()
            xr = [sbuf(es, "x2r%d" % i, [128, D]) for i in range(2)]; r_xr = [R(), R()]
            xo = [sbuf(es, "x2o%d" % i, [128, D]) for i in range(2)]; r_xo = [R(), R()]
            banks = [es.enter_context(nc.psum_tensor("p3_b%d" % i, [128, 512], F32)) for i in range(8)]
            rb = [R() for _ in range(8)]
            def load_blk(b):
                bi = b % 2
                P.dma(onb[bi][:], ON_scr[:, :, b * 512:(b + 1) * 512], wr=[r_onb[bi]])
                P.dma(hgb[bi][:], hgo_scr[:, b * 512:(b + 1) * 512].rearrange("(h p) t -> p h t", p=128), wr=[r_hgb[bi]])

            def load_gate(t):
                b, m = divmod(t, 8)
                gi = t % 6
                ms = slice(m * 128, (m + 1) * 128)
                P.dma(sgl[gi][:, 0, :], sga_scr[ms, b * 512:(b + 1) * 512], wr=[r_sgl[gi]])
                P.dma(sgl[gi][:, 1, :], sgb_scr[ms, b * 512:(b + 1) * 512], wr=[r_sgl[gi]])

            def load_x(t):
                b, j = divmod(t, 4)
                P.dma(xr[t % 2][:], x[b * 512 + j * 128:b * 512 + (j + 1) * 128, :], wr=[r_xr[t % 2]])
            load_blk(0)
            for t in range(4):
                load_gate(t)
            load_x(0)
            for b in range(NBLK):
                T0 = b * 512
                bi = b % 2
                if b + 1 < NBLK:
                    load_blk(b + 1)
                for m in range(8):
                    t = b * 8 + m
                    if t + 4 < NBLK * 8:
                        load_gate(t + 4)
                    ms = slice(m * 128, (m + 1) * 128)
                    gi = t % 6
                    pa, rpa = banks[(2 * m) % 4], rb[(2 * m) % 4]
                    pb, rpb = banks[(2 * m + 1) % 4], rb[(2 * m + 1) % 4]
                    P.mm(pa[:, :], [(wa[:, h, ms], onb[bi][:, h, :]) for h in range(8)], rd=[r_wa_, r_onb[bi]], wr=[rpa])
                    P.mm(pb[:, :], [(wb[:, h, ms], hgb[bi][:, h, :]) for h in range(4)], rd=[r_wb, r_hgb[bi]], wr=[rpb])
                    P.tt(t1[:], pa[:, :], sgl[gi][:, 0, :], ALU.mult, rd=[rpa, r_sgl[gi]], wr=[r_t])
                    P.tt(t2[:], pb[:, :], sgl[gi][:, 1, :], ALU.mult, rd=[rpb, r_sgl[gi]], wr=[r_t])
                    P.tt(mg[:, m, :], t1[:], t2[:], ALU.add, rd=[r_t], wr=[r_mg])
                for j in range(4):
                    js = slice(j * 128, (j + 1) * 128)
                    t = b * 4 + j
                    xi = t % 2
                    if t + 1 < NBLK * 4:
                        load_x(t + 1)
                    p0, rp0 = banks[4 + 2 * xi], rb[4 + 2 * xi]
                    p1, rp1 = banks[5 + 2 * xi], rb[5 + 2 * xi]
                    P.mm(p0[:, :], [(mg[:, k, js], wo[:, k, 0:512]) for k in range(8)], rd=[r_mg, r_wo], wr=[rp0])
                    P.mm(p1[:, :], [(mg[:, k, js], wo[:, k, 512:1024]) for k in range(8)], rd=[r_mg, r_wo], wr=[rp1])
                    P.tt(xo[xi][:, 0:512], p0[:, :], g1bc[:, 0:512], ALU.mult, rd=[rp0, r_gbc], wr=[r_xo[xi]])
                    P.tt(xo[xi][:, 512:1024], p1[:, :], g1bc[:, 512:1024], ALU.mult, rd=[rp1, r_gbc], wr=[r_xo[xi]])
                    P.tt(xo[xi][:], xo[xi][:], xr[xi][:], ALU.add, rd=[r_xo[xi], r_xr[xi]], wr=[r_xo[xi]], eng="pool")
                    P.dma(out[T0 + j * 128:T0 + (j + 1) * 128, :], xo[xi][:], rd=[r_xo[xi]])
            P.barrier()
            P.emit()

        es_w2b.close()
        if phases < 4:
            es_w3.close()
            return nc
        with ExitStack() as es:
            g2bc = sbuf(es, "g2bc", [128, D])
            P.dma(g2bc[:], g2src.partition_broadcast(128), wr=[r_gbc])
            wdn = sbuf(es, "wdn", [128, 22, D], BF16); r_wdn = R()
            for g in range(2):
                P.dma(wdn[:, g * 11:(g + 1) * 11, :], w_dn[g * 1408:(g + 1) * 1408, :].rearrange("(k p) n -> p k n", p=128),
                      wr=[r_wdn], eng="pool")
            xring = [sbuf(es, "x3r%d" % i, [128, D]) for i in range(2)]; r_x = [R(), R()]
            xnr = [sbuf(es, "x3n%d" % i, [128, D], BF16) for i in range(2)]; r_xn = [R(), R()]
            junkb = sbuf(es, "junk3", [128, D], BF16); r_jb = R()
            st4 = sbuf(es, "st43", [128, 16]); r_st4 = R()
            h2T = [sbuf(es, "h2T%d" % i, [128, 8, 512], BF16) for i in range(2)]; r_h2T = [R(), R()]
            actT = sbuf(es, "actT", [128, 22, 512], BF16); r_actT = R()
            uext = [sbuf(es, "uext%d" % i, [128, 514]) for i in range(2)]; r_ue = [R(), R()]
            yv = [sbuf(es, "yv%d" % i, [128, 512]) for i in range(2)]; r_yv = [R(), R()]
            gsil = sbuf(es, "gsil", [128, 512]); r_gsil = R()
            halo = sbuf(es, "halo", [128, 44, 2]); r_halo = [R() for _ in range(44)]
            r_uh = [R(), R()]
            _xo = sbuf(es, "x3o", [128, D]); _rxo = R()
            xo = [_xo, _xo]; r_xo = [_rxo, _rxo]
            banks = [es.enter_context(nc.psum_tensor("p4_b%d" % i, [128, 512], F32)) for i in range(8)]
            rb = [R() for _ in range(8)]
            P.memset(halo[:], 0.0, wr=r_halo)
            r_out = [[R() for _ in range(4)] for _ in range(NBLK)]
            uc = 0
            fc = 0
            xc = 0
            def stage1(b):
                T0 = b * 512
                hT_ = h2T[b % 2]
                for j in range(4):
                    xt, rxt = xring[j % 2], r_x[j % 2]
                    xn_, rxn = xnr[j % 2], r_xn[j % 2]
                    P.dma(xt[:], out[T0 + j * 128:T0 + (j + 1) * 128, :], rd=[r_out[b][j]], wr=[rxt])
                    P.act(junkb[:], xt[:], AF.Square, rd=[rxt], wr=[r_jb, r_st4], accum_out=st4[:, j:j + 1])
                    P.act(st4[:, 4 + j:5 + j], st4[:, j:j + 1], AF.Sqrt, rd=[r_st4, r_eps], wr=[r_st4],
                          scale=1.0 / D, bias=epsb[:, 0:1])
                    P.recip(st4[:, 8 + j:9 + j], st4[:, 4 + j:5 + j], rd=[r_st4], wr=[r_st4])
                    P.ts(xn_[:], xt[:], st4[:, 8 + j:9 + j], None, ALU.mult, rd=[rxt, r_st4], wr=[rxn])
                    for k in range(8):
                        bv = banks[4 + k // 2][:].bitcast(BF16)
                        c0 = (k % 2) * 512 + j * 128
                        P.tr(bv[:, c0:c0 + 128], xn_[:, k * 128:(k + 1) * 128], identb, rd=[rxn, r_cstb], wr=[rb[4 + k // 2]])
                for k in range(8):
                    bv = banks[4 + k // 2][:].bitcast(BF16)
                    P.act(hT_[:, k, :], bv[:, (k % 2) * 512:(k % 2) * 512 + 512], AF.Identity,
                          rd=[rb[4 + k // 2], r_gs, r_mod], wr=[r_h2T[b % 2]], scale=gs2[:, k:k + 1], bias=modT[:, 24 + k:25 + k])
            stage1(0)
            print("P3 sbuf bytes remaining:", nc.sbuf_bytes_remaining, flush=True)
            for b in range(NBLK):
                T0 = b * 512
                h2c, r_h2c = h2T[b % 2], r_h2T[b % 2]

                def upchunk(c):
                    nonlocal uc, fc
                    pf, rpf = banks[fc % 4], rb[fc % 4]
                    fc += 1
                    ui = uc % 2
                    uc += 1
                    ue, rue = uext[ui], r_ue[ui]
                    y, ry = yv[ui], r_yv[ui]
                    P.mm(pf[:, :], [(wup[:, k, c * 128:(c + 1) * 128], h2c[:, k, :]) for k in range(8)], rd=[r_wup, r_h2c], wr=[rpf])
                    ruh = r_uh[ui]
                    P.cp(ue[:, 0:2], halo[:, c, :], rd=[r_halo[c]], wr=[ruh])
                    P.act(ue[:, 2:514], pf[:, :], AF.Copy, rd=[rpf], wr=[rue])
                    P.act(y[:], pf[:, :], AF.Identity, rd=[rpf, r_prm], wr=[ry],
                          scale=prm[:, PC_CW + 2 * 44 + c:PC_CW + 2 * 44 + c + 1], bias=prm[:, PC_CB + c:PC_CB + c + 1])
                    P.stt(y[:], ue[:, 1:513], prm[:, PC_CW + 44 + c:PC_CW + 44 + c + 1], y[:], ALU.mult, ALU.add, rd=[rue, ruh, r_prm, ry], wr=[ry])
                    P.stt(y[:], ue[:, 0:512], prm[:, PC_CW + c:PC_CW + c + 1], y[:], ALU.mult, ALU.add, rd=[rue, ruh, r_prm, ry], wr=[ry])
                    P.cp(halo[:, c, :], ue[:, 512:514], rd=[rue], wr=[r_halo[c]])
                    return y, ry
                for c in range(22):
                    y, ry = upchunk(c)
                    y2, ry2 = upchunk(22 + c)
                    P.act(gsil[:], y[:], AF.Silu, rd=[ry], wr=[r_gsil])
                    P.tt(actT[:, c, :], gsil[:], y2[:], ALU.mult, rd=[r_gsil, ry2], wr=[r_actT])
                    if c == 15 and b + 1 < NBLK:
                        stage1(b + 1)
                for j in range(4):
                    js = slice(j * 128, (j + 1) * 128)
                    xi = xc % 2
                    xc += 1
                    xt, rxt = xring[xi], r_x[xi]
                    P.dma(xt[:], out[T0 + j * 128:T0 + (j + 1) * 128, :], rd=[r_out[b][j]], wr=[rxt])
                    p0, rp0 = banks[4 + 2 * xi], rb[4 + 2 * xi]
                    p1, rp1 = banks[5 + 2 * xi], rb[5 + 2 * xi]
                    P.mm(p0[:, :], [(actT[:, k, js], wdn[:, k, 0:512]) for k in range(22)], rd=[r_actT, r_wdn], wr=[rp0])
                    P.mm(p1[:, :], [(actT[:, k, js], wdn[:, k, 512:1024]) for k in range(22)], rd=[r_actT, r_wdn], wr=[rp1])
                    P.tt(xo[xi][:, 0:512], p0[:, :], g2bc[:, 0:512], ALU.mult, rd=[rp0, r_gbc], wr=[r_xo[xi]])
                    P.tt(xo[xi][:, 512:1024], p1[:, :], g2bc[:, 512:1024], ALU.mult, rd=[rp1, r_gbc], wr=[r_xo[xi]])
                    P.tt(xo[xi][:], xo[xi][:], xt[:], ALU.add, rd=[r_xo[xi], rxt], wr=[r_xo[xi]])
                    P.dma(out[T0 + j * 128:T0 + (j + 1) * 128, :], xo[xi][:], rd=[r_xo[xi]], wr=[r_out[b][j]])
            P.barrier()
            P.emit()
        es_w3.close()
    return nc


def _host_layout(inputs):
    f32 = np.float32
    g = {k: np.asarray(v) for k, v in inputs.items()}

    def fm(v):
        v = np.asarray(v, f32).reshape(-1, 128)
        return np.ascontiguousarray(v.T)
    shared = np.zeros((128, NP), f32)
    shared[:, PC_BADA:PC_BADA + 48] = fm(g["b_ada"][0])
    shared[:, PC_G1:PC_G1 + 8] = fm(g["norm1_g"][0])
    shared[:, PC_G2:PC_G2 + 8] = fm(g["norm2_g"][0])
    shared[:, PC_GQA:PC_GQA + 4] = fm(g["q_a_norm_g"][0])
    shared[:, PC_GKVA:PC_GKVA + 2] = fm(g["kv_a_norm_g"][0])
    shared[:, PC_LB0:PC_LB0 + 4] = fm(g["hg_lower_bound"][0])
    shared[:, PC_LB1:PC_LB1 + 4] = fm(g["hg_lower_bound"][1])
    shared[:, PC_GOUT] = np.asarray(g["hg_out_norm_g"][0], f32)
    shared[:, PC_CB:PC_CB + 44] = fm(g["conv_b"][0])
    for jj in range(3):
        shared[:, PC_CW + jj * 44:PC_CW + (jj + 1) * 44] = fm(g["conv_w"][0][jj])
    bc = np.zeros((128, NB), f32)
    bc[:, BC_GQ:BC_GQ + 768] = np.tile(np.asarray(g["q_norm_g"][0], f32), 8)[None, :]
    bc[:, BC_GKN:BC_GKN + 512] = np.tile(np.asarray(g["k_norm_g"][0], f32)[:64], 8)[None, :]
    bc[:, BC_GKR:BC_GKR + 32] = np.asarray(g["k_norm_g"][0], f32)[64:96][None, :]
    inv_freq = (np.float32(10000.0) ** (-np.arange(0, 32, 2, dtype=np.float32) / np.float32(32))).astype(f32)
    bc[:, BC_IF:BC_IF + 16] = inv_freq[None, :]
    p = np.arange(128)
    cst = np.zeros((128, 384), f32)
    cst[:, 0:128] = np.eye(128, dtype=f32)
    cst[:, 128:256] = (p[:, None] <= p[None, :]).astype(f32)
    cst[:, 256:384] = ((p[:, None] <= p[None, :]) & ((p[:, None] // 64) == (p[None, :] // 64))).astype(f32)
    common = {
        "bcp": bc, "cst": cst,
        "w_ada": np.ascontiguousarray(g["w_ada"][0], f32), "w_in": np.ascontiguousarray(g["w_in"][0], f32),
        "w_uq": np.ascontiguousarray(g["w_uq"][0], f32), "w_ukv": np.ascontiguousarray(g["w_ukv"][0], f32),
        "w_ba": np.ascontiguousarray(g["w_branch_a"][0], f32), "w_bb": np.ascontiguousarray(g["w_branch_b"][0], f32),
        "w_out": np.ascontiguousarray(g["w_out"][0], f32), "w_up": np.ascontiguousarray(g["w_up"][0], f32),
        "w_dn": np.ascontiguousarray(g["w_down"][0], f32),
    }
    in_maps = []
    for c in range(8):
        prm = shared.copy()
        prm[:, PC_C:PC_C + 8] = fm(g["c"][c])
        m = dict(common)
        m["x"] = np.ascontiguousarray(g["x"][c], f32)
        m["prm"] = prm
        m["posT"] = np.ascontiguousarray(np.asarray(g["positions"][c], np.int32).reshape(32, 128).T)
        in_maps.append(m)
    return in_maps


def kernel(**inputs):
    in_maps = _host_layout(inputs)
    nc = build()
    res = run_bass_kernel_spmd(nc, in_maps, core_ids=list(range(8)))
    return np.stack([np.asarray(r["out"], np.float32) for r in res.results], axis=0)
```

```python
import numpy as np
import concourse.bass as bass
import concourse.mybir as mybir
from concourse.bass_utils import run_bass_kernel_spmd
from contextlib import ExitStack

F32 = mybir.dt.float32
BF16 = mybir.dt.bfloat16
I32 = mybir.dt.int32
AF = mybir.ActivationFunctionType
ALU = mybir.AluOpType
AX = mybir.AxisListType

S = 4096
D = 1024
NBLK = 8
EPS = 1e-6
DFF = 2816
TWO_PI = 6.283185307179586
PI = 3.141592653589793

NDS = 12
ENGS = ("pe", "act", "dve", "pool", "sp")


class R:
    __slots__ = ("w", "r", "name")

    def __init__(self, name=""):
        self.w = None
        self.r = []
        self.name = name


class Prog:
    def __init__(self, nc, es):
        self.nc = nc
        self.sem = {e: es.enter_context(nc.semaphore("s_" + e)) for e in ENGS if e != "sp"}
        self.cnt = {e: 0 for e in ENGS}
        self.seen = {e: {} for e in ENGS}
        self.q = {e: [] for e in ENGS}
        self.dsem = {e: [es.enter_context(nc.semaphore("d_%s%d" % (e, i))) for i in range(NDS)]
                     for e in ("sp", "pool")}
        self.dcnt = {e: [0] * NDS for e in ("sp", "pool")}
        self.dnext = {e: 0 for e in ("sp", "pool")}
        self.opn = {}

    def _semh(self, key):
        if key[0] == "c":
            return self.sem[key[1]]
        return self.dsem[key[1]][key[2]]

    def op(self, eng, fn, rd=(), wr=(), dma=False, n=1):
        deps = []
        raw = set()
        for t in rd:
            if t.w is not None:
                deps.append(t.w)
                raw.add(t.w)
        for t in wr:
            if t.w is not None:
                deps.append(t.w)
            deps.extend(t.r)
        waits = {}
        if dma:
            slot = self.dnext[eng]
            self.dnext[eng] = (slot + 1) % NDS
            key = ("d", eng, slot)
            prev = self.dcnt[eng][slot]
            if prev > 0:
                deps.append((key, prev, eng))
            self.dcnt[eng][slot] = prev + 16
            me = (key, prev + 16, eng)
            sig = (self.dsem[eng][slot], 16)
        else:
            self.cnt[eng] += 1
            key = ("c", eng)
            me = (key, self.cnt[eng], eng)
            self.opn[me] = n
            sig = (self.sem[eng], 1)
        seen = self.seen[eng]
        for (k, v, e) in deps:
            if k[0] == "c" and e == eng:
                if eng == "pe" or (k, v, e) not in raw:
                    continue
                if n >= 256 and self.opn.get((k, v, e), 1) >= 256:
                    continue
            if seen.get(k, 0) >= v:
                continue
            if waits.get(k, 0) < v:
                waits[k] = v
        for k, v in waits.items():
            seen[k] = v
        for t in rd:
            if len(t.r) > 64:
                t.r = t.r[-32:] if t.w is None or True else t.r
            t.r.append(me)
        for t in wr:
            t.w = me
            t.r = []
        self.q[eng].append(([(self._semh(k), v) for k, v in waits.items()], fn, sig))

    def barrier(self):
        for e in ENGS:
            waits = []
            for e2 in ENGS:
                if e2 != "sp" and e2 != e and self.cnt[e2] > 0 and self.seen[e].get(("c", e2), 0) < self.cnt[e2]:
                    waits.append((self.sem[e2], self.cnt[e2]))
                    self.seen[e][("c", e2)] = self.cnt[e2]
            for qe in ("sp", "pool"):
                for i in range(NDS):
                    v = self.dcnt[qe][i]
                    if v > 0 and self.seen[e].get(("d", qe, i), 0) < v:
                        waits.append((self.dsem[qe][i], v))
                        self.seen[e][("d", qe, i)] = v
            self.q[e].append((waits, None, None))

    def emit(self):
        nc = self.nc
        q = self.q
        print("emit ops:", {e: (len(v), sum(len(w) for w, _, _ in v)) for e, v in q.items()}, flush=True)

        def run(e, items):
            for waits, fn, sig in items:
                for (sh, v) in waits:
                    e.wait_ge(sh, v)
                if fn is not None:
                    ins = fn(e)
                    ins.then_inc(sig[0], sig[1])

        with nc.allow_low_precision("bf16 staging of fp32-computed values is intended"), nc.Block() as block:
            @block.tensor
            def _(e):
                run(e, q["pe"])

            @block.scalar
            def _(e):
                run(e, q["act"])

            @block.vector
            def _(e):
                run(e, q["dve"])

            @block.gpsimd
            def _(e):
                run(e, q["pool"])

            @block.sync
            def _(e):
                run(e, q["sp"])
        self.q = {e: [] for e in ENGS}

    def dma(self, out, in_, rd=(), wr=(), eng="sp", **kw):
        self.op(eng, lambda e: e.dma_start(out=out, in_=in_, **kw), rd, wr, dma=True)

    def mm(self, out, pairs, rd=(), wr=(), start=True, stop=True):
        n = len(pairs)

        def fn(e):
            ins = None
            for i, (l, r) in enumerate(pairs):
                ins = e.matmul(out, l, r, start=(start and i == 0), stop=(stop and i == n - 1))
            return ins
        self.op("pe", fn, rd, wr)

    def tr(self, out, in_, ident, rd=(), wr=()):
        self.op("pe", lambda e: e.transpose(out, in_, ident), rd, wr)

    @staticmethod
    def _n(ap):
        n = 1
        for d in ap.shape[1:]:
            n *= int(d)
        return n

    def act(self, out, in_, func, rd=(), wr=(), **kw):
        n = 1 if "accum_out" in kw else self._n(out)
        self.op("act", lambda e: e.activation(out=out, in_=in_, func=func, **kw), rd, wr, n=n)

    def ts(self, out, in0, s1, s2, op0, op1=None, rd=(), wr=(), eng="dve"):
        if op1 is None:
            self.op(eng, lambda e: e.tensor_scalar(out=out, in0=in0, scalar1=s1, scalar2=None, op0=op0), rd, wr, n=self._n(out))
        else:
            self.op(eng, lambda e: e.tensor_scalar(out=out, in0=in0, scalar1=s1, scalar2=s2, op0=op0, op1=op1), rd, wr, n=self._n(out))

    def tt(self, out, in0, in1, op, rd=(), wr=(), eng="dve"):
        self.op(eng, lambda e: e.tensor_tensor(out=out, in0=in0, in1=in1, op=op), rd, wr, n=self._n(out))

    def stt(self, out, in0, scalar, in1, op0, op1, rd=(), wr=(), eng="dve"):
        self.op(eng, lambda e: e.scalar_tensor_tensor(out=out, in0=in0, scalar=scalar, in1=in1, op0=op0, op1=op1), rd, wr, n=self._n(out))

    def cp(self, out, in_, rd=(), wr=(), eng="dve"):
        self.op(eng, lambda e: e.tensor_copy(out=out, in_=in_), rd, wr, n=self._n(out))

    def red(self, out, in_, rd=(), wr=(), eng="dve"):
        self.op(eng, lambda e: e.tensor_reduce(out=out, in_=in_, axis=AX.X, op=ALU.add), rd, wr)

    def recip(self, out, in_, rd=(), wr=()):
        self.op("dve", lambda e: e.reciprocal(out=out, in_=in_), rd, wr, n=self._n(out))

    def memset(self, ap, val, wr=(), eng="pool"):
        self.op(eng, lambda e: e.memset(ap, val), (), wr)


PC_BADA, PC_G1, PC_G2, PC_GQA, PC_GKVA, PC_LB0, PC_LB1, PC_GOUT, PC_C, PC_CB, PC_CW = 0, 48, 56, 64, 68, 70, 74, 78, 79, 87, 131
NP = 131 + 132
BC_GQ, BC_GKN, BC_GKR, BC_IF = 0, 768, 1280, 1312
NB = 1328
C_CQ, C_CKV, C_KR, C_HQ, C_HF, C_HI, C_HG, C_GA, C_GB = 0, 512, 768, 800, 1312, 1824, 2336, 2848, 3872


def build(debug=False, phases=99):
    nc = bass.Bass("TRN2", target_bir_lowering=False)

    def din(name, shape, dt=F32):
        return nc.dram_tensor(name, shape, dt, kind="ExternalInput").ap()

    x = din("x", [S, D])
    prm_d = din("prm", [128, NP])
    bcp_d = din("bcp", [128, NB])
    pos_d = din("posT", [128, 32], I32)
    cst_d = din("cst", [128, 3 * 128])
    w_ada = din("w_ada", [D, 6 * D])
    w_in = din("w_in", [D, 4896])
    w_uq = din("w_uq", [512, 768])
    w_ukv = din("w_ukv", [256, 1024])
    w_ba = din("w_ba", [512, D])
    w_bb = din("w_bb", [512, D])
    w_out = din("w_out", [D, D])
    w_up = din("w_up", [D, 2 * DFF])
    w_dn = din("w_dn", [DFF, D])
    out = nc.dram_tensor("out", [S, D], F32, kind="ExternalOutput").ap()

    def scr(name, shape, dt):
        kind = "ExternalOutput" if debug else "Internal"
        return nc.dram_tensor(name, shape, dt, kind=kind).ap()

    modscr = scr("modscr", [48, 128], F32)
    qT_scr = scr("qT_scr", [96, 8, S], BF16)
    kT_scr = scr("kT_scr", [96, 8, S], BF16)
    V_scr = scr("V_scr", [S, 512], BF16)
    hgo_scr = scr("hgo_scr", [512, S], BF16)
    sga_scr = scr("sga_scr", [D, S], BF16)
    sgb_scr = scr("sgb_scr", [D, S], BF16)
    ON_scr = scr("ON_scr", [64, 8, S], BF16)

    with ExitStack() as es0:
        P = Prog(nc, es0)

        def sbuf(es, name, shape, dt=F32):
            return es.enter_context(nc.sbuf_tensor("sb_" + name, shape, dt))

        prm = sbuf(es0, "prm", [128, NP]); r_prm = R()
        cstf = sbuf(es0, "cstf", [128, 384]); r_cstf = R()
        cstb = sbuf(es0, "cstb", [128, 384], BF16); r_cstb = R()
        modT = sbuf(es0, "modT", [128, 48]); r_mod = R()
        gs1 = sbuf(es0, "gs1", [128, 8]); gs2 = sbuf(es0, "gs2", [128, 8]); r_gs = R()
        r_gbc = R()
        lb = sbuf(es0, "lb", [128, 4]); oml = sbuf(es0, "oml", [128, 4]); r_lb = R()
        epsb = sbuf(es0, "epsb", [128, 1]); r_eps = R()
        ones_b = sbuf(es0, "ones_b", [128, 128], BF16); ones_f = sbuf(es0, "ones_f", [128, 64]); r_ones = R()
        es01 = ExitStack()
        bcp = sbuf(es01, "bcp", [128, NB]); r_bcp = R()
        cos_t = sbuf(es01, "cos_t", [128, 32, 16]); sin_t = sbuf(es01, "sin_t", [128, 32, 16]); r_cs = R()
        identb = cstb[:, 0:128]
        tri_f = cstf[:, 128:256]
        bdtri_f = cstf[:, 256:384]
        tri_b = cstb[:, 128:256]
        identf = cstf[:, 0:128]

        P.dma(prm[:], prm_d[:, :], wr=[r_prm])
        P.dma(bcp[:], bcp_d[:, :], wr=[r_bcp])
        P.dma(cstf[:], cst_d[:, :], wr=[r_cstf])
        P.dma(cstb[:], cst_d[:, :], wr=[r_cstb], eng="pool")
        P.memset(epsb[:], EPS, wr=[r_eps])
        P.memset(ones_b[:], 1.0, wr=[r_ones])
        P.memset(ones_f[:], 1.0, wr=[r_ones])

        es_w1 = ExitStack()
        win = sbuf(es_w1, "win", [128, 8, 4896], BF16); r_win = R()
        wuq = sbuf(es_w1, "wuq", [128, 4, 768], BF16); r_wuq = R()
        wukv = sbuf(es_w1, "wukv", [128, 2, 1024], BF16); r_wukv = R()
        for k in range(8):
            P.dma(win[:, k, :], w_in[k * 128:(k + 1) * 128, :], wr=[r_win], eng="pool")
        P.dma(wuq[:], w_uq.rearrange("(k p) n -> p k n", p=128), wr=[r_wuq], eng="pool")
        P.dma(wukv[:], w_ukv.rearrange("(k p) n -> p k n", p=128), wr=[r_wukv], eng="pool")
        with ExitStack() as es:
            wa_ring = [sbuf(es, "wada%d" % i, [128, 6 * D]) for i in range(2)]
            r_wa = [R(), R()]
            pm = es.enter_context(nc.psum_tensor("p0_pm", [128, 512], F32)); r_pm = R()
            pt = es.enter_context(nc.psum_tensor("p0_pt", [128, 512], F32)); r_pt = R()
            for k in range(8):
                wt = wa_ring[k % 2]
                P.dma(wt[:], w_ada[k * 128:(k + 1) * 128, :], wr=[r_wa[k % 2]])

                def f0(e, wt=wt, k=k):
                    ins = None
                    for j in range(48):
                        ins = e.matmul(pm[:, k * 48 + j:k * 48 + j + 1], wt[:, j * 128:(j + 1) * 128],
                                       prm[:, PC_C + k:PC_C + k + 1], start=True, stop=True)
                    return ins
                P.op("pe", f0, [r_wa[k % 2], r_prm], [r_pm])
            P.red(modT[:], pm[:, 0:384].rearrange("p (k j) -> p j k", k=8), rd=[r_pm], wr=[r_mod])
            P.tt(modT[:], modT[:], prm[:, PC_BADA:PC_BADA + 48], ALU.add, rd=[r_mod, r_prm], wr=[r_mod])
            P.stt(gs1[:], modT[:, 8:16], 1.0, prm[:, PC_G1:PC_G1 + 8], ALU.add, ALU.mult, rd=[r_mod, r_prm], wr=[r_gs])
            P.stt(gs2[:], modT[:, 32:40], 1.0, prm[:, PC_G2:PC_G2 + 8], ALU.add, ALU.mult, rd=[r_mod, r_prm], wr=[r_gs])
            P.tr(pt[0:48, 0:128], modT[:, 0:48], identf, rd=[r_mod, r_cstf], wr=[r_pt])
            modrow = sbuf(es, "modrow", [48, 128]); r_mr = R()
            P.cp(modrow[:], pt[0:48, 0:128], rd=[r_pt], wr=[r_mr])
            r_ms = R()
            P.dma(modscr[:, :], modrow[:], rd=[r_mr], wr=[r_ms])
            g1src = modscr[16:24, :].rearrange("(o a) b -> o (a b)", o=1)
            g2src = modscr[40:48, :].rearrange("(o a) b -> o (a b)", o=1)
            posi = sbuf(es, "posi", [128, 32], I32); r_pi = R()
            posf = sbuf(es, "posf", [128, 32]); r_pf = R()
            P.dma(posi[:], pos_d[:, :], wr=[r_pi])
            P.cp(posf[:], posi[:], rd=[r_pi], wr=[r_pf])
            ang = sbuf(es, "ang", [128, 32, 16]); r_ang = R()
            ang2 = sbuf(es, "ang2", [128, 32, 16]); r_ang2 = R()
            kf = sbuf(es, "kf", [128, 32, 16]); r_kf = R()
            ki = sbuf(es, "ki", [128, 32, 16], I32); r_ki = R()
            P.tt(ang[:], posf[:].unsqueeze(2).to_broadcast([128, 32, 16]),
                 bcp[:, BC_IF:BC_IF + 16].unsqueeze(1).to_broadcast([128, 32, 16]), ALU.mult,
                 rd=[r_pf, r_bcp], wr=[r_ang])

            def range_reduce(a, ra):
                P.ts(kf[:], a[:], 1.0 / TWO_PI, None, ALU.mult, rd=[ra], wr=[r_kf])
                P.cp(ki[:], kf[:], rd=[r_kf], wr=[r_ki])
                P.cp(kf[:], ki[:], rd=[r_ki], wr=[r_kf])
                P.stt(a[:], kf[:], -TWO_PI, a[:], ALU.mult, ALU.add, rd=[r_kf, ra], wr=[ra])
                P.ts(kf[:], a[:], PI, None, ALU.is_gt, rd=[ra], wr=[r_kf])
                P.stt(a[:], kf[:], -TWO_PI, a[:], ALU.mult, ALU.add, rd=[r_kf, ra], wr=[ra])
                P.ts(kf[:], a[:], -PI, None, ALU.is_lt, rd=[ra], wr=[r_kf])
                P.stt(a[:], kf[:], TWO_PI, a[:], ALU.mult, ALU.add, rd=[r_kf, ra], wr=[ra])
                P.ts(a[:], a[:], PI, -PI, ALU.min, ALU.max, rd=[ra], wr=[ra])
            P.ts(ang2[:], ang[:], PI / 2, None, ALU.add, rd=[r_ang], wr=[r_ang2])
            range_reduce(ang, r_ang)
            range_reduce(ang2, r_ang2)
            P.act(sin_t[:], ang[:], AF.Sin, rd=[r_ang], wr=[r_cs])
            P.act(cos_t[:], ang2[:], AF.Sin, rd=[r_ang2], wr=[r_cs])
            P.tt(lb[:], prm[:, PC_LB1:PC_LB1 + 4], prm[:, PC_LB0:PC_LB0 + 4], ALU.subtract, rd=[r_prm], wr=[r_lb])
            P.act(lb[:], lb[:], AF.Sigmoid, rd=[r_lb], wr=[r_lb])
            P.ts(oml[:], lb[:], -1.0, 1.0, ALU.mult, ALU.add, rd=[r_lb], wr=[r_lb])
            P.ts(bcp[:, BC_GQ:BC_GQ + 768], bcp[:, BC_GQ:BC_GQ + 768], 1.0 / float(np.sqrt(96.0)), None, ALU.mult,
                 rd=[r_bcp], wr=[r_bcp])
            P.barrier()
            P.emit()

        if phases < 1:
            es_w1.close()
            es01.close()
            return nc
        with ExitStack() as es:
            banks = [es.enter_context(nc.psum_tensor("p1_b%d" % i, [128, 512], F32)) for i in range(8)]
            rb = [R() for _ in range(8)]
            xring = [sbuf(es, "xr%d" % i, [128, D]) for i in range(2)]; r_x = [R(), R()]
            xnr = [sbuf(es, "xn%d" % i, [128, D], BF16) for i in range(2)]; r_xn = [R(), R()]
            junkb = sbuf(es, "junkb", [128, D], BF16); r_jb = R()
            junkf = sbuf(es, "junkf", [128, D]); r_jf = R()
            st4 = sbuf(es, "st4", [128, 16]); r_st4 = R()
            hT = sbuf(es, "hT", [128, 8, 512], BF16); r_hT = R()
            cqT = sbuf(es, "cqT", [128, 4, 512], BF16); r_cqT = R()
            ckvT = sbuf(es, "ckvT", [128, 2, 512], BF16); r_ckvT = R()
            sqb = sbuf(es, "sqb", [128, 6, 512], BF16); r_sqb = R()
            rq = sbuf(es, "rq", [128, 16]); r_rq = R()
            silq = sbuf(es, "silq", [128, 512]); r_silq = R()
            sgf = sbuf(es, "sgf", [128, 512]); r_sgf = R()
            silg = sbuf(es, "silg", [128, 512], BF16); r_silg = R()
            gst = [sbuf(es, "gst%d" % i, [128, 512], BF16) for i in range(2)]; r_gst = [R() for _ in range(2)]
            vtok = sbuf(es, "vtok", [128, 4, 512], BF16); r_vtok = [R() for _ in range(4)]
            qst = [sbuf(es, "qst%d" % i, [128, 8, 128], BF16) for i in range(2)]; r_qst = [R(), R()]
            kst = [sbuf(es, "kst%d" % i, [128, 8, 128], BF16) for i in range(2)]; r_kst = [R(), R()]
            vst = [sbuf(es, "vst%d" % i, [128, 512], BF16) for i in range(2)]; r_vst = [R(), R()]
            qn = sbuf(es, "qn", [128, 768]); r_qn = R()
            qf = sbuf(es, "qf", [128, 768], BF16); r_qf = R()
            kfin = sbuf(es, "kfin", [128, 768], BF16); r_kfin = R()
            sm = sbuf(es, "sm", [128, 64]); r_sm = R()
            rp = sbuf(es, "rp", [128, 8, 64]); r_rp = R()
            ff = sbuf(es, "ff", [128, 512]); r_ff = R()
            lf = sbuf(es, "lf", [128, 512]); r_lf = R()
            kk = sbuf(es, "kk", [128, 512]); r_kk = R()
            bcum = sbuf(es, "bcum", [128, 512]); r_bc = R()
            E1 = sbuf(es, "E1", [128, 512]); E2 = sbuf(es, "E2", [128, 512]); r_E = R()
            nbm = sbuf(es, "nbm", [128, 8]); emid = sbuf(es, "emid", [128, 8]); elast = sbuf(es, "elast", [128, 8]); r_eb = R()
            qtT = sbuf(es, "qtT", [128, 512], BF16); ktT = sbuf(es, "ktT", [128, 512], BF16); r_qk = R()
            kttok = sbuf(es, "kttok", [128, 4, 128], BF16); r_kttok = R()
            state = sbuf(es, "state", [128, 4, 128]); r_state = [R() for _ in range(4)]
            stb = [sbuf(es, "stb%d" % i, [128, 128], BF16) for i in range(2)]; r_stb = [R(), R()]
            Am = [sbuf(es, "Am%d" % i, [128, 128], BF16) for i in range(2)]; r_Am = [R(), R()]
            kvt = sbuf(es, "kvt", [128, 128]); r_kvt = R()
            osq = sbuf(es, "osq", [128, 512], BF16); r_osq = R()
            hgst = [sbuf(es, "hgst%d" % i, [128, 512], BF16) for i in range(2)]; r_hgst = [R(), R()]
            P.memset(state[:], 0.0, wr=r_state)

            pF = [banks[0], banks[1]]; r_pF = [rb[0], rb[1]]
            pO, r_pO = banks[2], rb[2]
            pS, r_pS = banks[3], rb[3]
            pKV = (banks[4], banks[5]); r_pKV = [rb[4], rb[5]]
            pQ = (banks[6], banks[7]); r_pQ = [rb[6], rb[7]]
            pfc = [0]

            def nextF():
                i = pfc[0] % 2
                pfc[0] += 1
                return pF[i], r_pF[i]
            gsc = [0]
            hgc = [0]
            stc = [0]

            def rstd(out_ap, in_ap, scale, rd, rw):
                P.act(out_ap, in_ap, AF.Ln, rd=list(rd) + [r_eps], wr=[rw], scale=scale, bias=epsb[:, 0:1])
                P.act(out_ap, out_ap, AF.Exp, rd=[rw], wr=[rw], scale=-0.5)

            def sig_parts(tmp, rtmp, in_ap, rd):
                P.act(tmp, in_ap, AF.Exp, rd=rd, wr=[rtmp], scale=-1.0)
                P.act(tmp, tmp, AF.Ln, rd=[rtmp, r_ones], wr=[rtmp], bias=ones_f[:, 0:1])
                P.act(tmp, tmp, AF.Exp, rd=[rtmp], wr=[rtmp], scale=-1.0)
            tmpS = sbuf(es, "tmpS", [128, 512]); r_tmpS = R()
            smask = sbuf(es, "smask", [128, 512]); r_smask = R()
            P.memset(smask[:], 1.0, wr=[r_smask])
            P.memset(smask[:].rearrange("p (c t) -> p c t", t=64)[:, :, 0:1], 0.0, wr=[r_smask])
            gtmp = [sbuf(es, "gtmp%d" % i, [128, 512]) for i in range(2)]; r_gtmp = [R(), R()]
            print("P1 sbuf bytes remaining:", nc.sbuf_bytes_remaining, flush=True)

            def run_interleaved(gens):
                gens = list(gens)
                while gens:
                    for g in list(gens):
                        try:
                            next(g)
                        except StopIteration:
                            gens.remove(g)

            for b in range(NBLK):
                T0 = b * 512
                for j in range(4):
                    xt, rxt = xring[j % 2], r_x[j % 2]
                    xn_, rxn = xnr[j % 2], r_xn[j % 2]
                    P.dma(xt[:], x[T0 + j * 128:T0 + (j + 1) * 128, :], wr=[rxt])
                    P.act(junkb[:], xt[:], AF.Square, rd=[rxt], wr=[r_jb, r_st4], accum_out=st4[:, j:j + 1])
                    rstd(st4[:, 8 + j:9 + j], st4[:, j:j + 1], 1.0 / D, [r_st4], r_st4)
                    P.ts(xn_[:], xt[:], st4[:, 8 + j:9 + j], None, ALU.mult, rd=[rxt, r_st4], wr=[rxn])
                    for k in range(8):
                        bv = banks[4 + k // 2][:].bitcast(BF16)
                        c0 = (k % 2) * 512 + j * 128
                        P.tr(bv[:, c0:c0 + 128], xn_[:, k * 128:(k + 1) * 128], identb,
                             rd=[rxn, r_cstb], wr=[rb[4 + k // 2]])
                for k in range(8):
                    bv = banks[4 + k // 2][:].bitcast(BF16)
                    P.act(hT[:, k, :], bv[:, (k % 2) * 512:(k % 2) * 512 + 512], AF.Identity,
                          rd=[rb[4 + k // 2], r_gs, r_mod], wr=[r_hT], scale=gs1[:, k:k + 1], bias=modT[:, k:k + 1])

                def fproj(c0):
                    pf, rpf = nextF()
                    P.mm(pf[:, :], [(win[:, k, c0:c0 + 128], hT[:, k, :]) for k in range(8)],
                         rd=[r_win, r_hT], wr=[rpf])
                    return pf, rpf
                for i in range(4):
                    pf, rpf = fproj(C_CQ + i * 128)
                    P.act(cqT[:, i, :], pf[:, :], AF.Copy, rd=[rpf, r_prm], wr=[r_cqT], scale=prm[:, PC_GQA + i:PC_GQA + i + 1])
                    P.act(sqb[:, i, :], pf[:, :], AF.Square, rd=[rpf], wr=[r_sqb])
                for i in range(2):
                    pf, rpf = fproj(C_CKV + i * 128)
                    P.act(ckvT[:, i, :], pf[:, :], AF.Copy, rd=[rpf, r_prm], wr=[r_ckvT], scale=prm[:, PC_GKVA + i:PC_GKVA + i + 1])
                    P.act(sqb[:, 4 + i, :], pf[:, :], AF.Square, rd=[rpf], wr=[r_sqb])
                for j in range(4):
                    P.mm(pS[:, j:j + 1], [(sqb[:, i, j * 128:(j + 1) * 128], ones_b[:, 0:1]) for i in range(4)],
                         rd=[r_sqb, r_ones], wr=[r_pS])
                for j in range(4):
                    P.mm(pS[:, 4 + j:5 + j], [(sqb[:, 4 + i, j * 128:(j + 1) * 128], ones_b[:, 0:1]) for i in range(2)],
                         rd=[r_sqb, r_ones], wr=[r_pS])
                rstd(rq[:, 0:4], pS[:, 0:4], 1.0 / 512, [r_pS], r_rq)
                rstd(rq[:, 4:8], pS[:, 4:8], 1.0 / 256, [r_pS], r_rq)
                P.tt(rq[:, 8:16], rq[:, 0:8], rq[:, 0:8], ALU.mult, rd=[r_rq], wr=[r_rq])

                for j in range(4):
                    pf, rpf = nextF()
                    P.mm(pf[:, :], [(hT[:, k, j * 128:(j + 1) * 128], win[:, k, C_HI:C_HI + 512]) for k in range(8)],
                         rd=[r_hT, r_win], wr=[rpf])
                    P.act(vtok[:, j, :], pf[:, :], AF.Copy, rd=[rpf], wr=[r_vtok[j]])
                def gen_B(b=b, T0=T0):
                    for j in range(4):
                        jt = b * 4 + j
                        js = slice(j * 128, (j + 1) * 128)
                        si_ = stc[0] % 2
                        stc[0] += 1
                        P.mm(pQ[0][:, :], [(cqT[:, i, js], wuq[:, i, 0:512]) for i in range(4)], rd=[r_cqT, r_wuq], wr=[r_pQ[0]])
                        P.mm(pQ[1][:, 0:256], [(cqT[:, i, js], wuq[:, i, 512:768]) for i in range(4)], rd=[r_cqT, r_wuq], wr=[r_pQ[1]])
                        P.mm(pQ[1][:, 256:288], [(hT[:, k, js], win[:, k, C_KR:C_KR + 32]) for k in range(8)], rd=[r_hT, r_win], wr=[r_pQ[1]])
                        P.mm(pKV[0][:, :], [(ckvT[:, i, js], wukv[:, i, 0:512]) for i in range(2)], rd=[r_ckvT, r_wukv], wr=[r_pKV[0]])
                        P.mm(pKV[1][:, :], [(ckvT[:, i, js], wukv[:, i, 512:1024]) for i in range(2)], rd=[r_ckvT, r_wukv], wr=[r_pKV[1]])
                        yield
                        cosb8 = cos_t[:, jt, :].unsqueeze(1).to_broadcast([128, 8, 16])
                        sinb8 = sin_t[:, jt, :].unsqueeze(1).to_broadcast([128, 8, 16])
                        P.act(junkf[:, 0:512], pQ[0][:, :], AF.Square, rd=[r_pQ[0]], wr=[r_jf])
                        P.act(junkf[:, 512:768], pQ[1][:, 0:256], AF.Square, rd=[r_pQ[1]], wr=[r_jf])
                        P.red(sm[:, 0:8], junkf[:, 0:768].rearrange("p (h d) -> p h d", d=96), rd=[r_jf], wr=[r_sm])
                        P.ts(sm[:, 0:8], sm[:, 0:8], rq[:, 8 + j:9 + j], 1.0 / 96, ALU.mult, ALU.mult, rd=[r_sm, r_rq], wr=[r_sm])
                        rstd(sm[:, 8:16], sm[:, 0:8], 1.0, [r_sm], r_sm)
                        P.ts(sm[:, 16:24], sm[:, 8:16], rq[:, j:j + 1], None, ALU.mult, rd=[r_sm, r_rq], wr=[r_sm])
                        yield
                        qn3 = qn[:].rearrange("p (h d) -> p h d", d=96)
                        P.tt(qn3[:, 0:5, :], pQ[0][:, 0:480].rearrange("p (h d) -> p h d", d=96),
                             sm[:, 16:21].unsqueeze(2).to_broadcast([128, 5, 96]), ALU.mult, rd=[r_pQ[0], r_sm], wr=[r_qn])
                        P.ts(qn[:, 480:512], pQ[0][:, 480:512], sm[:, 21:22], None, ALU.mult, rd=[r_pQ[0], r_sm], wr=[r_qn])
                        P.ts(qn[:, 512:576], pQ[1][:, 0:64], sm[:, 21:22], None, ALU.mult, rd=[r_pQ[1], r_sm], wr=[r_qn])
                        P.tt(qn3[:, 6:8, :], pQ[1][:, 64:256].rearrange("p (h d) -> p h d", d=96),
                             sm[:, 22:24].unsqueeze(2).to_broadcast([128, 2, 96]), ALU.mult, rd=[r_pQ[1], r_sm], wr=[r_qn])
                        P.tt(qn[:], qn[:], bcp[:, BC_GQ:BC_GQ + 768], ALU.mult, rd=[r_qn, r_bcp], wr=[r_qn], eng="pool")
                        yield
                        qf3 = qf[:].rearrange("p (h d) -> p h d", d=96)
                        rp3 = rp[:]
                        P.tt(rp3[:, :, 0:16], qn3[:, :, 64:80], cosb8, ALU.mult, rd=[r_qn, r_cs], wr=[r_rp], eng="pool")
                        P.tt(rp3[:, :, 16:32], qn3[:, :, 80:96], sinb8, ALU.mult, rd=[r_qn, r_cs], wr=[r_rp], eng="pool")
                        P.tt(qf3[:, :, 64:80], rp3[:, :, 0:16], rp3[:, :, 16:32], ALU.subtract, rd=[r_rp], wr=[r_qf], eng="pool")
                        P.tt(rp3[:, :, 32:48], qn3[:, :, 80:96], cosb8, ALU.mult, rd=[r_qn, r_cs], wr=[r_rp], eng="pool")
                        P.tt(rp3[:, :, 48:64], qn3[:, :, 64:80], sinb8, ALU.mult, rd=[r_qn, r_cs], wr=[r_rp], eng="pool")
                        P.tt(qf3[:, :, 80:96], rp3[:, :, 32:48], rp3[:, :, 48:64], ALU.add, rd=[r_rp], wr=[r_qf], eng="pool")
                        P.cp(qf3[:, :, 0:64], qn3[:, :, 0:64], rd=[r_qn], wr=[r_qf], eng="pool")
                        yield
                        kv0 = pKV[0][:, :].rearrange("p (h d) -> p h d", d=128)
                        kv1 = pKV[1][:, :].rearrange("p (h d) -> p h d", d=128)
                        vst3 = vst[si_][:].rearrange("p (h d) -> p h d", d=64)
                        P.ts(vst3[:, 0:4, :], kv0[:, :, 64:128], rq[:, 4 + j:5 + j], None, ALU.mult, rd=[r_pKV[0], r_rq], wr=[r_vst[si_]])
                        P.ts(vst3[:, 4:8, :], kv1[:, :, 64:128], rq[:, 4 + j:5 + j], None, ALU.mult, rd=[r_pKV[1], r_rq], wr=[r_vst[si_]])
                        P.act(junkf[:, 0:512], pKV[0][:, :], AF.Square, rd=[r_pKV[0]], wr=[r_jf])
                        P.act(junkf[:, 512:1024], pKV[1][:, :], AF.Square, rd=[r_pKV[1]], wr=[r_jf])
                        P.red(sm[:, 24:32], junkf[:].rearrange("p (h d) -> p h d", d=128)[:, :, 0:64], rd=[r_jf], wr=[r_sm])
                        P.act(junkb[:, 0:32], pQ[1][:, 256:288], AF.Square, rd=[r_pQ[1]], wr=[r_jb, r_sm], accum_out=sm[:, 32:33])
                        P.ts(sm[:, 24:32], sm[:, 24:32], rq[:, 12 + j:13 + j], sm[:, 32:33], ALU.mult, ALU.add, rd=[r_sm, r_rq], wr=[r_sm])
                        rstd(sm[:, 40:48], sm[:, 24:32], 1.0 / 96, [r_sm], r_sm)
                        P.ts(sm[:, 48:56], sm[:, 40:48], rq[:, 4 + j:5 + j], None, ALU.mult, rd=[r_sm, r_rq], wr=[r_sm])
                        yield
                        kf3 = kfin[:].rearrange("p (h d) -> p h d", d=96)
                        gkn3 = bcp[:, BC_GKN:BC_GKN + 512].rearrange("p (h d) -> p h d", d=64)
                        P.tt(rp3[:, 0:4, :], kv0[:, :, 0:64], sm[:, 48:52].unsqueeze(2).to_broadcast([128, 4, 64]), ALU.mult,
                             rd=[r_pKV[0], r_sm, r_rp], wr=[r_rp])
                        P.tt(rp3[:, 4:8, :], kv1[:, :, 0:64], sm[:, 52:56].unsqueeze(2).to_broadcast([128, 4, 64]), ALU.mult,
                             rd=[r_pKV[1], r_sm], wr=[r_rp])
                        P.tt(kf3[:, :, 0:64], rp3[:, :, :], gkn3, ALU.mult, rd=[r_rp, r_bcp], wr=[r_kfin], eng="pool")
                        P.tt(junkf[:, 0:32], pQ[1][:, 256:288], bcp[:, BC_GKR:BC_GKR + 32], ALU.mult, rd=[r_pQ[1], r_bcp, r_jf], wr=[r_jf])
                        c16 = cos_t[:, jt, :]
                        s16 = sin_t[:, jt, :]
                        P.tt(junkf[:, 32:48], junkf[:, 0:16], c16, ALU.mult, rd=[r_jf, r_cs], wr=[r_jf], eng="pool")
                        P.tt(junkf[:, 48:64], junkf[:, 16:32], s16, ALU.mult, rd=[r_jf, r_cs], wr=[r_jf], eng="pool")
                        P.tt(junkf[:, 96:112], junkf[:, 32:48], junkf[:, 48:64], ALU.subtract, rd=[r_jf], wr=[r_jf], eng="pool")
                        P.tt(junkf[:, 64:80], junkf[:, 16:32], c16, ALU.mult, rd=[r_jf, r_cs], wr=[r_jf], eng="pool")
                        P.tt(junkf[:, 80:96], junkf[:, 0:16], s16, ALU.mult, rd=[r_jf, r_cs], wr=[r_jf], eng="pool")
                        P.tt(junkf[:, 112:128], junkf[:, 64:80], junkf[:, 80:96], ALU.add, rd=[r_jf], wr=[r_jf], eng="pool")
                        P.tt(kf3[:, :, 64:96], junkf[:, 96:128].unsqueeze(1).to_broadcast([128, 8, 32]),
                             sm[:, 40:48].unsqueeze(2).to_broadcast([128, 8, 32]), ALU.mult, rd=[r_jf, r_sm], wr=[r_kfin], eng="pool")
                        yield
                        qbv = pQ[0][:].bitcast(BF16)
                        kbv = pKV[0][:].bitcast(BF16)
                        for h in range(8):
                            P.tr(qbv[0:96, h * 128:(h + 1) * 128], qf[:, h * 96:(h + 1) * 96], identb, rd=[r_qf, r_cstb], wr=[r_pQ[0]])
                        for h in range(8):
                            P.tr(kbv[0:96, h * 128:(h + 1) * 128], kfin[:, h * 96:(h + 1) * 96], identb, rd=[r_kfin, r_cstb], wr=[r_pKV[0]])
                        P.act(qst[si_][0:96, :, :], qbv[0:96, :].rearrange("p (h t) -> p h t", t=128), AF.Copy, rd=[r_pQ[0]], wr=[r_qst[si_]])
                        P.cp(kst[si_][0:96, :, :], kbv[0:96, :].rearrange("p (h t) -> p h t", t=128), rd=[r_pKV[0]], wr=[r_kst[si_]])
                        P.dma(qT_scr[:, :, T0 + j * 128:T0 + (j + 1) * 128], qst[si_][0:96, :, :], rd=[r_qst[si_]])
                        P.dma(kT_scr[:, :, T0 + j * 128:T0 + (j + 1) * 128], kst[si_][0:96, :, :], rd=[r_kst[si_]])
                        P.dma(V_scr[T0 + j * 128:T0 + (j + 1) * 128, :], vst[si_][:], rd=[r_vst[si_]])
                        yield

                def gen_C(b=b, T0=T0):
                    for h in range(4):
                        hc = slice(h * 128, (h + 1) * 128)
                        pf, rpf = fproj(C_HF + h * 128)
                        sig_parts(tmpS[:], r_tmpS, pf[:, :], [rpf])
                        P.ts(ff[:], tmpS[:], oml[:, h:h + 1], lb[:, h:h + 1], ALU.mult, ALU.add, rd=[r_tmpS, r_lb], wr=[r_ff])
                        yield
                        pf, rpf = fproj(C_HQ + h * 128)
                        sig_parts(tmpS[:], r_tmpS, pf[:, :], [rpf])
                        P.tt(silq[:], pf[:, :], tmpS[:], ALU.mult, rd=[rpf, r_tmpS], wr=[r_silq])
                        yield
                        pf, rpf = fproj(C_HG + h * 128)
                        sig_parts(tmpS[:], r_tmpS, pf[:, :], [rpf])
                        P.tt(silg[:], pf[:, :], tmpS[:], ALU.mult, rd=[rpf, r_tmpS], wr=[r_silg])
                        yield
                        P.act(lf[:], ff[:], AF.Ln, rd=[r_ff], wr=[r_lf])
                        P.ts(kk[:], ff[:], -1.0, 1.0, ALU.mult, ALU.add, rd=[r_ff], wr=[r_kk])
                        P.op("dve", lambda e: e.tensor_tensor_scan(out=bcum[:], data0=smask[:], data1=lf[:],
                                                                   initial=0.0, op0=ALU.mult, op1=ALU.add),
                             [r_lf, r_smask], [r_bc], n=512)
                        b3 = bcum[:].rearrange("p (c t) -> p c t", t=64)
                        P.tt(lf[:].rearrange("p (c t) -> p c t", t=64), b3, b3[:, :, 31:32].to_broadcast([128, 8, 64]), ALU.subtract,
                             rd=[r_bc], wr=[r_lf])
                        yield
                        P.act(E1[:], lf[:], AF.Exp, rd=[r_lf], wr=[r_E])
                        P.act(E2[:], lf[:], AF.Exp, rd=[r_lf], wr=[r_E], scale=-1.0)
                        yield
                        P.act(emid[:], b3[:, :, 31], AF.Exp, rd=[r_bc], wr=[r_eb])
                        P.act(elast[:], b3[:, :, 63], AF.Exp, rd=[r_bc], wr=[r_eb])
                        P.tt(qtT[:], silq[:], E1[:], ALU.mult, rd=[r_silq, r_E], wr=[r_qk])
                        P.tt(ktT[:], kk[:], E2[:], ALU.mult, rd=[r_kk, r_E], wr=[r_qk])
                        psb = pS[:].bitcast(BF16)
                        for j in range(4):
                            P.tr(psb[:, 256 + j * 128:256 + (j + 1) * 128], ktT[:, j * 128:(j + 1) * 128], identb, rd=[r_qk, r_cstb], wr=[r_pS])
                        P.cp(kttok[:], psb[:, 256:768].rearrange("p (j k) -> p j k", k=128), rd=[r_pS], wr=[r_kttok])
                        yield
                        E13 = E1[:].rearrange("p (c t) -> p c t", t=64)
                        for j in range(4):
                            js = slice(j * 128, (j + 1) * 128)
                            ai = j % 2
                            P.mm(pS[:, 0:128], [(ktT[:, js], qtT[:, js])], rd=[r_qk], wr=[r_pS])
                            P.tt(Am[ai][:], pS[:, 0:128], bdtri_f, ALU.mult, rd=[r_pS, r_cstf], wr=[r_Am[ai]])
                            for cc in range(2):
                                c = 2 * j + cc
                                cs = slice(c * 64, (c + 1) * 64)
                                r0 = cc * 64
                                si = c % 2
                                P.mm(pS[:, 384:512], [(kttok[r0:r0 + 64, j, :], vtok[r0:r0 + 64, j, hc])], rd=[r_kttok, r_vtok[j]], wr=[r_pS])
                                P.ts(stb[si][:], state[:, h, :], emid[:, c:c + 1], None, ALU.mult, rd=[r_state[h], r_eb], wr=[r_stb[si]])
                                P.mm(pO[:, cs], [(stb[si][:], qtT[:, cs]), (vtok[:, j, hc], Am[ai][:, r0:r0 + 64])],
                                     rd=[r_stb[si], r_qk, r_vtok[j], r_Am[ai]], wr=[r_pO])
                                P.ts(kvt[:], pS[:, 384:512], E13[:, c, 63:64], None, ALU.mult, rd=[r_pS, r_E], wr=[r_kvt])
                                P.stt(state[:, h, :], state[:, h, :], elast[:, c:c + 1], kvt[:], ALU.mult, ALU.add,
                                      rd=[r_state[h], r_eb, r_kvt], wr=[r_state[h]])
                                yield
                        P.act(osq[:], pO[:, :], AF.Square, rd=[r_pO], wr=[r_osq])
                        pf, rpf = nextF()
                        P.mm(pf[:, :], [(ones_b[:, :], osq[:])], rd=[r_ones, r_osq], wr=[rpf])
                        rstd(ff[:], pf[:, :], 1.0 / 128, [rpf], r_ff)
                        P.tt(lf[:], pO[:, :], ff[:], ALU.mult, rd=[r_pO, r_ff], wr=[r_lf])
                        hi_ = hgc[0] % 2
                        hgc[0] += 1
                        P.stt(hgst[hi_][:], lf[:], prm[:, PC_GOUT:PC_GOUT + 1], silg[:], ALU.mult, ALU.mult,
                              rd=[r_lf, r_prm, r_silg], wr=[r_hgst[hi_]])
                        P.dma(hgo_scr[h * 128:(h + 1) * 128, T0:T0 + 512], hgst[hi_][:], rd=[r_hgst[hi_]])
                        yield

                def gen_D(b=b, T0=T0):
                    for (cbase, dst) in ((C_GA, sga_scr), (C_GB, sgb_scr)):
                        for m in range(8):
                            pf, rpf = fproj(cbase + m * 128)
                            gi = gsc[0] % 2
                            gsc[0] += 1
                            P.act(gtmp[gi][:], pf[:, :], AF.Exp, rd=[rpf], wr=[r_gtmp[gi]], scale=-1.0)
                            P.act(gtmp[gi][:], gtmp[gi][:], AF.Ln, rd=[r_gtmp[gi], r_ones], wr=[r_gtmp[gi]], bias=ones_f[:, 0:1])
                            P.act(gst[gi][:], gtmp[gi][:], AF.Exp, rd=[r_gtmp[gi]], wr=[r_gst[gi]], scale=-1.0)
                            P.dma(dst[m * 128:(m + 1) * 128, T0:T0 + 512], gst[gi][:], rd=[r_gst[gi]])
                            yield
                            yield
                run_interleaved([gen_B(), gen_C(), gen_D()])
            P.barrier()
            P.emit()
        es_w1.close()
        es01.close()

        if phases < 2:
            return nc
        es_w3 = ExitStack()
        wup = sbuf(es_w3, "wup", [128, 8, 2 * DFF], BF16); r_wup = R()
        es_w2b = ExitStack()
        wa = sbuf(es_w2b, "wa", [64, 8, D], BF16); r_wa_ = R()
        wb = sbuf(es_w2b, "wb", [128, 4, D], BF16); r_wb = R()
        wo = sbuf(es_w2b, "wo", [128, 8, D], BF16); r_wo = R()
        prefetch = []
        prefetch.append(lambda rd: P.dma(wa[:], w_ba.rearrange("(h p) n -> p h n", p=64), rd=rd, wr=[r_wa_], eng="pool"))
        prefetch.append(lambda rd: P.dma(wb[:], w_bb.rearrange("(h p) n -> p h n", p=128), rd=rd, wr=[r_wb], eng="pool"))
        for k in range(8):
            prefetch.append(lambda rd, k=k: P.dma(wo[:, k, :], w_out[k * 128:(k + 1) * 128, :], rd=rd, wr=[r_wo], eng="pool"))
        for k in range(8):
            for hf_ in range(2):
                prefetch.append(lambda rd, k=k, hf_=hf_: P.dma(wup[:, k, hf_ * DFF:(hf_ + 1) * DFF],
                                                               w_up[k * 128:(k + 1) * 128, hf_ * DFF:(hf_ + 1) * DFF],
                                                               rd=rd, wr=[r_wup], eng="pool"))
        with ExitStack() as es:
            kTh = [sbuf(es, "kTh%d" % i, [128, S], BF16) for i in range(2)]; r_kTh = [R(), R()]
            vh = [sbuf(es, "vh%d" % i, [128, 32, 65], BF16) for i in range(2)]; r_vh = [R(), R()]
            qTb = [sbuf(es, "qTb%d" % i, [128, 512], BF16) for i in range(2)]; r_qTb = [R(), R()]
            ptr = [sbuf(es, "ptr%d" % i, [128, 512], BF16) for i in range(4)]; r_ptr = [R() for _ in range(4)]
            den = sbuf(es, "den", [128, 512]); r_den = R()
            osb = sbuf(es, "osb", [64, 512]); r_osb = R()
            onst = [sbuf(es, "onst%d" % i, [64, 512], BF16) for i in range(2)]; r_onst = [R(), R()]
            banks = [es.enter_context(nc.psum_tensor("p2_b%d" % i, [128, 512], F32)) for i in range(7)]
            rb = [R() for _ in range(7)]
            for i in range(2):
                P.memset(vh[i][:, :, 64:65], 1.0, wr=[r_vh[i]])
            LA = 3
            items = []
            for h in range(8):
                for qb in range(8):
                    nkt = 4 * qb + 4
                    for kt in range(nkt):
                        items.append((h, qb, kt, nkt))
            n_it = len(items)
            slot_of = {}

            def load_head(h):
                hi_ = h % 2
                P.dma(kTh[hi_][0:96, :], kT_scr[:, h, :], wr=[r_kTh[hi_]])
                for g in range(4):
                    P.dma(vh[hi_][:, g * 8:(g + 1) * 8, 0:64],
                          V_scr[g * 1024:(g + 1) * 1024, h * 64:(h + 1) * 64].rearrange("(kt p) v -> p kt v", p=128),
                          wr=[r_vh[hi_]])

            def load_q(h, qb):
                qi = (h * 8 + qb) % 2
                P.dma(qTb[qi][0:96, :], qT_scr[:, h, qb * 512:(qb + 1) * 512], wr=[r_qTb[qi]])
            load_head(0)
            load_q(0, 0)
            pending = []
            den2 = [sbuf(es, "den2_%d" % i, [128, 512]) for i in range(2)]; r_den2 = [R(), R()]
            osb2 = [sbuf(es, "osb2_%d" % i, [64, 512]) for i in range(2)]; r_osb2 = [R(), R()]
            for idx in range(n_it + LA + 4):
                if idx < n_it:
                    h, qb, kt, nkt = items[idx]
                    hi_ = h % 2
                    qi = (h * 8 + qb) % 2
                    if kt == 0:
                        nq = h * 8 + qb + 1
                        if nq < 64:
                            load_q(nq // 8, nq % 8)
                    r = kt - 4 * qb
                    c0 = 128 * r if r > 0 else 0
                    si = idx % 4
                    pSb, r_pSb = banks[si], rb[si]
                    P.mm(pSb[:, c0:512], [(kTh[hi_][0:96, kt * 128:(kt + 1) * 128], qTb[qi][0:96, c0:512])],
                         rd=[r_kTh[hi_], r_qTb[qi]], wr=[r_pSb])
                    if kt == 0 and prefetch and (h * 8 + qb) % 2 == 0:
                        r_pace = R()
                        P.act(ptr[si][:, c0:512], pSb[:, c0:512], AF.Exp, rd=[r_pSb], wr=[r_ptr[si], r_pace])
                        prefetch.pop(0)([r_pace])
                    else:
                        P.act(ptr[si][:, c0:512], pSb[:, c0:512], AF.Exp, rd=[r_pSb], wr=[r_ptr[si]])
                    if r >= 0:
                        P.tt(ptr[si][:, c0:c0 + 128], ptr[si][:, c0:c0 + 128], tri_b, ALU.mult,
                             rd=[r_ptr[si], r_cstb], wr=[r_ptr[si]])
                i2 = idx - LA
                if 0 <= i2 < n_it:
                    h, qb, kt, nkt = items[i2]
                    hi_ = h % 2
                    qi = (h * 8 + qb) % 2
                    r = kt - 4 * qb
                    c0 = 128 * r if r > 0 else 0
                    si = i2 % 4
                    pOb, r_pOb = banks[4 + qi], rb[4 + qi]
                    if kt == 0 and qb == 0 and h + 1 < 8:
                        load_head(h + 1)
                    P.mm(pOb[0:65, c0:512], [(vh[hi_][:, kt, 0:65], ptr[si][:, c0:512])],
                         rd=[r_vh[hi_], r_ptr[si]], wr=[r_pOb], start=(kt == 0), stop=(kt == nkt - 1))
                    if kt == nkt - 1:
                        P.act(den2[qi][64:65, :], pOb[64:65, :], AF.Ln, rd=[r_pOb], wr=[r_den2[qi]])
                        P.act(den2[qi][64:65, :], den2[qi][64:65, :], AF.Exp, rd=[r_den2[qi]], wr=[r_den2[qi]], scale=-1.0)
                        P.cp(osb2[qi][:], pOb[0:64, :], rd=[r_pOb], wr=[r_osb2[qi]])

                        def tail(h=h, qb=qb, qi=qi):
                            P.mm(banks[6][0:64, :], [(ones_f[64:65, 0:64], den2[qi][64:65, :])], rd=[r_ones, r_den2[qi]], wr=[rb[6]])
                            P.tt(onst[qi][:], osb2[qi][:], banks[6][0:64, :], ALU.mult, rd=[r_osb2[qi], rb[6]], wr=[r_onst[qi]])
                            P.dma(ON_scr[:, h, qb * 512:(qb + 1) * 512], onst[qi][:], rd=[r_onst[qi]])
                        pending.append((idx + 3, tail))
                while pending and pending[0][0] <= idx:
                    pending.pop(0)[1]()
            while prefetch:
                prefetch.pop(0)([])
            P.barrier()
            P.emit()

        if phases < 3:
            es_w2b.close()
            es_w3.close()
            return nc
        with ExitStack() as es:
            g1bc = sbuf(es, "g1bc", [128, D])
            P.dma(g1bc[:], g1src.partition_broadcast(128), wr=[r_gbc])
            onb = [sbuf(es, "onb%d" % i, [64, 8, 512], BF16) for i in range(2)]; r_onb = [R(), R()]
            hgb = [sbuf(es, "hgb%d" % i, [128, 4, 512], BF16) for i in range(2)]; r_hgb = [R(), R()]
            sgl = [sbuf(es, "sgl%d" % i, [128, 2, 512], BF16) for i in range(6)]; r_sgl = [R() for _ in range(6)]
            t1 = sbuf(es, "t1", [128, 512]); t2 = sbuf(es, "t2", [128, 512]); r_t = R()
            mg = sbuf(es, "mg", [128, 8, 512], BF16); r_mg = R()
            xr = [sbuf(es, "x2r%d" % i, [128, D]) for i in range(2)]; r_xr = [R(), R()]
            xo = [sbuf(es, "x2o%d" % i, [128, D]) for i in range(2)]; r_xo = [R(), R()]
            banks = [es.enter_context(nc.psum_tensor("p3_b%d" % i, [128, 512], F32)) for i in range(8)]
            rb = [R() for _ in range(8)]
            def load_blk(b):
                bi = b % 2
                P.dma(onb[bi][:], ON_scr[:, :, b * 512:(b + 1) * 512], wr=[r_onb[bi]])
                P.dma(hgb[bi][:], hgo_scr[:, b * 512:(b + 1) * 512].rearrange("(h p) t -> p h t", p=128), wr=[r_hgb[bi]])

            def load_gate(t):
                b, m = divmod(t, 8)
                gi = t % 6
                ms = slice(m * 128, (m + 1) * 128)
                P.dma(sgl[gi][:, 0, :], sga_scr[ms, b * 512:(b + 1) * 512], wr=[r_sgl[gi]])
                P.dma(sgl[gi][:, 1, :], sgb_scr[ms, b * 512:(b + 1) * 512], wr=[r_sgl[gi]])

            def load_x(t):
                b, j = divmod(t, 4)
                P.dma(xr[t % 2][:], x[b * 512 + j * 128:b * 512 + (j + 1) * 128, :], wr=[r_xr[t % 2]])
            load_blk(0)
            for t in range(4):
                load_gate(t)
            load_x(0)
            for b in range(NBLK):
                T0 = b * 512
                bi = b % 2
                if b + 1 < NBLK:
                    load_blk(b + 1)
                for m in range(8):
                    t = b * 8 + m
                    if t + 4 < NBLK * 8:
                        load_gate(t + 4)
                    ms = slice(m * 128, (m + 1) * 128)
                    gi = t % 6
                    pa, rpa = banks[(2 * m) % 4], rb[(2 * m) % 4]
                    pb, rpb = banks[(2 * m + 1) % 4], rb[(2 * m + 1) % 4]
                    P.mm(pa[:, :], [(wa[:, h, ms], onb[bi][:, h, :]) for h in range(8)], rd=[r_wa_, r_onb[bi]], wr=[rpa])
                    P.mm(pb[:, :], [(wb[:, h, ms], hgb[bi][:, h, :]) for h in range(4)], rd=[r_wb, r_hgb[bi]], wr=[rpb])
                    P.tt(t1[:], pa[:, :], sgl[gi][:, 0, :], ALU.mult, rd=[rpa, r_sgl[gi]], wr=[r_t])
                    P.tt(t2[:], pb[:, :], sgl[gi][:, 1, :], ALU.mult, rd=[rpb, r_sgl[gi]], wr=[r_t])
                    P.tt(mg[:, m, :], t1[:], t2[:], ALU.add, rd=[r_t], wr=[r_mg])
                for j in range(4):
                    js = slice(j * 128, (j + 1) * 128)
                    t = b * 4 + j
                    xi = t % 2
                    if t + 1 < NBLK * 4:
                        load_x(t + 1)
                    p0, rp0 = banks[4 + 2 * xi], rb[4 + 2 * xi]
                    p1, rp1 = banks[5 + 2 * xi], rb[5 + 2 * xi]
                    P.mm(p0[:, :], [(mg[:, k, js], wo[:, k, 0:512]) for k in range(8)], rd=[r_mg, r_wo], wr=[rp0])
                    P.mm(p1[:, :], [(mg[:, k, js], wo[:, k, 512:1024]) for k in range(8)], rd=[r_mg, r_wo], wr=[rp1])
                    P.tt(xo[xi][:, 0:512], p0[:, :], g1bc[:, 0:512], ALU.mult, rd=[rp0, r_gbc], wr=[r_xo[xi]])
                    P.tt(xo[xi][:, 512:1024], p1[:, :], g1bc[:, 512:1024], ALU.mult, rd=[rp1, r_gbc], wr=[r_xo[xi]])
                    P.tt(xo[xi][:], xo[xi][:], xr[xi][:], ALU.add, rd=[r_xo[xi], r_xr[xi]], wr=[r_xo[xi]], eng="pool")
                    P.dma(out[T0 + j * 128:T0 + (j + 1) * 128, :], xo[xi][:], rd=[r_xo[xi]])
            P.barrier()
            P.emit()

        es_w2b.close()
        if phases < 4:
            es_w3.close()
            return nc
        with ExitStack() as es:
            g2bc = sbuf(es, "g2bc", [128, D])
            P.dma(g2bc[:], g2src.partition_broadcast(128), wr=[r_gbc])
            wdn = sbuf(es, "wdn", [128, 22, D], BF16); r_wdn = R()
            for g in range(2):
                P.dma(wdn[:, g * 11:(g + 1) * 11, :], w_dn[g * 1408:(g + 1) * 1408, :].rearrange("(k p) n -> p k n", p=128),
                      wr=[r_wdn], eng="pool")
            xring = [sbuf(es, "x3r%d" % i, [128, D]) for i in range(2)]; r_x = [R(), R()]
            xnr = [sbuf(es, "x3n%d" % i, [128, D], BF16) for i in range(2)]; r_xn = [R(), R()]
            junkb = sbuf(es, "junk3", [128, D], BF16); r_jb = R()
            st4 = sbuf(es, "st43", [128, 16]); r_st4 = R()
            h2T = [sbuf(es, "h2T%d" % i, [128, 8, 512], BF16) for i in range(2)]; r_h2T = [R(), R()]
            actT = sbuf(es, "actT", [128, 22, 512], BF16); r_actT = R()
            uext = [sbuf(es, "uext%d" % i, [128, 514]) for i in range(2)]; r_ue = [R(), R()]
            yv = [sbuf(es, "yv%d" % i, [128, 512]) for i in range(2)]; r_yv = [R(), R()]
            gsil = sbuf(es, "gsil", [128, 512]); r_gsil = R()
            halo = sbuf(es, "halo", [128, 44, 2]); r_halo = [R() for _ in range(44)]
            r_uh = [R(), R()]
            _xo = sbuf(es, "x3o", [128, D]); _rxo = R()
            xo = [_xo, _xo]; r_xo = [_rxo, _rxo]
            banks = [es.enter_context(nc.psum_tensor("p4_b%d" % i, [128, 512], F32)) for i in range(8)]
            rb = [R() for _ in range(8)]
            P.memset(halo[:], 0.0, wr=r_halo)
            r_out = [[R() for _ in range(4)] for _ in range(NBLK)]
            uc = 0
            fc = 0
            xc = 0
            def stage1(b):
                T0 = b * 512
                hT_ = h2T[b % 2]
                for j in range(4):
                    xt, rxt = xring[j % 2], r_x[j % 2]
                    xn_, rxn = xnr[j % 2], r_xn[j % 2]
                    P.dma(xt[:], out[T0 + j * 128:T0 + (j + 1) * 128, :], rd=[r_out[b][j]], wr=[rxt])
                    P.act(junkb[:], xt[:], AF.Square, rd=[rxt], wr=[r_jb, r_st4], accum_out=st4[:, j:j + 1])
                    P.act(st4[:, 4 + j:5 + j], st4[:, j:j + 1], AF.Sqrt, rd=[r_st4, r_eps], wr=[r_st4],
                          scale=1.0 / D, bias=epsb[:, 0:1])
                    P.recip(st4[:, 8 + j:9 + j], st4[:, 4 + j:5 + j], rd=[r_st4], wr=[r_st4])
                    P.ts(xn_[:], xt[:], st4[:, 8 + j:9 + j], None, ALU.mult, rd=[rxt, r_st4], wr=[rxn])
                    for k in range(8):
                        bv = banks[4 + k // 2][:].bitcast(BF16)
                        c0 = (k % 2) * 512 + j * 128
                        P.tr(bv[:, c0:c0 + 128], xn_[:, k * 128:(k + 1) * 128], identb, rd=[rxn, r_cstb], wr=[rb[4 + k // 2]])
                for k in range(8):
                    bv = banks[4 + k // 2][:].bitcast(BF16)
                    P.act(hT_[:, k, :], bv[:, (k % 2) * 512:(k % 2) * 512 + 512], AF.Identity,
                          rd=[rb[4 + k // 2], r_gs, r_mod], wr=[r_h2T[b % 2]], scale=gs2[:, k:k + 1], bias=modT[:, 24 + k:25 + k])
            stage1(0)
            print("P3 sbuf bytes remaining:", nc.sbuf_bytes_remaining, flush=True)
            for b in range(NBLK):
                T0 = b * 512
                h2c, r_h2c = h2T[b % 2], r_h2T[b % 2]

                def upchunk(c):
                    nonlocal uc, fc
                    pf, rpf = banks[fc % 4], rb[fc % 4]
                    fc += 1
                    ui = uc % 2
                    uc += 1
                    ue, rue = uext[ui], r_ue[ui]
                    y, ry = yv[ui], r_yv[ui]
                    P.mm(pf[:, :], [(wup[:, k, c * 128:(c + 1) * 128], h2c[:, k, :]) for k in range(8)], rd=[r_wup, r_h2c], wr=[rpf])
                    ruh = r_uh[ui]
                    P.cp(ue[:, 0:2], halo[:, c, :], rd=[r_halo[c]], wr=[ruh])
                    P.act(ue[:, 2:514], pf[:, :], AF.Copy, rd=[rpf], wr=[rue])
                    P.act(y[:], pf[:, :], AF.Identity, rd=[rpf, r_prm], wr=[ry],
                          scale=prm[:, PC_CW + 2 * 44 + c:PC_CW + 2 * 44 + c + 1], bias=prm[:, PC_CB + c:PC_CB + c + 1])
                    P.stt(y[:], ue[:, 1:513], prm[:, PC_CW + 44 + c:PC_CW + 44 + c + 1], y[:], ALU.mult, ALU.add, rd=[rue, ruh, r_prm, ry], wr=[ry])
                    P.stt(y[:], ue[:, 0:512], prm[:, PC_CW + c:PC_CW + c + 1], y[:], ALU.mult, ALU.add, rd=[rue, ruh, r_prm, ry], wr=[ry])
                    P.cp(halo[:, c, :], ue[:, 512:514], rd=[rue], wr=[r_halo[c]])
                    return y, ry
                for c in range(22):
                    y, ry = upchunk(c)
                    y2, ry2 = upchunk(22 + c)
                    P.act(gsil[:], y[:], AF.Silu, rd=[ry], wr=[r_gsil])
                    P.tt(actT[:, c, :], gsil[:], y2[:], ALU.mult, rd=[r_gsil, ry2], wr=[r_actT])
                    if c == 15 and b + 1 < NBLK:
                        stage1(b + 1)
                for j in range(4):
                    js = slice(j * 128, (j + 1) * 128)
                    xi = xc % 2
                    xc += 1
                    xt, rxt = xring[xi], r_x[xi]
                    P.dma(xt[:], out[T0 + j * 128:T0 + (j + 1) * 128, :], rd=[r_out[b][j]], wr=[rxt])
                    p0, rp0 = banks[4 + 2 * xi], rb[4 + 2 * xi]
                    p1, rp1 = banks[5 + 2 * xi], rb[5 + 2 * xi]
                    P.mm(p0[:, :], [(actT[:, k, js], wdn[:, k, 0:512]) for k in range(22)], rd=[r_actT, r_wdn], wr=[rp0])
                    P.mm(p1[:, :], [(actT[:, k, js], wdn[:, k, 512:1024]) for k in range(22)], rd=[r_actT, r_wdn], wr=[rp1])
                    P.tt(xo[xi][:, 0:512], p0[:, :], g2bc[:, 0:512], ALU.mult, rd=[rp0, r_gbc], wr=[r_xo[xi]])
                    P.tt(xo[xi][:, 512:1024], p1[:, :], g2bc[:, 512:1024], ALU.mult, rd=[rp1, r_gbc], wr=[r_xo[xi]])
                    P.tt(xo[xi][:], xo[xi][:], xt[:], ALU.add, rd=[r_xo[xi], rxt], wr=[r_xo[xi]])
                    P.dma(out[T0 + j * 128:T0 + (j + 1) * 128, :], xo[xi][:], rd=[r_xo[xi]], wr=[r_out[b][j]])
            P.barrier()
            P.emit()
        es_w3.close()
    return nc


def _host_layout(inputs):
    f32 = np.float32
    g = {k: np.asarray(v) for k, v in inputs.items()}

    def fm(v):
        v = np.asarray(v, f32).reshape(-1, 128)
        return np.ascontiguousarray(v.T)
    shared = np.zeros((128, NP), f32)
    shared[:, PC_BADA:PC_BADA + 48] = fm(g["b_ada"][0])
    shared[:, PC_G1:PC_G1 + 8] = fm(g["norm1_g"][0])
    shared[:, PC_G2:PC_G2 + 8] = fm(g["norm2_g"][0])
    shared[:, PC_GQA:PC_GQA + 4] = fm(g["q_a_norm_g"][0])
    shared[:, PC_GKVA:PC_GKVA + 2] = fm(g["kv_a_norm_g"][0])
    shared[:, PC_LB0:PC_LB0 + 4] = fm(g["hg_lower_bound"][0])
    shared[:, PC_LB1:PC_LB1 + 4] = fm(g["hg_lower_bound"][1])
    shared[:, PC_GOUT] = np.asarray(g["hg_out_norm_g"][0], f32)
    shared[:, PC_CB:PC_CB + 44] = fm(g["conv_b"][0])
    for jj in range(3):
        shared[:, PC_CW + jj * 44:PC_CW + (jj + 1) * 44] = fm(g["conv_w"][0][jj])
    bc = np.zeros((128, NB), f32)
    bc[:, BC_GQ:BC_GQ + 768] = np.tile(np.asarray(g["q_norm_g"][0], f32), 8)[None, :]
    bc[:, BC_GKN:BC_GKN + 512] = np.tile(np.asarray(g["k_norm_g"][0], f32)[:64], 8)[None, :]
    bc[:, BC_GKR:BC_GKR + 32] = np.asarray(g["k_norm_g"][0], f32)[64:96][None, :]
    inv_freq = (np.float32(10000.0) ** (-np.arange(0, 32, 2, dtype=np.float32) / np.float32(32))).astype(f32)
    bc[:, BC_IF:BC_IF + 16] = inv_freq[None, :]
    p = np.arange(128)
    cst = np.zeros((128, 384), f32)
    cst[:, 0:128] = np.eye(128, dtype=f32)
    cst[:, 128:256] = (p[:, None] <= p[None, :]).astype(f32)
    cst[:, 256:384] = ((p[:, None] <= p[None, :]) & ((p[:, None] // 64) == (p[None, :] // 64))).astype(f32)
    common = {
        "bcp": bc, "cst": cst,
        "w_ada": np.ascontiguousarray(g["w_ada"][0], f32), "w_in": np.ascontiguousarray(g["w_in"][0], f32),
        "w_uq": np.ascontiguousarray(g["w_uq"][0], f32), "w_ukv": np.ascontiguousarray(g["w_ukv"][0], f32),
        "w_ba": np.ascontiguousarray(g["w_branch_a"][0], f32), "w_bb": np.ascontiguousarray(g["w_branch_b"][0], f32),
        "w_out": np.ascontiguousarray(g["w_out"][0], f32), "w_up": np.ascontiguousarray(g["w_up"][0], f32),
        "w_dn": np.ascontiguousarray(g["w_down"][0], f32),
    }
    in_maps = []
    for c in range(8):
        prm = shared.copy()
        prm[:, PC_C:PC_C + 8] = fm(g["c"][c])
        m = dict(common)
        m["x"] = np.ascontiguousarray(g["x"][c], f32)
        m["prm"] = prm
        m["posT"] = np.ascontiguousarray(np.asarray(g["positions"][c], np.int32).reshape(32, 128).T)
        in_maps.append(m)
    return in_maps


def kernel(**inputs):
    in_maps = _host_layout(inputs)
    nc = build()
    res = run_bass_kernel_spmd(nc, in_maps, core_ids=list(range(8)))
    return np.stack([np.asarray(r["out"], np.float32) for r in res.results], axis=0)
```
